# Optimizing a Trainium2 kernel written in Bass

```python
import math
import jax, jax.numpy as jnp
from jax import lax
import numpy as np

D_MODEL = 1024
BATCH = 8
SEQ = 4096
DEPTH = 2

MLSTM_HEADS = 4
MLSTM_HEAD_DIM = D_MODEL // 8
MLSTM_WIDTH = MLSTM_HEADS * MLSTM_HEAD_DIM
MLSTM_CHUNK = 64
MLA_HEADS = 4
MLA_NOPE_DIM = D_MODEL // 8
MLA_ROPE_DIM = 64
MLA_V_DIM = D_MODEL // 8
MLA_Q_LORA = D_MODEL // 4
MLA_KV_LORA = D_MODEL // 8
MLA_WIDTH = MLA_HEADS * MLA_V_DIM
ROPE_THETA = 10000.0
ATTN_BLOCK = 128
POOL_WINDOWS = (2, 4, 8, 16)
POOL_GROUPS = len(POOL_WINDOWS)
POOL_GROUP_DIM = D_MODEL // POOL_GROUPS
FFN_DIM = ((8 * D_MODEL // 3 + 127) // 128) * 128
CONV_WIDTH = 3
LN_EPS = 1e-5
RMS_EPS = 1e-6
DEEPNORM_ALPHA = (2 * DEPTH) ** 0.25
DEEPNORM_BETA = (8 * DEPTH) ** -0.25
N_EVEN = (DEPTH + 1) // 2
N_ODD = DEPTH // 2
IN_SIZES = (MLSTM_WIDTH, MLSTM_WIDTH, MLSTM_WIDTH, MLSTM_WIDTH, MLSTM_HEADS, MLSTM_HEADS,
            MLA_Q_LORA, MLA_KV_LORA, MLA_ROPE_DIM)
IN_COLS = sum(IN_SIZES)

kernel_name = "hybrid_mlstm_mla_pool_deepnorm"


def layer_norm(x, g, b):
    xf = x.astype(jnp.float32)
    mu = jnp.mean(xf, axis=-1, keepdims=True)
    var = jnp.mean(jnp.square(xf - mu), axis=-1, keepdims=True)
    return ((xf - mu) * lax.rsqrt(var + LN_EPS) * g + b).astype(x.dtype)


def rms_norm(x, g):
    xf = x.astype(jnp.float32)
    return (xf * lax.rsqrt(jnp.mean(jnp.square(xf), axis=-1, keepdims=True) + RMS_EPS) * g).astype(x.dtype)


def rope_tables(positions):
    inv_freq = ROPE_THETA ** (-jnp.arange(0, MLA_ROPE_DIM, 2, dtype=jnp.float32) / MLA_ROPE_DIM)
    ang = positions.astype(jnp.float32)[..., None] * inv_freq
    return jnp.cos(ang), jnp.sin(ang)


def apply_rope(x, cos, sin):
    xf = x.astype(jnp.float32)
    x1, x2 = jnp.split(xf, 2, axis=-1)
    return jnp.concatenate([x1 * cos - x2 * sin, x2 * cos + x1 * sin], axis=-1).astype(x.dtype)


def mlstm_chunkwise(q, k, v, i_pre, f_pre):
    B, H, S, d = q.shape
    L = MLSTM_CHUNK
    nc = S // L
    f32 = jnp.float32

    def chunks(t):
        return jnp.moveaxis(t.astype(f32).reshape(B, H, nc, L, *t.shape[3:]), 2, 0)

    qc, kc, vc = chunks(q), chunks(k), chunks(v)
    ic = chunks(i_pre)
    bc = jnp.cumsum(chunks(jax.nn.log_sigmoid(f_pre.astype(f32))), axis=-1)
    causal = jnp.tril(jnp.ones((L, L), dtype=bool))

    def step(carry, xs):
        C, n, m = carry
        q_, k_, v_, i_, b_ = xs
        D = jnp.where(causal, b_[..., :, None] - b_[..., None, :] + i_[..., None, :], -jnp.inf)
        inter = b_ + m[..., None]
        m_t = jnp.maximum(jnp.max(D, axis=-1), inter)
        A = jnp.einsum('bhtk,bhsk->bhts', q_, k_) * jnp.exp(D - m_t[..., None])
        sc = jnp.exp(inter - m_t)
        num = jnp.einsum('bhts,bhsv->bhtv', A, v_) + sc[..., None] * jnp.einsum('bhvk,bhtk->bhtv', C, q_)
        den = jnp.sum(A, axis=-1) + sc * jnp.einsum('bhk,bhtk->bht', n, q_)
        h = num / jnp.maximum(jnp.abs(den), jnp.exp(-m_t))[..., None]
        b_last = b_[..., -1]
        g = b_last[..., None] - b_ + i_
        m_new = jnp.maximum(b_last + m, jnp.max(g, axis=-1))
        w = jnp.exp(g - m_new[..., None])
        decay = jnp.exp(b_last + m - m_new)
        C = decay[..., None, None] * C + jnp.einsum('bhs,bhsv,bhsk->bhvk', w, v_, k_)
        n = decay[..., None] * n + jnp.einsum('bhs,bhsk->bhk', w, k_)
        return (C, n, m_new), h

    init = (jnp.zeros((B, H, d, d), f32), jnp.zeros((B, H, d), f32), jnp.zeros((B, H), f32))
    _, h = lax.scan(step, init, (qc, kc, vc, ic, bc))
    return jnp.moveaxis(h, 0, 2).reshape(B, H, S, d).astype(q.dtype)


def causal_attention_blocked(q, k, v):
    B, H, S, dk = q.shape
    nb = S // ATTN_BLOCK
    scale = dk ** -0.5
    qb = jnp.moveaxis(q.reshape(B, H, nb, ATTN_BLOCK, dk), 2, 0)
    key_pos = jnp.arange(S)

    def one_block(args):
        idx, qblk = args
        s = jnp.einsum('bhqd,bhkd->bhqk', qblk, k, preferred_element_type=jnp.float32) * scale
        q_pos = idx * ATTN_BLOCK + jnp.arange(ATTN_BLOCK)
        mask = key_pos[None, :] <= q_pos[:, None]
        p = jax.nn.softmax(jnp.where(mask, s, -jnp.inf), axis=-1)
        return jnp.einsum('bhqk,bhkd->bhqd', p.astype(v.dtype), v)

    o = lax.map(one_block, (jnp.arange(nb), qb))
    return jnp.moveaxis(o, 0, 2).reshape(B, H, S, v.shape[-1])


def hybrid_mixer(x, cos, sin, w_in, b_igate, b_fgate, mlstm_norm, q_norm, kv_norm, w_uq, w_ukv, w_out):
    B, S, _ = x.shape
    h = x @ w_in
    offs = np.cumsum(IN_SIZES)[:-1].tolist()
    q_m, k_m, v_m, o_m, i_pre, f_pre, c_q, c_kv, k_r = jnp.split(h, offs, axis=-1)

    def heads(t, nh):
        return t.reshape(B, S, nh, -1).transpose(0, 2, 1, 3)

    hm = mlstm_chunkwise(heads(q_m, MLSTM_HEADS),
                         heads(k_m, MLSTM_HEADS) * (MLSTM_HEAD_DIM ** -0.5),
                         heads(v_m, MLSTM_HEADS),
                         (i_pre + b_igate).transpose(0, 2, 1),
                         (f_pre + b_fgate).transpose(0, 2, 1))
    hm = rms_norm(hm.transpose(0, 2, 1, 3), mlstm_norm.reshape(MLSTM_HEADS, MLSTM_HEAD_DIM))
    y_m = (hm * jax.nn.sigmoid(o_m.reshape(B, S, MLSTM_HEADS, MLSTM_HEAD_DIM))).reshape(B, S, MLSTM_WIDTH)

    q = (rms_norm(c_q, q_norm) @ w_uq).reshape(B, S, MLA_HEADS, MLA_NOPE_DIM + MLA_ROPE_DIM)
    q_nope, q_rope = q[..., :MLA_NOPE_DIM], q[..., MLA_NOPE_DIM:]
    q_rope = apply_rope(q_rope, cos[:, :, None, :], sin[:, :, None, :])
    kv = (rms_norm(c_kv, kv_norm) @ w_ukv).reshape(B, S, MLA_HEADS, MLA_NOPE_DIM + MLA_V_DIM)
    k_nope, v = kv[..., :MLA_NOPE_DIM], kv[..., MLA_NOPE_DIM:]
    k_rope = jnp.broadcast_to(apply_rope(k_r, cos, sin)[:, :, None, :], (B, S, MLA_HEADS, MLA_ROPE_DIM))
    qh = jnp.concatenate([q_nope, q_rope], axis=-1).transpose(0, 2, 1, 3)
    kh = jnp.concatenate([k_nope, k_rope], axis=-1).transpose(0, 2, 1, 3)
    y_a = causal_attention_blocked(qh, kh, v.transpose(0, 2, 1, 3))
    y_a = y_a.transpose(0, 2, 1, 3).reshape(B, S, MLA_WIDTH)

    return jnp.concatenate([y_m, y_a], axis=-1) @ w_out


def pool_mixer(x, pool_w, layer_scale):
    B, S, D = x.shape
    xf = x.astype(jnp.float32)
    cs = jnp.concatenate([jnp.zeros((B, 1, D), jnp.float32), jnp.cumsum(xf, axis=1)], axis=1)
    t = jnp.arange(S)
    outs = []
    for g, w in enumerate(POOL_WINDOWS):
        sl = slice(g * POOL_GROUP_DIM, (g + 1) * POOL_GROUP_DIM)
        start = jnp.maximum(t + 1 - w, 0)
        csg = cs[..., sl]
        mean = (csg[:, 1:] - csg[:, start]) / (t + 1 - start).astype(jnp.float32)[:, None]
        outs.append(mean - xf[..., sl])
    pooled = jnp.stack(outs, axis=2).astype(x.dtype)
    y = jnp.einsum('bsgc,gcd->bsgd', pooled, pool_w).reshape(B, S, D)
    return y * layer_scale


def conv_ffn(x, w_up, conv_w, conv_b, w_down):
    S = x.shape[1]
    u = x @ w_up
    up = jnp.pad(u, ((0, 0), (CONV_WIDTH - 1, 0), (0, 0)))
    u = sum(up[:, j:j + S] * conv_w[j] for j in range(CONV_WIDTH)) + conv_b
    gate, val = jnp.split(u, 2, axis=-1)
    return (jax.nn.silu(gate) * val) @ w_down


def setup_inputs(seed: int = 0) -> dict:
    key = jax.random.key(seed)
    ks = jax.random.split(key, 24)
    f32 = jnp.float32

    def nrm(k, shape, scale):
        return jax.random.normal(k, shape, f32) * scale

    x = jax.random.normal(ks[0], (BATCH, SEQ, D_MODEL), f32)
    positions = (jnp.arange(SEQ, dtype=jnp.int32)[None, :]
                 + jax.random.randint(ks[1], (BATCH, 1), 0, 1024, dtype=jnp.int32))
    v_lo = 2 * MLSTM_WIDTH
    even_w_in = nrm(ks[2], (N_EVEN, D_MODEL, IN_COLS), D_MODEL ** -0.5)
    even_w_in = even_w_in.at[..., v_lo:v_lo + MLSTM_WIDTH].multiply(DEEPNORM_BETA)
    even_b_igate = -2.0 + nrm(ks[3], (N_EVEN, MLSTM_HEADS), 0.1)
    even_b_fgate = jnp.linspace(3.0, 6.0, MLSTM_HEADS, dtype=f32)[None] + nrm(ks[4], (N_EVEN, MLSTM_HEADS), 0.1)
    even_mlstm_norm = 1.0 + nrm(ks[5], (N_EVEN, MLSTM_WIDTH), 0.05)
    even_q_norm = 1.0 + nrm(ks[6], (N_EVEN, MLA_Q_LORA), 0.05)
    even_kv_norm = 1.0 + nrm(ks[7], (N_EVEN, MLA_KV_LORA), 0.05)
    even_w_uq = nrm(ks[8], (N_EVEN, MLA_Q_LORA, MLA_HEADS * (MLA_NOPE_DIM + MLA_ROPE_DIM)), MLA_Q_LORA ** -0.5)
    w_ukv = nrm(ks[9], (N_EVEN, MLA_KV_LORA, MLA_HEADS, MLA_NOPE_DIM + MLA_V_DIM), MLA_KV_LORA ** -0.5)
    even_w_ukv = w_ukv.at[..., MLA_NOPE_DIM:].multiply(DEEPNORM_BETA).reshape(N_EVEN, MLA_KV_LORA, -1)
    even_w_out = nrm(ks[10], (N_EVEN, MLSTM_WIDTH + MLA_WIDTH, D_MODEL), DEEPNORM_BETA * (MLSTM_WIDTH + MLA_WIDTH) ** -0.5)
    odd_pool_w = nrm(ks[11], (N_ODD, POOL_GROUPS, POOL_GROUP_DIM, POOL_GROUP_DIM), DEEPNORM_BETA * POOL_GROUP_DIM ** -0.5)
    odd_layer_scale = 1.0 + nrm(ks[12], (N_ODD, D_MODEL), 0.1)
    ffn_w_up = nrm(ks[13], (DEPTH, D_MODEL, 2 * FFN_DIM), DEEPNORM_BETA * D_MODEL ** -0.5)
    ffn_conv_w = nrm(ks[14], (DEPTH, CONV_WIDTH, 2 * FFN_DIM), CONV_WIDTH ** -0.5)
    ffn_conv_b = nrm(ks[15], (DEPTH, 2 * FFN_DIM), 0.02)
    ffn_w_down = nrm(ks[16], (DEPTH, FFN_DIM, D_MODEL), DEEPNORM_BETA * FFN_DIM ** -0.5)
    ln_mix_g = 1.0 + nrm(ks[17], (DEPTH, D_MODEL), 0.05)
    ln_mix_b = nrm(ks[18], (DEPTH, D_MODEL), 0.02)
    ln_ffn_g = 1.0 + nrm(ks[19], (DEPTH, D_MODEL), 0.05)
    ln_ffn_b = nrm(ks[20], (DEPTH, D_MODEL), 0.02)
    return {"x": x, "positions": positions,
            "even_w_in": even_w_in, "even_b_igate": even_b_igate, "even_b_fgate": even_b_fgate,
            "even_mlstm_norm": even_mlstm_norm, "even_q_norm": even_q_norm, "even_kv_norm": even_kv_norm,
            "even_w_uq": even_w_uq, "even_w_ukv": even_w_ukv, "even_w_out": even_w_out,
            "odd_pool_w": odd_pool_w, "odd_layer_scale": odd_layer_scale,
            "ffn_w_up": ffn_w_up, "ffn_conv_w": ffn_conv_w, "ffn_conv_b": ffn_conv_b, "ffn_w_down": ffn_w_down,
            "ln_mix_g": ln_mix_g, "ln_mix_b": ln_mix_b, "ln_ffn_g": ln_ffn_g, "ln_ffn_b": ln_ffn_b}


def reference(x, positions, even_w_in, even_b_igate, even_b_fgate, even_mlstm_norm, even_q_norm,
              even_kv_norm, even_w_uq, even_w_ukv, even_w_out, odd_pool_w, odd_layer_scale,
              ffn_w_up, ffn_conv_w, ffn_conv_b, ffn_w_down, ln_mix_g, ln_mix_b, ln_ffn_g, ln_ffn_b):
    cos, sin = rope_tables(positions)
    for layer in range(DEPTH):
        e = layer // 2
        if layer % 2 == 0:
            y = hybrid_mixer(x, cos, sin, even_w_in[e], even_b_igate[e], even_b_fgate[e],
                             even_mlstm_norm[e], even_q_norm[e], even_kv_norm[e],
                             even_w_uq[e], even_w_ukv[e], even_w_out[e])
        else:
            y = pool_mixer(x, odd_pool_w[e], odd_layer_scale[e])
        x = layer_norm(DEEPNORM_ALPHA * x + y, ln_mix_g[layer], ln_mix_b[layer])
        y = conv_ffn(x, ffn_w_up[layer], ffn_conv_w[layer], ffn_conv_b[layer], ffn_w_down[layer])
        x = layer_norm(DEEPNORM_ALPHA * x + y, ln_ffn_g[layer], ln_ffn_b[layer])
    return x
```

```python
import numpy as np
from contextlib import ExitStack
import concourse.bass as bass
import concourse.mybir as mybir
from concourse.bass_utils import run_bass_kernel_spmd

F32 = mybir.dt.float32
BF16 = mybir.dt.bfloat16
I32 = mybir.dt.int32
ALU = mybir.AluOpType
AF = mybir.ActivationFunctionType
AX = mybir.AxisListType

D = 1024
S = 4096
FF = 2816
ALPHA = float((2 * 2) ** 0.25)
LN_EPS = 1e-5
RMS_EPS = 1e-6
ARENA_LO, ARENA_HI = 16512, 229344


class Buf:
    __slots__ = ("name", "w", "r", "excl")

    def __init__(self, name="", excl=False):
        self.name = name
        self.w = None
        self.r = {}
        self.excl = excl


class Prog:
    ENG = ("pe", "act", "dve", "pool", "sp")

    def __init__(self, nc, n_dma_sems=24):
        self.nc = nc
        self.q = {e: [] for e in self.ENG}
        self.cnt = {e: 0 for e in self.ENG}
        self.waited = {e: {} for e in self.ENG}
        self.n_dma_sems = n_dma_sems
        self.dma_val = [0] * n_dma_sems
        self.dma_next = 0
        self.dma_next_sw = 0
        self.sems = {}
        self.ninstr = 0

    def _wait(self, eng, ev):
        if ev is None:
            return
        key, val = ev
        if val <= 0:
            return
        if key == eng and eng == "pe":
            return
        if self.waited[eng].get(key, 0) >= val:
            return
        self.waited[eng][key] = val
        self.q[eng].append(("w", key, val))

    def _deps(self, eng, reads, writes):
        for b in reads:
            self._wait(eng, b.w)
            if b.excl:
                for k, v in b.r.items():
                    if k != eng:
                        self._wait(eng, (k, v))
        for b in writes:
            self._wait(eng, b.w)
            for k, v in b.r.items():
                self._wait(eng, (k, v))

    def _record(self, ev, reads, writes):
        k, v = ev
        for b in reads:
            if b.r.get(k, 0) < v:
                b.r[k] = v
        for b in writes:
            b.w = ev
            b.r = {}

    def op(self, eng, fn, reads=(), writes=(), signal=True):
        self._deps(eng, reads, writes)
        self.ninstr += 1
        if signal:
            self.cnt[eng] += 1
            ev = (eng, self.cnt[eng])
            self.q[eng].append(("i", fn, eng))
        else:
            ev = (eng, self.cnt[eng] + 1)
            self.q[eng].append(("n", fn, None))
        self._record(ev, reads, writes)
        return ev

    def dma(self, queue, out_ap, in_ap, reads=(), writes=(), **kw):
        nsw = self.n_dma_sems // 3
        if queue == "pool":
            k = self.dma_next_sw
            self.dma_next_sw = (k + 1) % nsw
        else:
            k = nsw + self.dma_next
            self.dma_next = (self.dma_next + 1) % (self.n_dma_sems - nsw)
        key = ("dma", k)
        self._wait(queue, (key, self.dma_val[k]))
        self._deps(queue, reads, writes)
        self.dma_val[k] += 16
        ev = (key, self.dma_val[k])
        self.ninstr += 1
        self.q[queue].append(("d", out_ap, in_ap, key, kw))
        self._record(ev, reads, writes)
        return ev

    def barrier(self):
        for eng in self.ENG:
            for k in range(self.n_dma_sems):
                self._wait(eng, (("dma", k), self.dma_val[k]))
            for e in ("pe", "act", "dve", "pool"):
                if e != eng:
                    self._wait(eng, (e, self.cnt[e]))

    def finish(self, eng="sp"):
        for k in range(self.n_dma_sems):
            self._wait(eng, (("dma", k), self.dma_val[k]))
        for e in ("pe", "act", "dve", "pool"):
            self._wait(eng, (e, self.cnt[e]))

    def emit(self):
        nc = self.nc
        with ExitStack() as st:
            for e in ("pe", "act", "dve", "pool"):
                self.sems[e] = st.enter_context(nc.semaphore("c_" + e))
            for k in range(self.n_dma_sems):
                self.sems[("dma", k)] = st.enter_context(nc.semaphore("d_%d" % k))
            block = st.enter_context(nc.Block())
            sems = self.sems

            def run(eng_obj, items):
                for it in items:
                    t = it[0]
                    if t == "w":
                        eng_obj.wait_ge(sems[it[1]], it[2])
                    elif t == "i":
                        it[1](eng_obj).then_inc(sems[it[2]], 1)
                    elif t == "n":
                        it[1](eng_obj)
                    else:
                        eng_obj.dma_start(out=it[1], in_=it[2], **it[4]).then_inc(sems[it[3]], 16)

            q = self.q

            @block.tensor
            def _(e):
                run(e, q["pe"])

            @block.scalar
            def _(e):
                run(e, q["act"])

            @block.vector
            def _(e):
                run(e, q["dve"])

            @block.gpsimd
            def _(e):
                run(e, q["pool"])

            @block.sync
            def _(e):
                run(e, q["sp"])


def _dtsize(dt):
    return {F32: 4, BF16: 2, I32: 4}[dt]


class Arena:
    def __init__(self, nc):
        self.nc = nc
        self.off = ARENA_LO
        self.n = 0

    def alloc(self, name, shape, dtype):
        nb = _dtsize(dtype)
        for s in shape[1:]:
            nb *= s
        off = (self.off + 31) // 32 * 32
        assert off + nb <= ARENA_HI, ("SBUF overflow", name, off, nb)
        self.off = off + nb
        self.n += 1
        return self.nc.alloc_sbuf_tensor_at("%s_%d" % (name, self.n), list(shape), dtype, offset=off)


def f_mm(out, lhsT, rhs, start, stop):
    return lambda e: e.matmul(out, lhsT, rhs, start=start, stop=stop)


def f_tr(out, in_, ident):
    return lambda e: e.transpose(out, in_, ident)


def f_act(out, in_, func, bias=None, scale=None, accum_out=None):
    kw = {}
    if bias is not None:
        kw["bias"] = bias
    if scale is not None:
        kw["scale"] = scale
    if accum_out is not None:
        kw["accum_out"] = accum_out
    return lambda e: e.activation(out, in_, func, **kw)


def f_copy(out, in_):
    return lambda e: e.tensor_copy(out, in_)


def f_ts(out, in0, s1, s2, op0, op1=None):
    if op1 is None:
        return lambda e: e.tensor_scalar(out, in0, s1, None, op0=op0)
    return lambda e: e.tensor_scalar(out, in0, s1, s2, op0=op0, op1=op1)


def f_tt(out, in0, in1, op):
    return lambda e: e.tensor_tensor(out, in0, in1, op=op)


def f_stt(out, in0, scalar, in1, op0, op1):
    return lambda e: e.scalar_tensor_tensor(out, in0, scalar, in1, op0=op0, op1=op1)


def f_memset(ap, v):
    return lambda e: e.memset(ap, v)


class Ctx:
    pass


def make_consts(P, C):
    A = C.A
    C.ident = A.alloc("ident", [128, 128], BF16)
    C.Bident = Buf("ident")
    C.eps = A.alloc("eps", [128, 2], F32)
    C_EPS[0] = C.eps
    P.op("pool", f_memset(C.eps[:, 0:1], LN_EPS), writes=[C.Bident])
    P.op("pool", f_memset(C.eps[:, 1:2], RMS_EPS), writes=[C.Bident])
    P.op("pool", f_memset(C.ident[:], 1.0), writes=[C.Bident])
    P.op("pool", lambda e: e.affine_select(out=C.ident[:], in_=C.ident[:], compare_op=ALU.is_equal, fill=0.0,
                                           base=0, pattern=[[-1, 128]], channel_multiplier=1),
         reads=[C.Bident], writes=[C.Bident])


def ln_stats(P, z, Bz, st, Bst, col):
    P.op("dve", lambda e: e.bn_stats(st[:, col, 0:6], z[:, 0:512]), reads=[Bz], writes=[Bst])
    P.op("dve", lambda e: e.bn_stats(st[:, col, 6:12], z[:, 512:1024]), reads=[Bz], writes=[Bst])
    P.op("dve", lambda e: e.bn_aggr(st[:, col, 12:14], st[:, col, 0:12]), reads=[Bst], writes=[Bst])


def ln_rstd(P, st, Bst, n):
    P.op("act", f_act(st[:, 0:n, 15:16], st[:, 0:n, 13:14], AF.Ln, bias=C_EPS[0][:, 0:1]), reads=[Bst], writes=[Bst])
    P.op("act", f_act(st[:, 0:n, 14:15], st[:, 0:n, 15:16], AF.Exp, scale=-0.5), reads=[Bst], writes=[Bst])


def ln_apply(P, z, Bz, st, Bst, col, gb, Bgb, out_dram_rows, Bout):
    P.op("dve", f_ts(z, z, st[:, col, 12:13], st[:, col, 14:15], ALU.subtract, ALU.mult), reads=[Bz, Bst], writes=[Bz])
    P.op("dve", f_tt(z, z, gb[:, 0, :], ALU.mult), reads=[Bz, Bgb], writes=[Bz])
    P.op("dve", f_tt(z, z, gb[:, 1, :], ALU.add), reads=[Bz, Bgb], writes=[Bz])
    P.dma("sp", out_dram_rows, z, reads=[Bz], writes=[Bout])


C_EPS = [None]


def ffn_alloc_weights(C):
    A = C.A
    W = Ctx()
    W.wup = A.alloc("wup", [128, 8, 2 * FF], BF16)
    W.wdn = A.alloc("wdn", [128, 22, D], BF16)
    W.Bwup = [Buf("wup%d" % j) for j in range(22)]
    W.Bwdn = Buf("wdn")
    return W


def ffn_load_weights(P, W, w_up, w_down, queue="pool"):
    w_up_v = w_up.rearrange("(c p) (h j n) -> p c h j n", p=128, h=2, j=22)
    wv = W.wup[:, :, :].rearrange("p c (h j n) -> p c h j n", h=2, j=22)
    for j in range(22):
        for h in range(2):
            P.dma(queue, wv[:, :, h, j, :], w_up_v[:, :, h, j, :], writes=[W.Bwup[j]])
        if j == 10 or j == 21:
            j0 = 0 if j == 10 else 11
            w_dn_v = w_down.rearrange("(j p) n -> p j n", p=128)
            P.dma(queue, W.wdn[:, j0:j0 + 11, :], w_dn_v[:, j0:j0 + 11, :], writes=[W.Bwdn])


def ffn_phase(P, C, x_in, Bxin, x_out, Bxout, w_up, cwb, w_down, ln_g, ln_b, W=None, wqueue="pool"):
    nc, A = C.nc, C.A
    save = A.off
    TT, NT = 256, S // 256
    NSET = DBG.get("nset", 5)
    preloaded = W is not None
    if W is None:
        W = ffn_alloc_weights(C)
    else:
        A.off = W.end_off
    wup, wdn, Bwup, Bwdn = W.wup, W.wdn, W.Bwup, W.Bwdn
    cw = A.alloc("cw", [128, 44, 4], F32)
    gb = A.alloc("gb", [128, 2, D], F32)
    hT = [A.alloc("hT", [128, 22, TT], BF16) for _ in range(2)]
    xT = [A.alloc("xT", [128, 8, TT], BF16) for _ in range(2)]
    xr = [A.alloc("xr", [128, D], F32) for _ in range(2)]
    xb = [A.alloc("xb", [128, 2, D], BF16) for _ in range(2)]
    u = [A.alloc("u", [128, 2, TT], F32) for _ in range(NSET)]
    upx = [A.alloc("upx", [128, 2, TT + 2], F32) for _ in range(NSET)]
    halo = [A.alloc("halo", [128, 22, 2, 2], F32)] * 2
    st = A.alloc("st", [128, 2, 16], F32)
    Bcw, Bgb = Buf("cw"), Buf("gb")
    BhT = [Buf("hT0"), Buf("hT1")]
    BxT = [Buf("xT0"), Buf("xT1")]
    Bxr = [Buf("xr0"), Buf("xr1")]
    Bxb = [[Buf("xb"), Buf("xb")] for _ in range(2)]
    Bu = [[Buf("u"), Buf("u")] for _ in range(NSET)]
    Bupx = [Buf("upx") for _ in range(NSET)]
    Bhalo = [Buf("halo0")] * 2
    Bst = Buf("st")
    ps = C.ps
    Bps = C.Bps
    up_banks = [0, 1, 2]
    dn_banks = [3, 4, 5, 6]
    tp_bank = 7
    tp = ps[tp_bank][:, :].bitcast(BF16)

    P.dma("sp", cw[:], cwb, writes=[Bcw])
    P.dma("sp", gb[:, 0, :], ln_g.partition_broadcast(128), writes=[Bgb])
    P.dma("sp", gb[:, 1, :], ln_b.partition_broadcast(128), writes=[Bgb])
    P.op("pool", f_memset(halo[0][:], 0.0), writes=[Bhalo[0]])

    def load_xb(tt):
        b = tt % 2
        for s in range(2):
            r0 = tt * TT + s * 128
            P.dma("pool", xb[b][:, s, :], x_in[r0:r0 + 128, :], reads=[Bxin], writes=[Bxb[b][s]])

    def prep_xT(tt):
        b = tt % 2
        for s in range(2):
            for c in range(8):
                P.op("pe", f_tr(tp[:, c * 128:(c + 1) * 128], xb[b][:, s, c * 128:(c + 1) * 128], C.ident[:]),
                     reads=[Bxb[b][s], C.Bident], writes=[Bps[tp_bank]], signal=(c == 7))
            P.op("dve", f_copy(xT[b][:, :, s * 128:(s + 1) * 128], tp.rearrange("p (c t) -> p c t", c=8)),
                 reads=[Bps[tp_bank]], writes=[BxT[b]])

    def pair_ctx(n):
        tt, j = divmod(n, 22)
        k = up_banks[n % 3]
        ub = n % NSET
        pk = ps[k][:, :].rearrange("p (h t) -> p h t", h=2)
        return tt, j, tt % 2, k, ub, pk

    def S0(n):
        tt, j, b, k, ub, pk = pair_ctx(n)
        for half, jj in ((0, j), (1, j + 22)):
            for c in range(8):
                P.op("pe", f_mm(pk[:, half, :], wup[:, c, jj * 128:(jj + 1) * 128], xT[b][:, c, :], c == 0, c == 7),
                     reads=[Bwup[j], BxT[b]], writes=[Bps[k]], signal=(c == 7 and half == 1))

    def S1(n):
        tt, j, b, k, ub, pk = pair_ctx(n)
        ho, hn = halo[tt % 2], halo[(tt + 1) % 2]
        Bho, Bhn = Bhalo[tt % 2], Bhalo[(tt + 1) % 2]
        ux = upx[ub]
        P.op("act", f_act(ux[:, :, 2:TT + 2], pk, AF.Copy), reads=[Bps[k]], writes=[Bupx[ub]])
        P.op("act", f_act(ux[:, :, 0:2], ho[:, j, :, :], AF.Copy), reads=[Bho], writes=[Bupx[ub]])
        P.op("act", f_act(hn[:, j, :, :], ux[:, :, TT:TT + 2], AF.Copy), reads=[Bupx[ub]], writes=[Bhn])

    def S2(n):
        tt, j, b, k, ub, pk = pair_ctx(n)
        uu, ux = u[ub], upx[ub]
        for half, jj in ((0, j), (1, j + 22)):
            P.op("act", f_act(uu[:, half, :], ux[:, half, 2:TT + 2], AF.Identity, bias=cw[:, jj, 3:4], scale=cw[:, jj, 2:3]),
                 reads=[Bupx[ub], Bcw], writes=[Bu[ub][half]])

    def S3(n):
        tt, j, b, k, ub, pk = pair_ctx(n)
        uu, ux = u[ub], upx[ub]
        for half, jj in ((0, j), (1, j + 22)):
            P.op("dve", f_stt(uu[:, half, :], ux[:, half, 1:TT + 1], cw[:, jj, 1:2], uu[:, half, :], ALU.mult, ALU.add),
                 reads=[Bupx[ub], Bcw, Bu[ub][half]], writes=[Bu[ub][half]])
        for half, jj in ((0, j), (1, j + 22)):
            P.op("dve", f_stt(uu[:, half, :], ux[:, half, 0:TT], cw[:, jj, 0:1], uu[:, half, :], ALU.mult, ALU.add),
                 reads=[Bupx[ub], Bcw, Bu[ub][half]], writes=[Bu[ub][half]])

    def S4(n):
        tt, j, b, k, ub, pk = pair_ctx(n)
        uu = u[ub]
        P.op("act", f_act(uu[:, 0, :], uu[:, 0, :], AF.Silu), reads=[Bu[ub][0]], writes=[Bu[ub][0]])

    def S5(n):
        tt, j, b, k, ub, pk = pair_ctx(n)
        uu = u[ub]
        P.op("dve", f_tt(hT[b][:, j, :], uu[:, 0, :], uu[:, 1, :], ALU.mult),
             reads=[Bu[ub][0], Bu[ub][1]], writes=[BhT[b]])

    def down_pieces(tt):
        b = tt % 2
        pieces = []

        def p_load():
            for s in range(2):
                r0 = tt * TT + s * 128
                P.dma("sp", xr[s][:], x_in[r0:r0 + 128, :], reads=[Bxin], writes=[Bxr[s]])
        pieces.append(p_load)

        def mk_mm(s, n):
            def f():
                k = dn_banks[s * 2 + n]
                for j in range(22):
                    P.op("pe", f_mm(ps[k][:, :], hT[b][:, j, s * 128:(s + 1) * 128], wdn[:, j, n * 512:(n + 1) * 512], j == 0, j == 21),
                         reads=[BhT[b], Bwdn], writes=[Bps[k]], signal=(j == 21))
            return f

        def mk_res(s, n):
            def f():
                k = dn_banks[s * 2 + n]
                zz = xr[s][:]
                P.op("dve", f_stt(zz[:, n * 512:(n + 1) * 512], zz[:, n * 512:(n + 1) * 512], ALPHA, ps[k][:, :], ALU.mult, ALU.add),
                     reads=[Bps[k]], writes=[Bxr[s]])
            return f

        for s in range(2):
            for n in range(2):
                pieces.append(mk_mm(s, n))
                if s * 2 + n >= 1:
                    s2, n2 = divmod(s * 2 + n - 1, 2)
                    pieces.append(mk_res(s2, n2))
        pieces.append(mk_res(1, 1))
        pieces.append(lambda: ln_stats(P, xr[0][:], Bxr[0], st, Bst, 0))
        pieces.append(lambda: ln_stats(P, xr[1][:], Bxr[1], st, Bst, 1))
        pieces.append(lambda: ln_rstd(P, st, Bst, 2))
        for s in range(2):
            r0 = tt * TT + s * 128
            pieces.append((lambda s=s: P.op("dve", f_ts(xr[s][:], xr[s][:], st[:, s, 12:13], st[:, s, 14:15], ALU.subtract, ALU.mult),
                                            reads=[Bxr[s], Bst], writes=[Bxr[s]])))
            pieces.append((lambda s=s: P.op("dve", f_tt(xr[s][:], xr[s][:], gb[:, 0, :], ALU.mult), reads=[Bxr[s], Bgb], writes=[Bxr[s]])))
            pieces.append((lambda s=s, r0=r0: (P.op("dve", f_tt(xr[s][:], xr[s][:], gb[:, 1, :], ALU.add), reads=[Bxr[s], Bgb], writes=[Bxr[s]]),
                                               P.dma("sp", x_out[r0:r0 + 128, :], xr[s][:], reads=[Bxr[s]], writes=[Bxout]))))
        return pieces

    load_xb(0)
    load_xb(1)
    if not preloaded:
        ffn_load_weights(P, W, w_up, w_down, queue=wqueue)
    prep_xT(0)
    NP = NT * 22
    SKEW = ((S0, 0), (S1, 1), (S2, 1), (S3, 2), (S4, 3), (S5, 3 if NSET == 4 else 4))
    pending = []
    for n in range(NP + 26):
        for fn, lag in SKEW:
            m = n - lag
            if 0 <= m < NP:
                fn(m)
        tt, j = divmod(n, 22)
        if j == 4 and 1 <= tt <= NT:
            pending = down_pieces(tt - 1)
        if pending:
            pending.pop(0)()
        if n < NP:
            if j == 12 and tt + 1 < NT:
                prep_xT(tt + 1)
            if j == 16 and tt + 2 < NT:
                load_xb(tt + 2)
    assert not pending
    P.barrier()
    A.off = save


def new_ctx():
    nc = bass.Bass("TRN2", target_bir_lowering=False)
    C = Ctx()
    C.nc = nc
    C.A = Arena(nc)
    C.ps = [nc.alloc_psum_tensor("psb%d" % i, [128, 512], F32) for i in range(8)]
    C.Bps = [Buf("ps%d" % i, excl=True) for i in range(8)]
    return nc, C


def host_cwb(conv_w, conv_b):
    a = np.concatenate([conv_w, conv_b[None, :]], axis=0)
    return np.ascontiguousarray(a.reshape(4, 44, 128).transpose(2, 1, 0))


def build_ffn_only():
    nc, C = new_ctx()
    x_in = nc.dram_tensor("x_in", [S, D], F32, kind="ExternalInput").ap()
    w_up = nc.dram_tensor("w_up", [D, 2 * FF], F32, kind="ExternalInput").ap()
    cwb = nc.dram_tensor("cwb", [128, 44, 4], F32, kind="ExternalInput").ap()
    w_dn = nc.dram_tensor("w_dn", [FF, D], F32, kind="ExternalInput").ap()
    g = nc.dram_tensor("ln_g", [D], F32, kind="ExternalInput").ap()
    b = nc.dram_tensor("ln_b", [D], F32, kind="ExternalInput").ap()
    x_out = nc.dram_tensor("x_out", [S, D], F32, kind="ExternalOutput").ap()
    P = Prog(nc)
    make_consts(P, C)
    ffn_phase(P, C, x_in, Buf("xin"), x_out, Buf("xout"), w_up, cwb, w_dn, g, b)
    P.finish()
    P.emit()
    return nc, P


DBG = {}
NWIN = 2696


def mixer0_phase(P, C, x_in, Bxin, x_out, Bxout, prm, tab, Btab, bgq=None):
    nc, A = C.nc, C.A
    save = A.off
    ps, Bps = C.ps, C.Bps
    NT = S // 128
    ATT_SCALE = float(192 ** -0.5)
    LNK = float(np.log(128 ** -0.5))
    PI = float(np.pi)
    win = A.alloc("win", [128, 8, NWIN], BF16)
    wuq = A.alloc("wuq", [128, 2, 1024], BF16)
    wukv = A.alloc("wukv", [128, 1024], BF16)
    wout = A.alloc("wout", [128, 8, D], BF16)
    off_kn = A.off
    knT = A.alloc("knT", [128, 4, S], BF16)
    krT = A.alloc("krT", [128, S], BF16)
    vaug = A.alloc("vaug", [128, NT, 4, 130], BF16)
    gb = A.alloc("gb", [128, 2, D], F32)
    gmn = A.alloc("gmn", [128, 512], F32)
    bg = A.alloc("bg", [128, 8], F32)
    frq = A.alloc("frq", [128, 1], F32)
    cst = A.alloc("cst", [128, 4], F32)
    U32 = A.alloc("U32", [128, 4, 129], F32)
    Cbf = A.alloc("Cbf", [128, 4, 130], BF16)
    mask4 = A.alloc("mask4", [128, 512], BF16)
    tri = A.alloc("tri", [128, 128], F32)
    ones = A.alloc("ones", [128, 128], F32)
    Bw = Buf("w")
    Bcst, BU, BC = Buf("cst"), Buf("U32"), Buf("Cbf")
    xb = [A.alloc("xb", [128, D], BF16) for _ in range(2)]; Bxb = [Buf("xb0"), Buf("xb1")]
    xT = A.alloc("xT", [128, 8, 128], BF16); BxT = Buf("xT")
    qmT = [A.alloc("qmT", [128, 4, 128], BF16) for _ in range(2)]; BqmT = [Buf("qmT0"), Buf("qmT1")]
    kmT = [A.alloc("kmT", [128, 4, 128], BF16) for _ in range(2)]; BkmT = [Buf("kmT0"), Buf("kmT1")]
    ktok = [A.alloc("ktok", [128, 512], BF16) for _ in range(2)]; Bktok = [Buf("ktok0"), Buf("ktok1")]
    vs = [A.alloc("vs", [128, 512], F32) for _ in range(2)]; Bvs = [Buf("vs0"), Buf("vs1")]
    og = [A.alloc("og", [128, 512], F32) for _ in range(2)]; Bog = [Buf("og0"), Buf("og1")]
    gA = [A.alloc("gA", [128, 32], F32) for _ in range(2)]; BgA = [Buf("gA0"), Buf("gA1")]
    ebt = [A.alloc("ebt", [128, 4], F32) for _ in range(3)]; Bebt = [Buf("ebt0"), Buf("ebt1"), Buf("ebt2")]
    gC = A.alloc("gC", [128, 8], F32); BgC = Buf("gC")
    gB = A.alloc("gB", [128, 16], F32); BgB = Buf("gB")
    vpa = A.alloc("vpa", [128, 4, 130], BF16); Bvpa = Buf("vpa")
    APT = A.alloc("APT", [128, 512], BF16); BAPT = Buf("APT")
    Xs = A.alloc("Xs", [128, 4, 129], F32); BXs = Buf("Xs")
    junk1 = A.alloc("junk1", [128, 256], F32); Bjunk1 = Buf("junk1")
    junk2, Bjunk2 = junk1, Bjunk1
    cq32 = A.alloc("cq32", [128, 384], F32); Bcq = Buf("cq32")
    cqn = A.alloc("cqn", [128, 384], BF16); Bcqn = Buf("cqn")
    cqT = A.alloc("cqT", [128, 3, 128], BF16); BcqT = Buf("cqT")
    q2 = [A.alloc("q2", [128, 6, 256], BF16) for _ in range(3)]; Bq2 = [[Buf("q2"), Buf("q2")] for _ in range(3)]
    cs = [A.alloc("cs", [128, 2, 128], F32) for _ in range(2)]; Bcs = [Buf("cs0"), Buf("cs1")]
    rt1 = A.alloc("rt1", [128, 128], F32); rt2 = A.alloc("rt2", [128, 128], F32); Brt = Buf("rt")
    LA = DBG.get("la", 1)
    PT = [A.alloc("PT", [128, 512], BF16) for _ in range(LA + 1)]; BPT = [Buf("PT%d" % i) for i in range(LA + 1)]
    SCB = [2, 3] if LA == 1 else [2, 3, 4]
    ym = [A.alloc("ym", [128, 512], BF16) for _ in range(4)]; Bym = [Buf("ym%d" % i) for i in range(4)]
    ya = [A.alloc("ya", [128, 512], BF16) for _ in range(2)]; Bya = [Buf("ya0"), Buf("ya1")]
    yT = A.alloc("yT", [128, 8, 128], BF16); ByT = Buf("yT")
    st = A.alloc("st", [128, 1, 16], F32); Bst = Buf("st")
    rc = A.alloc("rc", [128, 4], F32); Brc = Buf("rc")
    xr = A.alloc("xr", [128, D], F32); Bxr = Buf("xr")
    work_end = A.off
    Bkn = [Buf("kn%d" % i) for i in range(NT)]
    Bkr = [Buf("kr%d" % i) for i in range(NT)]
    Bva = [Buf("va%d" % i) for i in range(NT)]

    bank_ctr = [0]

    def nb():
        k = 2 + bank_ctr[0] % 6
        bank_ctr[0] += 1
        return k

    P.op("pool", f_memset(cst[:, 0:1], 1.0), writes=[Bcst])
    P.op("pool", f_memset(cst[:, 1:2], LNK), writes=[Bcst])
    P.op("pool", f_memset(cst[:, 2:3], RMS_EPS), writes=[Bcst])
    P.op("pool", f_memset(ones[:], 1.0), writes=[Bcst])
    P.op("pool", f_memset(tri[:], 1.0), writes=[Bcst])
    P.op("pool", lambda e: e.affine_select(out=tri[:], in_=tri[:], compare_op=ALU.is_ge, fill=0.0, base=0,
                                           pattern=[[1, 128]], channel_multiplier=-1), reads=[Bcst], writes=[Bcst])
    for h in range(4):
        P.op("pool", f_copy(mask4[:, h * 128:(h + 1) * 128], tri[:]), reads=[Bcst], writes=[Bcst])
    P.op("pool", f_memset(U32[:], 0.0), writes=[BU])
    P.op("pool", f_memset(Cbf[:], 0.0), writes=[BC])
    P.op("pool", f_memset(vaug[:, :, :, 128:129], 1.0), writes=Bva)
    P.op("pool", f_memset(ebt[2][:], 1.0), writes=[Bebt[2]])
    Bfrq = Buf("frq")
    w_in_v = prm["w_in"].rearrange("(c p) n -> p c n", p=128)
    for c in range(8):
        P.dma("pool", win[:, c, 0:2440], w_in_v[:, c, 0:2440], writes=[Buf()])
    w_out_v = prm["w_out"].rearrange("(c p) n -> p c n", p=128)
    for c0 in range(0, 8, 4):
        P.dma("pool", wout[:, c0:c0 + 4, :], w_out_v[:, c0:c0 + 4, :], writes=[Buf()])
    P.dma("sp", gb[:, 0, :], prm["ln_g"].partition_broadcast(128), writes=[Buf()])
    P.dma("sp", gb[:, 1, :], prm["ln_b"].partition_broadcast(128), writes=[Buf()])
    P.dma("sp", gmn[:], prm["mnorm"].partition_broadcast(128), writes=[Buf()])
    P.dma("sp", bg[:], prm["bgate"].partition_broadcast(128), writes=[Buf()])
    P.dma("sp", frq[:], prm["rope_freq"], writes=[Bfrq])
    A.off = off_kn
    krs = A.alloc("krs", [128, 8, 64], F32)
    wqs = A.alloc("wqs", [128, 2, 768], F32)
    wks = A.alloc("wks", [128, 1024], F32)
    gq = A.alloc("gq", [128, 2], F32)
    gkv = A.alloc("gkv", [128, 1], F32)
    Bstg = Buf("stg")
    Bw2 = Buf("w2")
    BstgL = [Buf("stg%d" % i) for i in range(5)]
    P.dma("sp", krs[:], w_in_v[:, :, 2440:2504], writes=[BstgL[0]])
    P.dma("sp", wqs[:], prm["w_uq"].rearrange("(c p) n -> p c n", p=128), writes=[BstgL[1]])
    P.dma("sp", wks[:], prm["w_ukv"], writes=[BstgL[2]])
    P.dma("sp", gq[:], prm["qnorm"], writes=[BstgL[3]])
    P.dma("sp", gkv[:], prm["kvnorm"], writes=[BstgL[4]])
    def staging_ops():
        for o in (2440, 2504):
            P.op("dve", f_copy(win[:, :, o:o + 64], krs[:]), reads=BstgL, writes=[Bw2])
        for o in (2568, 2632):
            P.op("dve", f_ts(win[:, :, o:o + 32], krs[:, :, 32:64], -1.0, None, ALU.mult), reads=BstgL, writes=[Bw2])
            P.op("dve", f_copy(win[:, :, o + 32:o + 64], krs[:, :, 0:32]), reads=BstgL, writes=[Bw2])
        for c in range(2):
            src = wqs[:, c, :].rearrange("p (h d) -> p h d", d=192)
            g1 = gq[:, c:c + 1]
            P.op("dve", f_ts(wuq[:, c, 0:512].rearrange("p (h d) -> p h d", d=128), src[:, :, 0:128], g1, None, ALU.mult),
                 reads=BstgL, writes=[Bw2])
            P.op("dve", f_ts(wuq[:, c, 512:768].rearrange("p (h d) -> p h d", d=64), src[:, :, 128:192], g1, None, ALU.mult),
                 reads=BstgL, writes=[Bw2])
            rot = wuq[:, c, 768:1024].rearrange("p (h d) -> p h d", d=64)
            P.op("dve", f_ts(rot[:, :, 0:32], src[:, :, 160:192], g1, -1.0, ALU.mult, ALU.mult), reads=BstgL, writes=[Bw2])
            P.op("dve", f_ts(rot[:, :, 32:64], src[:, :, 128:160], g1, None, ALU.mult), reads=BstgL, writes=[Bw2])
        wk4 = wks[:, :].rearrange("p (h d) -> p h d", d=256)
        P.op("dve", f_ts(wukv[:, 0:512].rearrange("p (h d) -> p h d", d=128), wk4[:, :, 0:128], gkv[:, 0:1], None, ALU.mult),
             reads=BstgL, writes=[Bw2])
        P.op("dve", f_ts(wukv[:, 512:1024].rearrange("p (h d) -> p h d", d=128), wk4[:, :, 128:256], gkv[:, 0:1], None, ALU.mult),
             reads=BstgL, writes=[Bw2])

    CH = 1024
    posi = A.alloc("posi", [128, CH], I32)
    ang = A.alloc("ang", [128, CH], F32)
    ki = A.alloc("ki", [128, CH], I32)
    kf = A.alloc("kf", [128, CH], F32)
    r2 = A.alloc("r2", [128, CH], F32)
    sn = A.alloc("sn", [128, CH], F32)
    Bt = Buf("ropetmp")
    C1 = 6.28125
    C2 = float(2 * np.pi - 6.28125)

    def wrap(r):
        P.op("dve", f_ts(kf[:], r[:], PI, None, ALU.is_gt), reads=[Bt], writes=[Bt])
        P.op("dve", f_stt(r[:], kf[:], -2 * PI, r[:], ALU.mult, ALU.add), reads=[Bt], writes=[Bt])
        P.op("dve", f_ts(kf[:], r[:], -PI, None, ALU.is_lt), reads=[Bt], writes=[Bt])
        P.op("dve", f_stt(r[:], kf[:], 2 * PI, r[:], ALU.mult, ALU.add), reads=[Bt], writes=[Bt])

    for ch in range(S // CH):
        P.dma("sp", posi[:], prm["pos"][ch * CH:(ch + 1) * CH].partition_broadcast(128), writes=[Bt])
        P.op("dve", f_copy(ang[:], posi[:]), reads=[Bt], writes=[Bt])
        P.op("dve", f_ts(ang[:], ang[:], frq[:, 0:1], None, ALU.mult), reads=[Bt, Bfrq], writes=[Bt])
        P.op("dve", f_ts(ki[:], ang[:], float(1 / (2 * np.pi)), None, ALU.mult), reads=[Bt], writes=[Bt])
        P.op("dve", f_copy(kf[:], ki[:]), reads=[Bt], writes=[Bt])
        P.op("dve", f_stt(ang[:], kf[:], -C1, ang[:], ALU.mult, ALU.add), reads=[Bt], writes=[Bt])
        P.op("dve", f_stt(ang[:], kf[:], -C2, ang[:], ALU.mult, ALU.add), reads=[Bt], writes=[Bt])
        wrap(ang)
        P.op("dve", f_ts(r2[:], ang[:], PI / 2, None, ALU.add), reads=[Bt], writes=[Bt])
        wrap(r2)
        P.op("act", f_act(sn[:], r2[:], AF.Sin), reads=[Bt], writes=[Bt])
        P.dma("sp", tab[0, :, ch * CH:(ch + 1) * CH], sn[:], reads=[Bt], writes=[Btab])
        P.op("act", f_act(sn[:], ang[:], AF.Sin), reads=[Bt], writes=[Bt])
        P.dma("sp", tab[1, :, ch * CH:(ch + 1) * CH], sn[:], reads=[Bt], writes=[Btab])
        if ch == 0:
            staging_ops()
    P.barrier()
    A.off = work_end

    def load_x(i):
        P.dma("pool", xb[i % 2][:], x_in[i * 128:(i + 1) * 128, :], reads=[Bxin], writes=[Bxb[i % 2]])
        P.dma("sp", cs[i % 2][:], tab[:, :, i * 128:(i + 1) * 128].rearrange("a p t -> p a t"), reads=[Btab], writes=[Bcs[i % 2]])

    def tpv(k):
        return ps[k][:, :].bitcast(BF16)

    bctr = {"m1": 0, "m2": 0}

    def nb1():
        bctr["m1"] += 1
        return 6 + bctr["m1"] % 2

    def nb2():
        bctr["m2"] += 1
        return 4 + bctr["m2"] % 2

    def M1(i):
        b = i % 2
        r0 = i * 128
        if i + 1 < NT:
            load_x(i + 1)
        if bgq and i >= 1:
            bgq.pop(0)()
        k = nb1()
        for c in range(8):
            P.op("pe", f_tr(tpv(k)[:, c * 128:(c + 1) * 128], xb[b][:, c * 128:(c + 1) * 128], C.ident[:]),
                 reads=[Bxb[b], C.Bident], writes=[Bps[k]], signal=(c == 7))
        P.op("dve", f_copy(xT[:], tpv(k).rearrange("p (c t) -> p c t", c=8)), reads=[Bps[k]], writes=[BxT])
        yield
        for (dst, Bdst, c0, eng) in ((qmT[b], BqmT[b], 0, "act"), (kmT[b], BkmT[b], 512, "dve")):
            k = nb1()
            for h in range(4):
                for c in range(8):
                    P.op("pe", f_mm(ps[k][:, h * 128:(h + 1) * 128], win[:, c, c0 + h * 128:c0 + (h + 1) * 128], xT[:, c, :], c == 0, c == 7),
                         reads=[Bw, BxT], writes=[Bps[k]], signal=(c == 7 and h == 3))
                if DBG.get("coarse", 1) < 1:
                    yield
            if eng == "act":
                P.op("act", f_act(dst[:].rearrange("p h t -> p (h t)"), ps[k][:, :], AF.Copy), reads=[Bps[k]], writes=[Bdst])
            else:
                P.op("dve", f_copy(dst[:].rearrange("p h t -> p (h t)"), ps[k][:, :]), reads=[Bps[k]], writes=[Bdst])
            yield
        k = nb1()
        for m in range(2):
            for c in range(8):
                P.op("pe", f_mm(ps[k][:, m * 128:(m + 1) * 128], win[:, c, 2440 + m * 128:2440 + (m + 1) * 128], xT[:, c, :], c == 0, c == 7),
                     reads=[Bw, BxT], writes=[Bps[k]], signal=(c == 7 and m == 1))
        P.op("dve", f_tt(rt1[:], ps[k][:, 0:128], cs[b][:, 0, :], ALU.mult), reads=[Bps[k], Bcs[b]], writes=[Brt])
        P.op("dve", f_tt(rt2[:], ps[k][:, 128:256], cs[b][:, 1, :], ALU.mult), reads=[Bps[k], Bcs[b]], writes=[Brt])
        P.op("dve", f_tt(krT[:, r0:r0 + 128], rt1[:], rt2[:], ALU.add), reads=[Brt], writes=[Bkr[i]])
        yield
        g = gA[b]
        Bg = BgA[b]
        for gi, (c0, c1) in enumerate(((512, 1024), (1024, 1536), (1536, 2048), (2048, 2440))):
            k = nb1()
            for c in range(8):
                P.op("pe", f_mm(ps[k][:, 0:c1 - c0], xT[:, c, :], win[:, c, c0:c1], c == 0, c == 7),
                     reads=[Bw, BxT], writes=[Bps[k]], signal=(c == 7))
            if DBG.get("coarse", 1) < 2:
                yield
            if gi == 0:
                P.op("act", f_act(ktok[b][:], ps[k][:, :], AF.Copy), reads=[Bps[k]], writes=[Bktok[b]])
            elif gi == 1:
                P.op("dve", f_copy(vs[b][:], ps[k][:, :]), reads=[Bps[k]], writes=[Bvs[b]])
            elif gi == 2:
                P.op("act", f_act(og[b][:], ps[k][:, :], AF.Exp, scale=-1.0), reads=[Bps[k]], writes=[Bog[b]])
            else:
                P.op("dve", f_tt(g[:, 0:8], ps[k][:, 0:8], bg[:], ALU.add), reads=[Bps[k], Bw], writes=[Bg])
                P.op("act", f_act(cq32[:], ps[k][:, 8:392], AF.Copy), reads=[Bps[k]], writes=[Bcq])
            yield
        P.op("dve", f_ts(og[b][:], og[b][:], 1.0, None, ALU.add), reads=[Bog[b]], writes=[Bog[b]])
        P.op("dve", lambda e: e.reciprocal(og[b][:], og[b][:]), reads=[Bog[b]], writes=[Bog[b]])
        yield
        P.op("act", f_act(g[:, 8:12], g[:, 4:8], AF.Exp, scale=-1.0), reads=[Bg], writes=[Bg])
        P.op("act", f_act(g[:, 8:12], g[:, 8:12], AF.Ln, bias=cst[:, 0:1]), reads=[Bg, Bcst], writes=[Bg])
        yield
        k = nb1()
        P.op("pe", f_mm(ps[k][:, 0:4], tri[:], g[:, 8:12], True, True), reads=[Bcst, Bg], writes=[Bps[k]], signal=False)
        P.op("pe", f_mm(ps[k][:, 4:8], ones[:], g[:, 8:12], True, True), reads=[Bcst, Bg], writes=[Bps[k]])
        P.op("dve", f_copy(g[:, 12:20], ps[k][:, 0:8]), reads=[Bps[k]], writes=[Bg])
        P.op("dve", f_tt(g[:, 20:24], g[:, 0:4], g[:, 12:16], ALU.add), reads=[Bg], writes=[Bg])
        yield
        P.op("act", f_act(g[:, 24:28], g[:, 20:24], AF.Exp, bias=cst[:, 1:2]), reads=[Bg, Bcst], writes=[Bg])
        P.op("act", f_act(g[:, 28:32], g[:, 12:16], AF.Exp, scale=-1.0), reads=[Bg], writes=[Bg])
        P.op("act", f_act(ebt[i % 3][:], g[:, 16:20], AF.Exp, scale=-1.0), reads=[Bg], writes=[Bebt[i % 3]])
        yield
        P.op("pool", f_memset(gC[:, 0:2], 0.0), reads=[BgC], writes=[BgC])
        P.op("act", f_act(junk1[:, 0:256], cq32[:, 0:256], AF.Square, accum_out=gC[:, 0:1]), reads=[Bcq], writes=[BgC, Bjunk1])
        P.op("act", f_act(junk1[:, 0:128], cq32[:, 256:384], AF.Square, accum_out=gC[:, 1:2]), reads=[Bcq], writes=[BgC, Bjunk1])
        P.op("act", f_act(gC[:, 2:3], gC[:, 0:1], AF.Ln, bias=cst[:, 2:3], scale=1.0 / 256), reads=[BgC, Bcst], writes=[BgC])
        P.op("act", f_act(gC[:, 3:4], gC[:, 1:2], AF.Ln, bias=cst[:, 2:3], scale=1.0 / 128), reads=[BgC, Bcst], writes=[BgC])
        P.op("act", f_act(gC[:, 2:4], gC[:, 2:4], AF.Exp, scale=-0.5), reads=[BgC], writes=[BgC])
        yield
        P.op("dve", f_ts(cqn[:, 0:256], cq32[:, 0:256], gC[:, 2:3], None, ALU.mult), reads=[Bcq, BgC], writes=[Bcqn])
        P.op("dve", f_ts(cqn[:, 256:384], cq32[:, 256:384], gC[:, 3:4], None, ALU.mult), reads=[Bcq, BgC], writes=[Bcqn])
        k = nb1()
        for c in range(3):
            P.op("pe", f_tr(tpv(k)[:, c * 128:(c + 1) * 128], cqn[:, c * 128:(c + 1) * 128], C.ident[:]),
                 reads=[Bcqn, C.Bident], writes=[Bps[k]], signal=(c == 2))
        P.op("dve", f_copy(cqT[:].rearrange("p c t -> p (c t)"), tpv(k)[:, 0:384]), reads=[Bps[k]], writes=[BcqT])
        yield
        q3 = q2[(i // 2) % 3][:, :, (i % 2) * 128:(i % 2 + 1) * 128]
        Bq3 = Bq2[(i // 2) % 3][i % 2]
        ka = nb1()
        for m in range(4):
            for c in range(2):
                P.op("pe", f_mm(ps[ka][:, m * 128:(m + 1) * 128], wuq[:, c, m * 128:(m + 1) * 128], cqT[:, c, :], c == 0, c == 1),
                     reads=[Bw, BcqT], writes=[Bps[ka]], signal=(c == 1 and m == 3))
        P.op("act", f_act(q3[:, 0:4, :], ps[ka][:, :].rearrange("p (h t) -> p h t", h=4), AF.Copy), reads=[Bps[ka]], writes=[Bq3])
        yield
        kb_ = nb1()
        for m in range(4, 8):
            for c in range(2):
                P.op("pe", f_mm(ps[kb_][:, (m - 4) * 128:(m - 3) * 128], wuq[:, c, m * 128:(m + 1) * 128], cqT[:, c, :], c == 0, c == 1),
                     reads=[Bw, BcqT], writes=[Bps[kb_]], signal=(c == 1 and m == 7))
        for blk in range(2):
            P.op("dve", f_tt(rt1[:], ps[kb_][:, blk * 128:(blk + 1) * 128], cs[b][:, 0, :], ALU.mult), reads=[Bps[kb_], Bcs[b]], writes=[Brt])
            P.op("dve", f_tt(rt2[:], ps[kb_][:, (2 + blk) * 128:(3 + blk) * 128], cs[b][:, 1, :], ALU.mult), reads=[Bps[kb_], Bcs[b]], writes=[Brt])
            P.op("dve", f_tt(q3[:, 4 + blk, :], rt1[:], rt2[:], ALU.add), reads=[Brt], writes=[Bq3])
        yield
        k = nb1()
        for h in range(4):
            P.op("pe", f_mm(ps[k][:, h * 128:(h + 1) * 128], wukv[:, h * 128:(h + 1) * 128], cqT[:, 2, :], True, True),
                 reads=[Bw, BcqT], writes=[Bps[k]], signal=(h == 3))
        P.op("act", f_act(knT[:, :, r0:r0 + 128], ps[k][:, :].rearrange("p (h t) -> p h t", h=4), AF.Copy), reads=[Bps[k]], writes=[Bkn[i]])
        yield
        k = nb1()
        P.op("pe", f_mm(ps[k][:, :], cqT[:, 2, :], wukv[:, 512:1024], True, True), reads=[Bw, BcqT], writes=[Bps[k]])
        P.op("dve", f_copy(vaug[:, i, :, 0:128], ps[k][:, :].rearrange("p (h d) -> p h d", h=4)), reads=[Bps[k]], writes=[Bva[i]])
        yield

    def M2(i):
        b = i % 2
        g, Bg = gA[b], BgA[b]
        eb, Beb = ebt[i % 3], Bebt[i % 3]
        ebp, Bebp = ebt[(i - 1) % 3], Bebt[(i - 1) % 3]
        hm3 = Xs[:, :, 0:128]
        k = nb2()
        for h in range(4):
            P.op("pe", f_mm(ps[k][:, h * 128:(h + 1) * 128], kmT[b][:, h, :], qmT[b][:, h, :], True, True),
                 reads=[BkmT[b], BqmT[b]], writes=[Bps[k]], signal=(h == 3))
        P.op("dve", f_tt(APT[:], ps[k][:, :], mask4[:], ALU.mult), reads=[Bps[k], Bcst], writes=[BAPT])
        yield
        for h in range(4):
            P.op("act", f_act(vpa[:, h, 0:128], vs[b][:, h * 128:(h + 1) * 128], AF.Copy, scale=g[:, 24 + h:25 + h]),
                 reads=[Bvs[b], Bg], writes=[Bvpa])
        P.op("pool", f_copy(vpa[:, :, 128:129], g[:, 24:28].rearrange("p (h o) -> p h o", o=1)), reads=[Bg], writes=[Bvpa])
        yield
        kx = [nb2(), nb2()]
        for h in range(4):
            o_ = ps[kx[h // 2]][:, (h % 2) * 129:(h % 2) * 129 + 129]
            P.op("pe", f_mm(o_, APT[:, h * 128:(h + 1) * 128], vpa[:, h, 0:129], True, False), reads=[BAPT, Bvpa], writes=[Bps[kx[h // 2]]], signal=False)
            P.op("pe", f_mm(o_, qmT[b][:, h, :], Cbf[:, h, 0:129], False, True), reads=[BqmT[b], BC], writes=[Bps[kx[h // 2]]], signal=(h % 2 == 1))
            if h % 2 == 1:
                j = h // 2
                P.op("act", f_act(Xs[:, 2 * j:2 * j + 2, :].rearrange("p h d -> p (h d)"), ps[kx[j]][:, 0:258], AF.Copy), reads=[Bps[kx[j]]], writes=[BXs])
                yield
        kd = [nb2(), nb2()]
        for h in range(4):
            o_ = ps[kd[h // 2]][:, (h % 2) * 129:(h % 2) * 129 + 129]
            P.op("pe", f_mm(o_, ktok[b][:, h * 128:(h + 1) * 128], vpa[:, h, 0:129], True, True), reads=[Bktok[b], Bvpa], writes=[Bps[kd[h // 2]]], signal=(h % 2 == 1))
        yield
        for h in range(4):
            o_ = ps[kd[h // 2]][:, (h % 2) * 129:(h % 2) * 129 + 129]
            P.op("dve", f_stt(U32[:, h, :], U32[:, h, :], ebp[:, h:h + 1], o_, ALU.mult, ALU.add),
                 reads=[Bebp, Bps[kd[h // 2]]], writes=[BU])
        yield
        for h in range(4):
            P.op("act", f_act(Cbf[:, h, 0:129], U32[:, h, :], AF.Copy, scale=eb[:, h:h + 1]), reads=[BU, Beb], writes=[BC])
        yield
        P.op("dve", f_tt(gB[:, 0:4], Xs[:, :, 128:129].rearrange("p h o -> p (h o)"), g[:, 28:32], ALU.mult), reads=[BXs, Bg], writes=[BgB])
        P.op("act", f_act(gB[:, 0:4], gB[:, 0:4], AF.Abs), reads=[BgB], writes=[BgB])
        P.op("dve", f_ts(gB[:, 0:4], gB[:, 0:4], 1.0, None, ALU.max), reads=[BgB], writes=[BgB])
        P.op("dve", lambda e: e.reciprocal(gB[:, 4:8], gB[:, 0:4]), reads=[BgB], writes=[BgB])
        P.op("dve", f_tt(gB[:, 4:8], gB[:, 4:8], g[:, 28:32], ALU.mult), reads=[BgB, Bg], writes=[BgB])
        yield
        P.op("pool", f_memset(gB[:, 8:12], 0.0), reads=[BgB], writes=[BgB])
        for h in range(4):
            P.op("act", f_act(Xs[:, h, 0:128], Xs[:, h, 0:128], AF.Copy, scale=gB[:, 4 + h:5 + h]), reads=[BgB], writes=[BXs])
        yield
        for h in range(4):
            P.op("act", f_act(junk2[:, 0:128], Xs[:, h, 0:128], AF.Square, accum_out=gB[:, 8 + h:9 + h]), reads=[BXs], writes=[BgB, Bjunk2])
        P.op("act", f_act(gB[:, 12:16], gB[:, 8:12], AF.Ln, bias=cst[:, 2:3], scale=1.0 / 128), reads=[BgB, Bcst], writes=[BgB])
        P.op("act", f_act(gB[:, 12:16], gB[:, 12:16], AF.Exp, scale=-0.5), reads=[BgB], writes=[BgB])
        yield
        P.op("dve", f_tt(hm3, hm3, gmn[:, :].rearrange("p (h d) -> p h d", h=4), ALU.mult), reads=[Bw], writes=[BXs])
        P.op("dve", f_tt(hm3, hm3, og[b][:, :].rearrange("p (h d) -> p h d", h=4), ALU.mult), reads=[Bog[b]], writes=[BXs])
        yield
        ymi, Bymi = ym[i % 4], Bym[i % 4]
        for h in range(4):
            P.op("act", f_act(ymi[:, h * 128:(h + 1) * 128], Xs[:, h, 0:128], AF.Copy, scale=gB[:, 12 + h:13 + h]), reads=[BXs, BgB], writes=[Bymi])
        yield

    def M3(p):
        i0, i1 = 2 * p, 2 * p + 1
        q2p = q2[p % 3]
        Bq = Bq2[p % 3]
        G = [(h, g) for h in range(4) for g in range(p + 1)]

        def scores(n):
            h, g = G[n]
            k = SCB[n % len(SCB)]
            pr = (h % 2) * 64
            for j, kb in enumerate((2 * g, 2 * g + 1)):
                o_ = ps[k][:, j * 256:(j + 1) * 256]
                P.op("pe", f_mm(o_, knT[:, h, kb * 128:(kb + 1) * 128], q2p[:, h, :], True, False), reads=[Bkn[kb], Bq[0], Bq[1]], writes=[Bps[k]], signal=False)
                P.op("pe", f_mm(o_, krT[pr:pr + 64, kb * 128:(kb + 1) * 128], q2p[pr:pr + 64, 4 + h // 2, :], False, True),
                     reads=[Bkr[kb], Bq[0], Bq[1]], writes=[Bps[k]], signal=(j == 1))

        def expmask(n):
            h, g = G[n]
            k = SCB[n % len(SCB)]
            pb = n % (LA + 1)
            P.op("act", f_act(PT[pb][:, :], ps[k][:, :], AF.Exp, scale=ATT_SCALE), reads=[Bps[k]], writes=[BPT[pb]])
            if g == p:
                for off in (0, 256 + 128):
                    dsl = PT[pb][:, off:off + 128]
                    P.op("pool", (lambda d: (lambda e: e.affine_select(out=d, in_=d, compare_op=ALU.is_ge, fill=0.0, base=0,
                                                                      pattern=[[1, 128]], channel_multiplier=-1)))(dsl),
                         reads=[BPT[pb]], writes=[BPT[pb]])

        def pv(n):
            h, g = G[n]
            pb = n % (LA + 1)
            for j, kb in enumerate((2 * g, 2 * g + 1)):
                for t in range(2):
                    if kb > 2 * p + t:
                        continue
                    acc = ps[t][:, (h % 2) * 129:(h % 2) * 129 + 129]
                    P.op("pe", f_mm(acc, PT[pb][:, j * 256 + t * 128:j * 256 + (t + 1) * 128], vaug[:, kb, h, 0:129], kb == 0, kb == 2 * p + t),
                         reads=[BPT[pb], Bva[kb]], writes=[Bps[t]], signal=(j == 1 and t == 1))

        if not DBG.get("skip_att"):
            for n0 in range(min(LA, len(G))):
                scores(n0)
            for n in range(len(G)):
                if n + LA < len(G):
                    scores(n + LA)
                expmask(n)
                pv(n)
                h, g = G[n]
                if g == p:
                    for t in range(2):
                        a2 = ps[t][:, (h % 2) * 129:(h % 2) * 129 + 129]
                        P.op("dve", lambda e, a2=a2, c_=2 * (h % 2) + t: e.reciprocal(rc[:, c_:c_ + 1], a2[:, 128:129]), reads=[Bps[t]], writes=[Brc])
                        P.op("dve", f_ts(ya[t][:, h * 128:(h + 1) * 128], a2[:, 0:128], rc[:, 2 * (h % 2) + t:2 * (h % 2) + t + 1], None, ALU.mult),
                             reads=[Bps[t], Brc], writes=[Bya[t]])
                if n % DBG.get("m3g", 1) == 0:
                    yield
        for t, i in enumerate((i0, i1)):
            r0 = i * 128
            P.dma("sp", xr[:], x_in[r0:r0 + 128, :], reads=[Bxin], writes=[Bxr])
            k = 2
            for c in range(8):
                src = ym[i % 4][:, c * 128:(c + 1) * 128] if c < 4 else ya[t][:, (c - 4) * 128:(c - 3) * 128]
                P.op("pe", f_tr(tpv(k)[:, c * 128:(c + 1) * 128], src, C.ident[:]),
                     reads=[Bym[i % 4], Bya[t], C.Bident], writes=[Bps[k]], signal=(c == 7))
            P.op("dve", f_copy(yT[:].rearrange("p c t -> p (c t)"), tpv(k)), reads=[Bps[k]], writes=[ByT])
            yield
            ko = [3, 2]
            for n in range(2):
                for c in range(8):
                    P.op("pe", f_mm(ps[ko[n]][:, :], yT[:, c, :], wout[:, c, n * 512:(n + 1) * 512], c == 0, c == 7),
                         reads=[ByT, Bw], writes=[Bps[ko[n]]], signal=(c == 7))
                yield
            for n in range(2):
                P.op("dve", f_stt(xr[:, n * 512:(n + 1) * 512], xr[:, n * 512:(n + 1) * 512], ALPHA, ps[ko[n]][:, :], ALU.mult, ALU.add),
                     reads=[Bps[ko[n]]], writes=[Bxr])
            yield
            ln_stats(P, xr[:], Bxr, st, Bst, 0)
            ln_rstd(P, st, Bst, 1)
            yield
            ln_apply(P, xr[:], Bxr, st, Bst, 0, gb, Bw, x_out[r0:r0 + 128, :], Bxout)
            yield

    def interleave(gens):
        gens = [g_ for g_ in gens if g_ is not None]
        while gens:
            for g_ in list(gens):
                try:
                    next(g_)
                except StopIteration:
                    gens.remove(g_)

    def step(gen):
        try:
            next(gen)
            return True
        except StopIteration:
            return False

    load_x(0)
    m3 = None
    for t in range(NT + 3):
        gens = []
        if 0 <= t - 1 < NT:
            gens.append(M2(t - 1))
        if t < NT:
            gens.append(M1(t))
        if t >= 3 and (t - 3) % 2 == 0 and (t - 3) // 2 < NT // 2:
            while m3 is not None and step(m3):
                pass
            m3 = M3((t - 3) // 2)
        while gens:
            if not DBG.get("m3last"):
                if m3 is not None and not step(m3):
                    m3 = None
            for g_ in (list(gens) if not DBG.get("rev12") else list(gens)[::-1]):
                if g_ in gens and not step(g_):
                    gens.remove(g_)
            if DBG.get("m3last"):
                if m3 is not None and not step(m3):
                    m3 = None
    while m3 is not None and step(m3):
        pass
    P.barrier()
    A.off = save


def pool_phase(P, C, x_in, Bxin, x_out, Bxout, prm, after_setup=None):
    nc, A = C.nc, C.A
    save = A.off
    ps, Bps = C.ps, C.Bps
    NT = S // 128
    WIN = (2, 4, 8, 16)
    Wc = A.alloc("Wc", [128, 4, 128], BF16)
    Wc0 = A.alloc("Wc0", [128, 4, 128], BF16)
    Wp = A.alloc("Wp", [128, 4, 128], BF16)
    pw = A.alloc("pw", [128, 8, 256], BF16)
    gb = A.alloc("gb", [128, 2, D], F32)
    lsb = A.alloc("lsb", [128, D], F32)
    stg = A.alloc("stg", [128, 8, 256], F32)
    idf = A.alloc("idf", [128, 128], F32)
    tmp = A.alloc("tmp", [128, 128], F32)
    rcn = A.alloc("rcn", [128, 128], F32)
    xs = [A.alloc("xs", [128, D], F32) for _ in range(4)]
    xb = [A.alloc("xb", [128, D], BF16) for _ in range(2)]
    pT = [A.alloc("pT", [128, 8, 128], BF16) for _ in range(2)]
    st = [A.alloc("st", [128, 1, 16], F32) for _ in range(2)]
    Bw, Bt = Buf("w"), Buf("t")
    Bxs = [Buf("xs0"), Buf("xs1"), Buf("xs2"), Buf("xs3")]
    Bxb = [Buf("xb0"), Buf("xb1")]
    BpT, Bst = [Buf("pT0"), Buf("pT1")], [Buf("st0"), Buf("st1")]

    def asel(t, pattern, cm, base, op):
        return lambda e: e.affine_select(out=t, in_=t, compare_op=op, fill=0.0, base=base, pattern=pattern, channel_multiplier=cm)

    Bs1, Bs2 = Buf("stg"), Buf("lsb")
    P.dma("sp", stg[:], prm["pool_w"].rearrange("g (cc p) d -> p (g cc) d", p=128), writes=[Bs1])
    P.dma("sp", lsb[:], prm["lscale"].partition_broadcast(128), writes=[Bs2])
    Bgb = Buf("gb")
    P.dma("sp", gb[:, 0, :], prm["ln_g"].partition_broadcast(128), writes=[Bgb])
    P.dma("sp", gb[:, 1, :], prm["ln_b"].partition_broadcast(128), writes=[Bgb])
    for j in range(8):
        g = j // 2
        P.op("dve", f_tt(pw[:, j, :], stg[:, j, :], lsb[:, g * 256:(g + 1) * 256], ALU.mult), reads=[Bs1, Bs2], writes=[Bw])
    P.op("pool", f_memset(idf[:], 1.0), writes=[Bt])
    P.op("pool", asel(idf[:], [[-1, 128]], 1, 0, ALU.is_equal), reads=[Bt], writes=[Bt])
    for g, w in enumerate(WIN):
        P.op("pool", f_memset(tmp[:], 1.0 / w), reads=[Bt], writes=[Bt])
        P.op("pool", asel(tmp[:], [[1, 128]], -1, 0, ALU.is_ge), reads=[Bt], writes=[Bt])
        P.op("pool", asel(tmp[:], [[-1, 128]], 1, w - 1, ALU.is_ge), reads=[Bt], writes=[Bt])
        P.op("pool", f_tt(Wc[:, g, :], tmp[:], idf[:], ALU.subtract), reads=[Bt], writes=[Bw])
        P.op("pool", f_memset(tmp[:], 1.0 / w), reads=[Bt, Bw], writes=[Bt])
        P.op("pool", asel(tmp[:], [[-1, 128]], 1, w - 1 - 128, ALU.is_ge), reads=[Bt], writes=[Bt])
        P.op("pool", f_copy(Wp[:, g, :], tmp[:]), reads=[Bt], writes=[Bw])
        P.op("pool", f_memset(rcn[:], 1.0 / w), reads=[Bt, Bw], writes=[Bt])
        for kk in range(w - 1, 0, -1):
            P.op("pool", (lambda kk=kk: (lambda e: e.affine_select(out=rcn[:], in_=rcn[:], compare_op=ALU.is_ge, fill=1.0 / kk, base=-kk,
                                                                pattern=[[1, 128]], channel_multiplier=0)))(),
                 reads=[Bt], writes=[Bt])
        P.op("pool", asel(rcn[:], [[1, 128]], -1, 0, ALU.is_ge), reads=[Bt], writes=[Bt])
        P.op("pool", asel(rcn[:], [[-1, 128]], 1, w - 1, ALU.is_ge), reads=[Bt], writes=[Bt])
        P.op("pool", f_tt(Wc0[:, g, :], rcn[:], idf[:], ALU.subtract), reads=[Bt], writes=[Bw])

    def load_x(i):
        P.dma("sp", xs[i % 4][:], x_in[i * 128:(i + 1) * 128, :], reads=[Bxin], writes=[Bxs[i % 4]])

    if after_setup is not None:
        after_setup()

    def TA(i):
        b = i % 2
        x4 = i % 4
        if i + 1 < NT:
            load_x(i + 1)
        P.op("act", f_act(xb[b][:], xs[x4][:], AF.Copy), reads=[Bxs[x4]], writes=[Bxb[b]])
        yield
        kp = [2 + (i % 2) * 2, 3 + (i % 2) * 2]
        for m in range(8):
            g = m // 2
            o_ = ps[kp[m // 4]][:, (m % 4) * 128:(m % 4 + 1) * 128]
            Wcur = Wc0 if i == 0 else Wc
            P.op("pe", f_mm(o_, xb[b][:, m * 128:(m + 1) * 128], Wcur[:, g, :], True, i == 0), reads=[Bxb[b], Bw], writes=[Bps[kp[m // 4]]],
                 signal=(i == 0 and m % 4 == 3))
            if i > 0:
                P.op("pe", f_mm(o_, xb[1 - b][:, m * 128:(m + 1) * 128], Wp[:, g, :], False, True), reads=[Bxb[1 - b], Bw], writes=[Bps[kp[m // 4]]],
                     signal=(m % 4 == 3))
            if m == 3:
                yield
        yield
        for j in range(2):
            P.op("act" if j == 0 else "dve",
                 (f_act(pT[b][:, 4 * j:4 * j + 4, :].rearrange("p c t -> p (c t)"), ps[kp[j]][:, :], AF.Copy) if j == 0 else
                  f_copy(pT[b][:, 4 * j:4 * j + 4, :].rearrange("p c t -> p (c t)"), ps[kp[j]][:, :])),
                 reads=[Bps[kp[j]]], writes=[BpT[b]])
        yield

    def TB(i):
        b = i % 2
        x4 = i % 4
        ko = [(i % 2), 6 + (i % 2)]
        for g in range(4):
            o_ = ps[ko[g // 2]][:, (g % 2) * 256:(g % 2 + 1) * 256]
            for cc in range(2):
                P.op("pe", f_mm(o_, pT[b][:, 2 * g + cc, :], pw[:, 2 * g + cc, :], cc == 0, cc == 1), reads=[BpT[b], Bw], writes=[Bps[ko[g // 2]]],
                     signal=(cc == 1 and g % 2 == 1))
            if g == 1:
                yield
        yield
        for n in range(2):
            P.op("dve", f_stt(xs[x4][:, n * 512:(n + 1) * 512], xs[x4][:, n * 512:(n + 1) * 512], ALPHA, ps[ko[n]][:, :], ALU.mult, ALU.add),
                 reads=[Bps[ko[n]]], writes=[Bxs[x4]])
            yield

    def TC(i):
        x4 = i % 4
        r0 = i * 128
        ln_stats(P, xs[x4][:], Bxs[x4], st[i % 2], Bst[i % 2], 0)
        yield
        ln_rstd(P, st[i % 2], Bst[i % 2], 1)
        yield
        ln_apply(P, xs[x4][:], Bxs[x4], st[i % 2], Bst[i % 2], 0, gb, Bgb, x_out[r0:r0 + 128, :], Bxout)
        yield

    def interleave(gens):
        gens = [g_ for g_ in gens if g_ is not None]
        while gens:
            for g_ in list(gens):
                try:
                    next(g_)
                except StopIteration:
                    gens.remove(g_)

    load_x(0)
    for t in range(NT + 2):
        interleave([TC(t - 2) if 0 <= t - 2 < NT else None, TB(t - 1) if 0 <= t - 1 < NT else None, TA(t) if t < NT else None])
    P.barrier()
    A.off = save


def build_full(phases=("A", "B", "C", "D"), debug=False):
    nc, C = new_ctx()
    ein = lambda n, s, d=F32: nc.dram_tensor(n, list(s), d, kind="ExternalInput").ap()
    x = ein("x", [S, D])
    prmA = dict(pos=ein("pos", [S], I32), w_in=ein("w_in", [D, 2504]), bgate=ein("bgate", [8]), mnorm=ein("mnorm", [512]),
                qnorm=ein("qnorm", [128, 2]), kvnorm=ein("kvnorm", [128, 1]), w_uq=ein("w_uq", [256, 768]),
                w_ukv=ein("w_ukv", [128, 1024]), w_out=ein("w_out", [D, D]), rope_freq=ein("rope_freq", [128, 1]),
                ln_g=ein("ln_mix_g0", [D]), ln_b=ein("ln_mix_b0", [D]))
    ffn = [dict(w_up=ein("w_up%d" % l, [D, 2 * FF]), cwb=ein("cwb%d" % l, [128, 44, 4]), w_dn=ein("w_dn%d" % l, [FF, D]),
                g=ein("ln_ffn_g%d" % l, [D]), b=ein("ln_ffn_b%d" % l, [D])) for l in range(2)]
    prmC = dict(pool_w=ein("pool_w", [4, 256, 256]), lscale=ein("lscale", [D]), ln_g=ein("ln_mix_g1", [D]), ln_b=ein("ln_mix_b1", [D]))
    out = nc.dram_tensor("out", [S, D], F32, kind="ExternalOutput").ap()
    kind = "ExternalOutput" if debug else "Internal"
    x1 = nc.dram_tensor("x1", [S, D], F32, kind=kind).ap()
    x2 = nc.dram_tensor("x2", [S, D], F32, kind=kind).ap()
    x3 = nc.dram_tensor("x3", [S, D], F32, kind=kind).ap()
    tab = nc.dram_tensor("ropetab", [2, 128, S], F32, kind="Internal").ap()
    P = Prog(nc)
    make_consts(P, C)
    Bx, B1, B2, B3, Bo, Btab = Buf("x"), Buf("x1"), Buf("x2"), Buf("x3"), Buf("out"), Buf("tab")
    precast = ("A" in phases) and ("B" in phases)
    if "A" in phases:
        bg = []
        if precast:
            wup_bf = nc.dram_tensor("wup0_bf", [D, 2 * FF], BF16, kind="Internal").ap()
            wdn_bf = nc.dram_tensor("wdn0_bf", [FF, D], BF16, kind="Internal").ap()
            for c in range(8):
                bg.append((lambda c=c: P.dma("pool", wup_bf[c * 128:(c + 1) * 128, :], ffn[0]["w_up"][c * 128:(c + 1) * 128, :], writes=[Buf()])))
            for c in range(4):
                bg.append((lambda c=c: P.dma("pool", wdn_bf[c * 704:(c + 1) * 704, :], ffn[0]["w_dn"][c * 704:(c + 1) * 704, :], writes=[Buf()])))
        mixer0_phase(P, C, x, Bx, x1, B1, prmA, tab, Btab, bgq=bg)
        assert not bg
    if "B" in phases:
        f = ffn[0]
        if precast:
            ffn_phase(P, C, x1, B1, x2, B2, wup_bf, f["cwb"], wdn_bf, f["g"], f["b"], wqueue="sp")
        else:
            ffn_phase(P, C, x1, B1, x2, B2, f["w_up"], f["cwb"], f["w_dn"], f["g"], f["b"])
    W1 = None
    if "C" in phases:
        if "D" in phases:
            W1 = ffn_alloc_weights(C)
            W1.end_off = C.A.off
        pool_phase(P, C, x2, B2, x3, B3, prmC,
                   after_setup=(lambda: ffn_load_weights(P, W1, ffn[1]["w_up"], ffn[1]["w_dn"])) if W1 is not None else None)
    if "D" in phases:
        f = ffn[1]
        ffn_phase(P, C, x3, B3, out, Bo, f["w_up"], f["cwb"], f["w_dn"], f["g"], f["b"], W=W1)
    P.finish()
    P.emit()
    return nc, P


def host_inputs(inp, bi):
    f32 = np.float32
    m = {}
    m["x"] = np.ascontiguousarray(inp["x"][bi])
    m["pos"] = np.ascontiguousarray(inp["positions"][bi]).astype(np.int32)
    m["w_in"] = np.ascontiguousarray(inp["even_w_in"][0])
    m["bgate"] = np.concatenate([inp["even_b_igate"][0], inp["even_b_fgate"][0]]).astype(f32)
    m["mnorm"] = np.ascontiguousarray(inp["even_mlstm_norm"][0])
    m["qnorm"] = np.ascontiguousarray(inp["even_q_norm"][0].reshape(2, 128).T)
    m["kvnorm"] = np.ascontiguousarray(inp["even_kv_norm"][0].reshape(128, 1))
    m["w_uq"] = np.ascontiguousarray(inp["even_w_uq"][0])
    m["w_ukv"] = np.ascontiguousarray(inp["even_w_ukv"][0])
    m["w_out"] = np.ascontiguousarray(inp["even_w_out"][0])
    m["rope_freq"] = (f32(10000.0) ** (-(np.arange(128) % 32).astype(f32) * f32(2) / f32(64))).astype(f32).reshape(128, 1)
    m["ln_mix_g0"] = np.ascontiguousarray(inp["ln_mix_g"][0]); m["ln_mix_b0"] = np.ascontiguousarray(inp["ln_mix_b"][0])
    m["ln_mix_g1"] = np.ascontiguousarray(inp["ln_mix_g"][1]); m["ln_mix_b1"] = np.ascontiguousarray(inp["ln_mix_b"][1])
    for l in range(2):
        m["w_up%d" % l] = np.ascontiguousarray(inp["ffn_w_up"][l])
        m["cwb%d" % l] = host_cwb(inp["ffn_conv_w"][l], inp["ffn_conv_b"][l])
        m["w_dn%d" % l] = np.ascontiguousarray(inp["ffn_w_down"][l])
        m["ln_ffn_g%d" % l] = np.ascontiguousarray(inp["ln_ffn_g"][l]); m["ln_ffn_b%d" % l] = np.ascontiguousarray(inp["ln_ffn_b"][l])
    m["pool_w"] = np.ascontiguousarray(inp["odd_pool_w"][0])
    m["lscale"] = np.ascontiguousarray(inp["odd_layer_scale"][0])
    return m


_CACHE = {}


def kernel(**inputs):
    inp = {k: np.asarray(v) for k, v in inputs.items()}
    if "nc" not in _CACHE:
        _CACHE["nc"] = build_full()[0]
    nc = _CACHE["nc"]
    in_maps = [host_inputs(inp, bi) for bi in range(8)]
    res = run_bass_kernel_spmd(nc, in_maps, core_ids=list(range(8)))
    return np.stack([np.asarray(r["out"]) for r in res.results], axis=0).astype(np.float32)
```

```python
import numpy as np
from contextlib import ExitStack
import concourse.bass as bass
import concourse.mybir as mybir
from concourse.bass_utils import run_bass_kernel_spmd

F32 = mybir.dt.float32
BF16 = mybir.dt.bfloat16
I32 = mybir.dt.int32
ALU = mybir.AluOpType
AF = mybir.ActivationFunctionType
AX = mybir.AxisListType

D = 1024
S = 4096
FF = 2816
ALPHA = float((2 * 2) ** 0.25)
LN_EPS = 1e-5
RMS_EPS = 1e-6
ARENA_LO, ARENA_HI = 16512, 229344


class Buf:
    __slots__ = ("name", "w", "r", "excl")

    def __init__(self, name="", excl=False):
        self.name = name
        self.w = None
        self.r = {}
        self.excl = excl


class Prog:
    ENG = ("pe", "act", "dve", "pool", "sp")

    def __init__(self, nc, n_dma_sems=24):
        self.nc = nc
        self.q = {e: [] for e in self.ENG}
        self.cnt = {e: 0 for e in self.ENG}
        self.waited = {e: {} for e in self.ENG}
        self.n_dma_sems = n_dma_sems
        self.dma_val = [0] * n_dma_sems
        self.dma_next = 0
        self.dma_next_sw = 0
        self.sems = {}
        self.ninstr = 0

    def _wait(self, eng, ev):
        if ev is None:
            return
        key, val = ev
        if val <= 0:
            return
        if key == eng and eng == "pe":
            return
        if self.waited[eng].get(key, 0) >= val:
            return
        self.waited[eng][key] = val
        self.q[eng].append(("w", key, val))

    def _deps(self, eng, reads, writes):
        for b in reads:
            self._wait(eng, b.w)
            if b.excl:
                for k, v in b.r.items():
                    if k != eng:
                        self._wait(eng, (k, v))
        for b in writes:
            self._wait(eng, b.w)
            for k, v in b.r.items():
                self._wait(eng, (k, v))

    def _record(self, ev, reads, writes):
        k, v = ev
        for b in reads:
            if b.r.get(k, 0) < v:
                b.r[k] = v
        for b in writes:
            b.w = ev
            b.r = {}

    def op(self, eng, fn, reads=(), writes=(), signal=True):
        self._deps(eng, reads, writes)
        self.ninstr += 1
        if signal:
            self.cnt[eng] += 1
            ev = (eng, self.cnt[eng])
            self.q[eng].append(("i", fn, eng))
        else:
            ev = (eng, self.cnt[eng] + 1)
            self.q[eng].append(("n", fn, None))
        self._record(ev, reads, writes)
        return ev

    def dma(self, queue, out_ap, in_ap, reads=(), writes=(), **kw):
        nsw = self.n_dma_sems // 3
        if queue == "pool":
            k = self.dma_next_sw
            self.dma_next_sw = (k + 1) % nsw
        else:
            k = nsw + self.dma_next
            self.dma_next = (self.dma_next + 1) % (self.n_dma_sems - nsw)
        key = ("dma", k)
        self._wait(queue, (key, self.dma_val[k]))
        self._deps(queue, reads, writes)
        self.dma_val[k] += 16
        ev = (key, self.dma_val[k])
        self.ninstr += 1
        self.q[queue].append(("d", out_ap, in_ap, key, kw))
        self._record(ev, reads, writes)
        return ev

    def barrier(self):
        for eng in self.ENG:
            for k in range(self.n_dma_sems):
                self._wait(eng, (("dma", k), self.dma_val[k]))
            for e in ("pe", "act", "dve", "pool"):
                if e != eng:
                    self._wait(eng, (e, self.cnt[e]))

    def finish(self, eng="sp"):
        for k in range(self.n_dma_sems):
            self._wait(eng, (("dma", k), self.dma_val[k]))
        for e in ("pe", "act", "dve", "pool"):
            self._wait(eng, (e, self.cnt[e]))

    def emit(self):
        nc = self.nc
        with ExitStack() as st:
            for e in ("pe", "act", "dve", "pool"):
                self.sems[e] = st.enter_context(nc.semaphore("c_" + e))
            for k in range(self.n_dma_sems):
                self.sems[("dma", k)] = st.enter_context(nc.semaphore("d_%d" % k))
            block = st.enter_context(nc.Block())
            sems = self.sems

            def run(eng_obj, items):
                for it in items:
                    t = it[0]
                    if t == "w":
                        eng_obj.wait_ge(sems[it[1]], it[2])
                    elif t == "i":
                        it[1](eng_obj).then_inc(sems[it[2]], 1)
                    elif t == "n":
                        it[1](eng_obj)
                    else:
                        eng_obj.dma_start(out=it[1], in_=it[2], **it[4]).then_inc(sems[it[3]], 16)

            q = self.q

            @block.tensor
            def _(e):
                run(e, q["pe"])

            @block.scalar
            def _(e):
                run(e, q["act"])

            @block.vector
            def _(e):
                run(e, q["dve"])

            @block.gpsimd
            def _(e):
                run(e, q["pool"])

            @block.sync
            def _(e):
                run(e, q["sp"])


def _dtsize(dt):
    return {F32: 4, BF16: 2, I32: 4}[dt]


class Arena:
    def __init__(self, nc):
        self.nc = nc
        self.off = ARENA_LO
        self.n = 0

    def alloc(self, name, shape, dtype):
        nb = _dtsize(dtype)
        for s in shape[1:]:
            nb *= s
        off = (self.off + 31) // 32 * 32
        assert off + nb <= ARENA_HI, ("SBUF overflow", name, off, nb)
        self.off = off + nb
        self.n += 1
        return self.nc.alloc_sbuf_tensor_at("%s_%d" % (name, self.n), list(shape), dtype, offset=off)


def f_mm(out, lhsT, rhs, start, stop):
    return lambda e: e.matmul(out, lhsT, rhs, start=start, stop=stop)


def f_tr(out, in_, ident):
    return lambda e: e.transpose(out, in_, ident)


def f_act(out, in_, func, bias=None, scale=None, accum_out=None):
    kw = {}
    if bias is not None:
        kw["bias"] = bias
    if scale is not None:
        kw["scale"] = scale
    if accum_out is not None:
        kw["accum_out"] = accum_out
    return lambda e: e.activation(out, in_, func, **kw)


def f_copy(out, in_):
    return lambda e: e.tensor_copy(out, in_)


def f_ts(out, in0, s1, s2, op0, op1=None):
    if op1 is None:
        return lambda e: e.tensor_scalar(out, in0, s1, None, op0=op0)
    return lambda e: e.tensor_scalar(out, in0, s1, s2, op0=op0, op1=op1)


def f_tt(out, in0, in1, op):
    return lambda e: e.tensor_tensor(out, in0, in1, op=op)


def f_stt(out, in0, scalar, in1, op0, op1):
    return lambda e: e.scalar_tensor_tensor(out, in0, scalar, in1, op0=op0, op1=op1)


def f_memset(ap, v):
    return lambda e: e.memset(ap, v)


class Ctx:
    pass


def make_consts(P, C):
    A = C.A
    C.ident = A.alloc("ident", [128, 128], BF16)
    C.Bident = Buf("ident")
    C.eps = A.alloc("eps", [128, 2], F32)
    C_EPS[0] = C.eps
    P.op("pool", f_memset(C.eps[:, 0:1], LN_EPS), writes=[C.Bident])
    P.op("pool", f_memset(C.eps[:, 1:2], RMS_EPS), writes=[C.Bident])
    P.op("pool", f_memset(C.ident[:], 1.0), writes=[C.Bident])
    P.op("pool", lambda e: e.affine_select(out=C.ident[:], in_=C.ident[:], compare_op=ALU.is_equal, fill=0.0,
                                           base=0, pattern=[[-1, 128]], channel_multiplier=1),
         reads=[C.Bident], writes=[C.Bident])


def ln_stats(P, z, Bz, st, Bst, col):
    P.op("dve", lambda e: e.bn_stats(st[:, col, 0:6], z[:, 0:512]), reads=[Bz], writes=[Bst])
    P.op("dve", lambda e: e.bn_stats(st[:, col, 6:12], z[:, 512:1024]), reads=[Bz], writes=[Bst])
    P.op("dve", lambda e: e.bn_aggr(st[:, col, 12:14], st[:, col, 0:12]), reads=[Bst], writes=[Bst])


def ln_rstd(P, st, Bst, n):
    P.op("act", f_act(st[:, 0:n, 15:16], st[:, 0:n, 13:14], AF.Ln, bias=C_EPS[0][:, 0:1]), reads=[Bst], writes=[Bst])
    P.op("act", f_act(st[:, 0:n, 14:15], st[:, 0:n, 15:16], AF.Exp, scale=-0.5), reads=[Bst], writes=[Bst])


def ln_apply(P, z, Bz, st, Bst, col, gb, Bgb, out_dram_rows, Bout):
    P.op("dve", f_ts(z, z, st[:, col, 12:13], st[:, col, 14:15], ALU.subtract, ALU.mult), reads=[Bz, Bst], writes=[Bz])
    P.op("dve", f_tt(z, z, gb[:, 0, :], ALU.mult), reads=[Bz, Bgb], writes=[Bz])
    P.op("dve", f_tt(z, z, gb[:, 1, :], ALU.add), reads=[Bz, Bgb], writes=[Bz])
    P.dma("sp", out_dram_rows, z, reads=[Bz], writes=[Bout])


C_EPS = [None]


def ffn_alloc_weights(C):
    A = C.A
    W = Ctx()
    W.wup = A.alloc("wup", [128, 8, 2 * FF], BF16)
    W.wdn = A.alloc("wdn", [128, 22, D], BF16)
    W.Bwup = [Buf("wup%d" % j) for j in range(22)]
    W.Bwdn = Buf("wdn")
    return W


def ffn_load_weights(P, W, w_up, w_down, queue="pool"):
    w_up_v = w_up.rearrange("(c p) (h j n) -> p c h j n", p=128, h=2, j=22)
    wv = W.wup[:, :, :].rearrange("p c (h j n) -> p c h j n", h=2, j=22)
    for j in range(22):
        for h in range(2):
            P.dma(queue, wv[:, :, h, j, :], w_up_v[:, :, h, j, :], writes=[W.Bwup[j]])
        if j == 10 or j == 21:
            j0 = 0 if j == 10 else 11
            w_dn_v = w_down.rearrange("(j p) n -> p j n", p=128)
            P.dma(queue, W.wdn[:, j0:j0 + 11, :], w_dn_v[:, j0:j0 + 11, :], writes=[W.Bwdn])


def ffn_phase(P, C, x_in, Bxin, x_out, Bxout, w_up, cwb, w_down, ln_g, ln_b, W=None, wqueue="pool"):
    nc, A = C.nc, C.A
    save = A.off
    TT, NT = 256, S // 256
    NSET = DBG.get("nset", 5)
    preloaded = W is not None
    if W is None:
        W = ffn_alloc_weights(C)
    else:
        A.off = W.end_off
    wup, wdn, Bwup, Bwdn = W.wup, W.wdn, W.Bwup, W.Bwdn
    cw = A.alloc("cw", [128, 44, 4], F32)
    gb = A.alloc("gb", [128, 2, D], F32)
    hT = [A.alloc("hT", [128, 22, TT], BF16) for _ in range(2)]
    xT = [A.alloc("xT", [128, 8, TT], BF16) for _ in range(2)]
    xr = [A.alloc("xr", [128, D], F32) for _ in range(2)]
    xb = [A.alloc("xb", [128, 2, D], BF16) for _ in range(2)]
    u = [A.alloc("u", [128, 2, TT], F32) for _ in range(NSET)]
    upx = [A.alloc("upx", [128, 2, TT + 2], F32) for _ in range(NSET)]
    halo = [A.alloc("halo", [128, 22, 2, 2], F32)] * 2
    st = A.alloc("st", [128, 2, 16], F32)
    Bcw, Bgb = Buf("cw"), Buf("gb")
    BhT = [Buf("hT0"), Buf("hT1")]
    BxT = [Buf("xT0"), Buf("xT1")]
    Bxr = [Buf("xr0"), Buf("xr1")]
    Bxb = [[Buf("xb"), Buf("xb")] for _ in range(2)]
    Bu = [[Buf("u"), Buf("u")] for _ in range(NSET)]
    Bupx = [Buf("upx") for _ in range(NSET)]
    Bhalo = [Buf("halo0")] * 2
    Bst = Buf("st")
    ps = C.ps
    Bps = C.Bps
    up_banks = [0, 1, 2]
    dn_banks = [3, 4, 5, 6]
    tp_bank = 7
    tp = ps[tp_bank][:, :].bitcast(BF16)

    P.dma("sp", cw[:], cwb, writes=[Bcw])
    P.dma("sp", gb[:, 0, :], ln_g.partition_broadcast(128), writes=[Bgb])
    P.dma("sp", gb[:, 1, :], ln_b.partition_broadcast(128), writes=[Bgb])
    P.op("pool", f_memset(halo[0][:], 0.0), writes=[Bhalo[0]])

    def load_xb(tt):
        b = tt % 2
        for s in range(2):
            r0 = tt * TT + s * 128
            P.dma("pool", xb[b][:, s, :], x_in[r0:r0 + 128, :], reads=[Bxin], writes=[Bxb[b][s]])

    def prep_xT(tt):
        b = tt % 2
        for s in range(2):
            for c in range(8):
                P.op("pe", f_tr(tp[:, c * 128:(c + 1) * 128], xb[b][:, s, c * 128:(c + 1) * 128], C.ident[:]),
                     reads=[Bxb[b][s], C.Bident], writes=[Bps[tp_bank]], signal=(c == 7))
            P.op("dve", f_copy(xT[b][:, :, s * 128:(s + 1) * 128], tp.rearrange("p (c t) -> p c t", c=8)),
                 reads=[Bps[tp_bank]], writes=[BxT[b]])

    def pair_ctx(n):
        tt, j = divmod(n, 22)
        k = up_banks[n % 3]
        ub = n % NSET
        pk = ps[k][:, :].rearrange("p (h t) -> p h t", h=2)
        return tt, j, tt % 2, k, ub, pk

    def S0(n):
        tt, j, b, k, ub, pk = pair_ctx(n)
        for half, jj in ((0, j), (1, j + 22)):
            for c in range(8):
                P.op("pe", f_mm(pk[:, half, :], wup[:, c, jj * 128:(jj + 1) * 128], xT[b][:, c, :], c == 0, c == 7),
                     reads=[Bwup[j], BxT[b]], writes=[Bps[k]], signal=(c == 7 and half == 1))

    def S1(n):
        tt, j, b, k, ub, pk = pair_ctx(n)
        ho, hn = halo[tt % 2], halo[(tt + 1) % 2]
        Bho, Bhn = Bhalo[tt % 2], Bhalo[(tt + 1) % 2]
        ux = upx[ub]
        P.op("act", f_act(ux[:, :, 2:TT + 2], pk, AF.Copy), reads=[Bps[k]], writes=[Bupx[ub]])
        P.op("act", f_act(ux[:, :, 0:2], ho[:, j, :, :], AF.Copy), reads=[Bho], writes=[Bupx[ub]])
        P.op("act", f_act(hn[:, j, :, :], ux[:, :, TT:TT + 2], AF.Copy), reads=[Bupx[ub]], writes=[Bhn])

    def S2(n):
        tt, j, b, k, ub, pk = pair_ctx(n)
        uu, ux = u[ub], upx[ub]
        for half, jj in ((0, j), (1, j + 22)):
            P.op("act", f_act(uu[:, half, :], ux[:, half, 2:TT + 2], AF.Identity, bias=cw[:, jj, 3:4], scale=cw[:, jj, 2:3]),
                 reads=[Bupx[ub], Bcw], writes=[Bu[ub][half]])

    def S3(n):
        tt, j, b, k, ub, pk = pair_ctx(n)
        uu, ux = u[ub], upx[ub]
        for half, jj in ((0, j), (1, j + 22)):
            P.op("dve", f_stt(uu[:, half, :], ux[:, half, 1:TT + 1], cw[:, jj, 1:2], uu[:, half, :], ALU.mult, ALU.add),
                 reads=[Bupx[ub], Bcw, Bu[ub][half]], writes=[Bu[ub][half]])
        for half, jj in ((0, j), (1, j + 22)):
            P.op("dve", f_stt(uu[:, half, :], ux[:, half, 0:TT], cw[:, jj, 0:1], uu[:, half, :], ALU.mult, ALU.add),
                 reads=[Bupx[ub], Bcw, Bu[ub][half]], writes=[Bu[ub][half]])

    def S4(n):
        tt, j, b, k, ub, pk = pair_ctx(n)
        uu = u[ub]
        P.op("act", f_act(uu[:, 0, :], uu[:, 0, :], AF.Silu), reads=[Bu[ub][0]], writes=[Bu[ub][0]])

    def S5(n):
        tt, j, b, k, ub, pk = pair_ctx(n)
        uu = u[ub]
        P.op("dve", f_tt(hT[b][:, j, :], uu[:, 0, :], uu[:, 1, :], ALU.mult),
             reads=[Bu[ub][0], Bu[ub][1]], writes=[BhT[b]])

    def down_pieces(tt):
        b = tt % 2
        pieces = []

        def p_load():
            for s in range(2):
                r0 = tt * TT + s * 128
                P.dma("sp", xr[s][:], x_in[r0:r0 + 128, :], reads=[Bxin], writes=[Bxr[s]])
        pieces.append(p_load)

        def mk_mm(s, n):
            def f():
                k = dn_banks[s * 2 + n]
                for j in range(22):
                    P.op("pe", f_mm(ps[k][:, :], hT[b][:, j, s * 128:(s + 1) * 128], wdn[:, j, n * 512:(n + 1) * 512], j == 0, j == 21),
                         reads=[BhT[b], Bwdn], writes=[Bps[k]], signal=(j == 21))
            return f

        def mk_res(s, n):
            def f():
                k = dn_banks[s * 2 + n]
                zz = xr[s][:]
                P.op("dve", f_stt(zz[:, n * 512:(n + 1) * 512], zz[:, n * 512:(n + 1) * 512], ALPHA, ps[k][:, :], ALU.mult, ALU.add),
                     reads=[Bps[k]], writes=[Bxr[s]])
            return f

        for s in range(2):
            for n in range(2):
                pieces.append(mk_mm(s, n))
                if s * 2 + n >= 1:
                    s2, n2 = divmod(s * 2 + n - 1, 2)
                    pieces.append(mk_res(s2, n2))
        pieces.append(mk_res(1, 1))
        pieces.append(lambda: ln_stats(P, xr[0][:], Bxr[0], st, Bst, 0))
        pieces.append(lambda: ln_stats(P, xr[1][:], Bxr[1], st, Bst, 1))
        pieces.append(lambda: ln_rstd(P, st, Bst, 2))
        for s in range(2):
            r0 = tt * TT + s * 128
            pieces.append((lambda s=s: P.op("dve", f_ts(xr[s][:], xr[s][:], st[:, s, 12:13], st[:, s, 14:15], ALU.subtract, ALU.mult),
                                            reads=[Bxr[s], Bst], writes=[Bxr[s]])))
            pieces.append((lambda s=s: P.op("dve", f_tt(xr[s][:], xr[s][:], gb[:, 0, :], ALU.mult), reads=[Bxr[s], Bgb], writes=[Bxr[s]])))
            pieces.append((lambda s=s, r0=r0: (P.op("dve", f_tt(xr[s][:], xr[s][:], gb[:, 1, :], ALU.add), reads=[Bxr[s], Bgb], writes=[Bxr[s]]),
                                               P.dma("sp", x_out[r0:r0 + 128, :], xr[s][:], reads=[Bxr[s]], writes=[Bxout]))))
        return pieces

    load_xb(0)
    load_xb(1)
    if not preloaded:
        ffn_load_weights(P, W, w_up, w_down, queue=wqueue)
    prep_xT(0)
    NP = NT * 22
    SKEW = ((S0, 0), (S1, 1), (S2, 1), (S3, 2), (S4, 3), (S5, 3 if NSET == 4 else 4))
    pending = []
    for n in range(NP + 26):
        for fn, lag in SKEW:
            m = n - lag
            if 0 <= m < NP:
                fn(m)
        tt, j = divmod(n, 22)
        if j == 4 and 1 <= tt <= NT:
            pending = down_pieces(tt - 1)
        if pending:
            pending.pop(0)()
        if n < NP:
            if j == 12 and tt + 1 < NT:
                prep_xT(tt + 1)
            if j == 16 and tt + 2 < NT:
                load_xb(tt + 2)
    assert not pending
    P.barrier()
    A.off = save


def new_ctx():
    nc = bass.Bass("TRN2", target_bir_lowering=False)
    C = Ctx()
    C.nc = nc
    C.A = Arena(nc)
    C.ps = [nc.alloc_psum_tensor("psb%d" % i, [128, 512], F32) for i in range(8)]
    C.Bps = [Buf("ps%d" % i, excl=True) for i in range(8)]
    return nc, C


def host_cwb(conv_w, conv_b):
    a = np.concatenate([conv_w, conv_b[None, :]], axis=0)
    return np.ascontiguousarray(a.reshape(4, 44, 128).transpose(2, 1, 0))


def build_ffn_only():
    nc, C = new_ctx()
    x_in = nc.dram_tensor("x_in", [S, D], F32, kind="ExternalInput").ap()
    w_up = nc.dram_tensor("w_up", [D, 2 * FF], F32, kind="ExternalInput").ap()
    cwb = nc.dram_tensor("cwb", [128, 44, 4], F32, kind="ExternalInput").ap()
    w_dn = nc.dram_tensor("w_dn", [FF, D], F32, kind="ExternalInput").ap()
    g = nc.dram_tensor("ln_g", [D], F32, kind="ExternalInput").ap()
    b = nc.dram_tensor("ln_b", [D], F32, kind="ExternalInput").ap()
    x_out = nc.dram_tensor("x_out", [S, D], F32, kind="ExternalOutput").ap()
    P = Prog(nc)
    make_consts(P, C)
    ffn_phase(P, C, x_in, Buf("xin"), x_out, Buf("xout"), w_up, cwb, w_dn, g, b)
    P.finish()
    P.emit()
    return nc, P


DBG = {}
NWIN = 2696


def mixer0_phase(P, C, x_in, Bxin, x_out, Bxout, prm, tab, Btab, bgq=None):
    nc, A = C.nc, C.A
    save = A.off
    ps, Bps = C.ps, C.Bps
    NT = S // 128
    ATT_SCALE = float(192 ** -0.5)
    LNK = float(np.log(128 ** -0.5))
    PI = float(np.pi)
    win = A.alloc("win", [128, 8, NWIN], BF16)
    wuq = A.alloc("wuq", [128, 2, 1024], BF16)
    wukv = A.alloc("wukv", [128, 1024], BF16)
    wout = A.alloc("wout", [128, 8, D], BF16)
    off_kn = A.off
    knT = A.alloc("knT", [128, 4, S], BF16)
    krT = A.alloc("krT", [128, S], BF16)
    vaug = A.alloc("vaug", [128, NT, 4, 130], BF16)
    gb = A.alloc("gb", [128, 2, D], F32)
    gmn = A.alloc("gmn", [128, 512], F32)
    bg = A.alloc("bg", [128, 8], F32)
    frq = A.alloc("frq", [128, 1], F32)
    cst = A.alloc("cst", [128, 4], F32)
    U32 = A.alloc("U32", [128, 4, 129], F32)
    Cbf = A.alloc("Cbf", [128, 4, 130], BF16)
    mask4 = A.alloc("mask4", [128, 512], BF16)
    tri = A.alloc("tri", [128, 128], F32)
    ones = A.alloc("ones", [128, 128], F32)
    Bw = Buf("w")
    Bcst, BU, BC = Buf("cst"), Buf("U32"), Buf("Cbf")
    xb = [A.alloc("xb", [128, D], BF16) for _ in range(2)]; Bxb = [Buf("xb0"), Buf("xb1")]
    xT = A.alloc("xT", [128, 8, 128], BF16); BxT = Buf("xT")
    qmT = [A.alloc("qmT", [128, 4, 128], BF16) for _ in range(2)]; BqmT = [Buf("qmT0"), Buf("qmT1")]
    kmT = [A.alloc("kmT", [128, 4, 128], BF16) for _ in range(2)]; BkmT = [Buf("kmT0"), Buf("kmT1")]
    ktok = [A.alloc("ktok", [128, 512], BF16) for _ in range(2)]; Bktok = [Buf("ktok0"), Buf("ktok1")]
    vs = [A.alloc("vs", [128, 512], F32) for _ in range(2)]; Bvs = [Buf("vs0"), Buf("vs1")]
    og = [A.alloc("og", [128, 512], F32) for _ in range(2)]; Bog = [Buf("og0"), Buf("og1")]
    gA = [A.alloc("gA", [128, 32], F32) for _ in range(2)]; BgA = [Buf("gA0"), Buf("gA1")]
    ebt = [A.alloc("ebt", [128, 4], F32) for _ in range(3)]; Bebt = [Buf("ebt0"), Buf("ebt1"), Buf("ebt2")]
    gC = A.alloc("gC", [128, 8], F32); BgC = Buf("gC")
    gB = A.alloc("gB", [128, 16], F32); BgB = Buf("gB")
    vpa = A.alloc("vpa", [128, 4, 130], BF16); Bvpa = Buf("vpa")
    APT = A.alloc("APT", [128, 512], BF16); BAPT = Buf("APT")
    Xs = A.alloc("Xs", [128, 4, 129], F32); BXs = Buf("Xs")
    junk1 = A.alloc("junk1", [128, 256], F32); Bjunk1 = Buf("junk1")
    junk2, Bjunk2 = junk1, Bjunk1
    cq32 = A.alloc("cq32", [128, 384], F32); Bcq = Buf("cq32")
    cqn = A.alloc("cqn", [128, 384], BF16); Bcqn = Buf("cqn")
    cqT = A.alloc("cqT", [128, 3, 128], BF16); BcqT = Buf("cqT")
    q2 = [A.alloc("q2", [128, 6, 256], BF16) for _ in range(3)]; Bq2 = [[Buf("q2"), Buf("q2")] for _ in range(3)]
    cs = [A.alloc("cs", [128, 2, 128], F32) for _ in range(2)]; Bcs = [Buf("cs0"), Buf("cs1")]
    rt1 = A.alloc("rt1", [128, 128], F32); rt2 = A.alloc("rt2", [128, 128], F32); Brt = Buf("rt")
    LA = DBG.get("la", 1)
    PT = [A.alloc("PT", [128, 512], BF16) for _ in range(LA + 1)]; BPT = [Buf("PT%d" % i) for i in range(LA + 1)]
    SCB = [2, 3] if LA == 1 else [2, 3, 4]
    ym = [A.alloc("ym", [128, 512], BF16) for _ in range(4)]; Bym = [Buf("ym%d" % i) for i in range(4)]
    ya = [A.alloc("ya", [128, 512], BF16) for _ in range(2)]; Bya = [Buf("ya0"), Buf("ya1")]
    yT = A.alloc("yT", [128, 8, 128], BF16); ByT = Buf("yT")
    st = A.alloc("st", [128, 1, 16], F32); Bst = Buf("st")
    rc = A.alloc("rc", [128, 4], F32); Brc = Buf("rc")
    xr = A.alloc("xr", [128, D], F32); Bxr = Buf("xr")
    work_end = A.off
    Bkn = [Buf("kn%d" % i) for i in range(NT)]
    Bkr = [Buf("kr%d" % i) for i in range(NT)]
    Bva = [Buf("va%d" % i) for i in range(NT)]

    bank_ctr = [0]

    def nb():
        k = 2 + bank_ctr[0] % 6
        bank_ctr[0] += 1
        return k

    P.op("pool", f_memset(cst[:, 0:1], 1.0), writes=[Bcst])
    P.op("pool", f_memset(cst[:, 1:2], LNK), writes=[Bcst])
    P.op("pool", f_memset(cst[:, 2:3], RMS_EPS), writes=[Bcst])
    P.op("pool", f_memset(ones[:], 1.0), writes=[Bcst])
    P.op("pool", f_memset(tri[:], 1.0), writes=[Bcst])
    P.op("pool", lambda e: e.affine_select(out=tri[:], in_=tri[:], compare_op=ALU.is_ge, fill=0.0, base=0,
                                           pattern=[[1, 128]], channel_multiplier=-1), reads=[Bcst], writes=[Bcst])
    for h in range(4):
        P.op("pool", f_copy(mask4[:, h * 128:(h + 1) * 128], tri[:]), reads=[Bcst], writes=[Bcst])
    P.op("pool", f_memset(U32[:], 0.0), writes=[BU])
    P.op("pool", f_memset(Cbf[:], 0.0), writes=[BC])
    P.op("pool", f_memset(vaug[:, :, :, 128:129], 1.0), writes=Bva)
    P.op("pool", f_memset(ebt[2][:], 1.0), writes=[Bebt[2]])
    Bfrq = Buf("frq")
    w_in_v = prm["w_in"].rearrange("(c p) n -> p c n", p=128)
    for c in range(8):
        P.dma("pool", win[:, c, 0:2440], w_in_v[:, c, 0:2440], writes=[Buf()])
    w_out_v = prm["w_out"].rearrange("(c p) n -> p c n", p=128)
    Bwout = Buf("wout")
    P.dma("sp", gb[:, 0, :], prm["ln_g"].partition_broadcast(128), writes=[Buf()])
    P.dma("sp", gb[:, 1, :], prm["ln_b"].partition_broadcast(128), writes=[Buf()])
    P.dma("sp", gmn[:], prm["mnorm"].partition_broadcast(128), writes=[Buf()])
    P.dma("sp", bg[:], prm["bgate"].partition_broadcast(128), writes=[Buf()])
    P.dma("sp", frq[:], prm["rope_freq"], writes=[Bfrq])
    A.off = off_kn
    krs = A.alloc("krs", [128, 8, 64], F32)
    wqs = A.alloc("wqs", [128, 2, 768], F32)
    wks = A.alloc("wks", [128, 1024], F32)
    gq = A.alloc("gq", [128, 2], F32)
    gkv = A.alloc("gkv", [128, 1], F32)
    Bstg = Buf("stg")
    Bw2 = Buf("w2")
    BstgL = [Buf("stg%d" % i) for i in range(5)]
    P.dma("sp", krs[:], w_in_v[:, :, 2440:2504], writes=[BstgL[0]])
    P.dma("sp", wqs[:], prm["w_uq"].rearrange("(c p) n -> p c n", p=128), writes=[BstgL[1]])
    P.dma("sp", wks[:], prm["w_ukv"], writes=[BstgL[2]])
    P.dma("sp", gq[:], prm["qnorm"], writes=[BstgL[3]])
    P.dma("sp", gkv[:], prm["kvnorm"], writes=[BstgL[4]])
    def staging_ops():
        for o in (2440, 2504):
            P.op("dve", f_copy(win[:, :, o:o + 64], krs[:]), reads=BstgL, writes=[Bw2])
        for o in (2568, 2632):
            P.op("dve", f_ts(win[:, :, o:o + 32], krs[:, :, 32:64], -1.0, None, ALU.mult), reads=BstgL, writes=[Bw2])
            P.op("dve", f_copy(win[:, :, o + 32:o + 64], krs[:, :, 0:32]), reads=BstgL, writes=[Bw2])
        for c in range(2):
            src = wqs[:, c, :].rearrange("p (h d) -> p h d", d=192)
            g1 = gq[:, c:c + 1]
            P.op("dve", f_ts(wuq[:, c, 0:512].rearrange("p (h d) -> p h d", d=128), src[:, :, 0:128], g1, None, ALU.mult),
                 reads=BstgL, writes=[Bw2])
            P.op("dve", f_ts(wuq[:, c, 512:768].rearrange("p (h d) -> p h d", d=64), src[:, :, 128:192], g1, None, ALU.mult),
                 reads=BstgL, writes=[Bw2])
            rot = wuq[:, c, 768:1024].rearrange("p (h d) -> p h d", d=64)
            P.op("dve", f_ts(rot[:, :, 0:32], src[:, :, 160:192], g1, -1.0, ALU.mult, ALU.mult), reads=BstgL, writes=[Bw2])
            P.op("dve", f_ts(rot[:, :, 32:64], src[:, :, 128:160], g1, None, ALU.mult), reads=BstgL, writes=[Bw2])
        wk4 = wks[:, :].rearrange("p (h d) -> p h d", d=256)
        P.op("dve", f_ts(wukv[:, 0:512].rearrange("p (h d) -> p h d", d=128), wk4[:, :, 0:128], gkv[:, 0:1], None, ALU.mult),
             reads=BstgL, writes=[Bw2])
        P.op("dve", f_ts(wukv[:, 512:1024].rearrange("p (h d) -> p h d", d=128), wk4[:, :, 128:256], gkv[:, 0:1], None, ALU.mult),
             reads=BstgL, writes=[Bw2])

    CH = 1024
    posi = A.alloc("posi", [128, CH], I32)
    ang = A.alloc("ang", [128, CH], F32)
    ki = A.alloc("ki", [128, CH], I32)
    kf = A.alloc("kf", [128, CH], F32)
    r2 = A.alloc("r2", [128, CH], F32)
    sn = A.alloc("sn", [128, CH], F32)
    Bt = Buf("ropetmp")
    C1 = 6.28125
    C2 = float(2 * np.pi - 6.28125)

    def wrap(r):
        P.op("dve", f_ts(kf[:], r[:], PI, None, ALU.is_gt), reads=[Bt], writes=[Bt])
        P.op("dve", f_stt(r[:], kf[:], -2 * PI, r[:], ALU.mult, ALU.add), reads=[Bt], writes=[Bt])
        P.op("dve", f_ts(kf[:], r[:], -PI, None, ALU.is_lt), reads=[Bt], writes=[Bt])
        P.op("dve", f_stt(r[:], kf[:], 2 * PI, r[:], ALU.mult, ALU.add), reads=[Bt], writes=[Bt])

    for ch in range(S // CH):
        P.dma("sp", posi[:], prm["pos"][ch * CH:(ch + 1) * CH].partition_broadcast(128), writes=[Bt])
        P.op("dve", f_copy(ang[:], posi[:]), reads=[Bt], writes=[Bt])
        P.op("dve", f_ts(ang[:], ang[:], frq[:, 0:1], None, ALU.mult), reads=[Bt, Bfrq], writes=[Bt])
        P.op("dve", f_ts(ki[:], ang[:], float(1 / (2 * np.pi)), None, ALU.mult), reads=[Bt], writes=[Bt])
        P.op("dve", f_copy(kf[:], ki[:]), reads=[Bt], writes=[Bt])
        P.op("dve", f_stt(ang[:], kf[:], -C1, ang[:], ALU.mult, ALU.add), reads=[Bt], writes=[Bt])
        P.op("dve", f_stt(ang[:], kf[:], -C2, ang[:], ALU.mult, ALU.add), reads=[Bt], writes=[Bt])
        wrap(ang)
        P.op("dve", f_ts(r2[:], ang[:], PI / 2, None, ALU.add), reads=[Bt], writes=[Bt])
        wrap(r2)
        P.op("act", f_act(sn[:], r2[:], AF.Sin), reads=[Bt], writes=[Bt])
        P.dma("sp", tab[0, :, ch * CH:(ch + 1) * CH], sn[:], reads=[Bt], writes=[Btab])
        P.op("act", f_act(sn[:], ang[:], AF.Sin), reads=[Bt], writes=[Bt])
        P.dma("sp", tab[1, :, ch * CH:(ch + 1) * CH], sn[:], reads=[Bt], writes=[Btab])
        if ch == 0:
            staging_ops()
    P.barrier()
    A.off = work_end

    def load_x(i):
        P.dma("pool", xb[i % 2][:], x_in[i * 128:(i + 1) * 128, :], reads=[Bxin], writes=[Bxb[i % 2]])
        P.dma("sp", cs[i % 2][:], tab[:, :, i * 128:(i + 1) * 128].rearrange("a p t -> p a t"), reads=[Btab], writes=[Bcs[i % 2]])

    def tpv(k):
        return ps[k][:, :].bitcast(BF16)

    bctr = {"m1": 0, "m2": 0}

    def nb1():
        bctr["m1"] += 1
        return 6 + bctr["m1"] % 2

    def nb2():
        bctr["m2"] += 1
        return 4 + bctr["m2"] % 2

    def M1(i):
        b = i % 2
        r0 = i * 128
        if i + 1 < NT:
            load_x(i + 1)
        if bgq and i >= 1:
            bgq.pop(0)()
        k = nb1()
        for c in range(8):
            P.op("pe", f_tr(tpv(k)[:, c * 128:(c + 1) * 128], xb[b][:, c * 128:(c + 1) * 128], C.ident[:]),
                 reads=[Bxb[b], C.Bident], writes=[Bps[k]], signal=(c == 7))
        P.op("dve", f_copy(xT[:], tpv(k).rearrange("p (c t) -> p c t", c=8)), reads=[Bps[k]], writes=[BxT])
        yield
        for (dst, Bdst, c0, eng) in ((qmT[b], BqmT[b], 0, "act"), (kmT[b], BkmT[b], 512, "dve")):
            k = nb1()
            for h in range(4):
                for c in range(8):
                    P.op("pe", f_mm(ps[k][:, h * 128:(h + 1) * 128], win[:, c, c0 + h * 128:c0 + (h + 1) * 128], xT[:, c, :], c == 0, c == 7),
                         reads=[Bw, BxT], writes=[Bps[k]], signal=(c == 7 and h == 3))
                if DBG.get("coarse", 1) < 1:
                    yield
            if eng == "act":
                P.op("act", f_act(dst[:].rearrange("p h t -> p (h t)"), ps[k][:, :], AF.Copy), reads=[Bps[k]], writes=[Bdst])
            else:
                P.op("dve", f_copy(dst[:].rearrange("p h t -> p (h t)"), ps[k][:, :]), reads=[Bps[k]], writes=[Bdst])
            yield
        k = nb1()
        for m in range(2):
            for c in range(8):
                P.op("pe", f_mm(ps[k][:, m * 128:(m + 1) * 128], win[:, c, 2440 + m * 128:2440 + (m + 1) * 128], xT[:, c, :], c == 0, c == 7),
                     reads=[Bw, BxT], writes=[Bps[k]], signal=(c == 7 and m == 1))
        P.op("dve", f_tt(rt1[:], ps[k][:, 0:128], cs[b][:, 0, :], ALU.mult), reads=[Bps[k], Bcs[b]], writes=[Brt])
        P.op("dve", f_tt(rt2[:], ps[k][:, 128:256], cs[b][:, 1, :], ALU.mult), reads=[Bps[k], Bcs[b]], writes=[Brt])
        P.op("dve", f_tt(krT[:, r0:r0 + 128], rt1[:], rt2[:], ALU.add), reads=[Brt], writes=[Bkr[i]])
        yield
        g = gA[b]
        Bg = BgA[b]
        for gi, (c0, c1) in enumerate(((512, 1024), (1024, 1536), (1536, 2048), (2048, 2440))):
            k = nb1()
            for c in range(8):
                P.op("pe", f_mm(ps[k][:, 0:c1 - c0], xT[:, c, :], win[:, c, c0:c1], c == 0, c == 7),
                     reads=[Bw, BxT], writes=[Bps[k]], signal=(c == 7))
            if DBG.get("coarse", 1) < 2:
                yield
            if gi == 0:
                P.op("act", f_act(ktok[b][:], ps[k][:, :], AF.Copy), reads=[Bps[k]], writes=[Bktok[b]])
            elif gi == 1:
                P.op("dve", f_copy(vs[b][:], ps[k][:, :]), reads=[Bps[k]], writes=[Bvs[b]])
            elif gi == 2:
                P.op("act", f_act(og[b][:], ps[k][:, :], AF.Exp, scale=-1.0), reads=[Bps[k]], writes=[Bog[b]])
            else:
                P.op("dve", f_tt(g[:, 0:8], ps[k][:, 0:8], bg[:], ALU.add), reads=[Bps[k], Bw], writes=[Bg])
                P.op("act", f_act(cq32[:], ps[k][:, 8:392], AF.Copy), reads=[Bps[k]], writes=[Bcq])
            yield
        P.op("dve", f_ts(og[b][:], og[b][:], 1.0, None, ALU.add), reads=[Bog[b]], writes=[Bog[b]])
        P.op("dve", lambda e: e.reciprocal(og[b][:], og[b][:]), reads=[Bog[b]], writes=[Bog[b]])
        yield
        P.op("act", f_act(g[:, 8:12], g[:, 4:8], AF.Exp, scale=-1.0), reads=[Bg], writes=[Bg])
        P.op("act", f_act(g[:, 8:12], g[:, 8:12], AF.Ln, bias=cst[:, 0:1]), reads=[Bg, Bcst], writes=[Bg])
        yield
        k = nb1()
        P.op("pe", f_mm(ps[k][:, 0:4], tri[:], g[:, 8:12], True, True), reads=[Bcst, Bg], writes=[Bps[k]], signal=False)
        P.op("pe", f_mm(ps[k][:, 4:8], ones[:], g[:, 8:12], True, True), reads=[Bcst, Bg], writes=[Bps[k]])
        P.op("dve", f_copy(g[:, 12:20], ps[k][:, 0:8]), reads=[Bps[k]], writes=[Bg])
        P.op("dve", f_tt(g[:, 20:24], g[:, 0:4], g[:, 12:16], ALU.add), reads=[Bg], writes=[Bg])
        yield
        P.op("act", f_act(g[:, 24:28], g[:, 20:24], AF.Exp, bias=cst[:, 1:2]), reads=[Bg, Bcst], writes=[Bg])
        P.op("act", f_act(g[:, 28:32], g[:, 12:16], AF.Exp, scale=-1.0), reads=[Bg], writes=[Bg])
        P.op("act", f_act(ebt[i % 3][:], g[:, 16:20], AF.Exp, scale=-1.0), reads=[Bg], writes=[Bebt[i % 3]])
        yield
        P.op("pool", f_memset(gC[:, 0:2], 0.0), reads=[BgC], writes=[BgC])
        P.op("act", f_act(junk1[:, 0:256], cq32[:, 0:256], AF.Square, accum_out=gC[:, 0:1]), reads=[Bcq], writes=[BgC, Bjunk1])
        P.op("act", f_act(junk1[:, 0:128], cq32[:, 256:384], AF.Square, accum_out=gC[:, 1:2]), reads=[Bcq], writes=[BgC, Bjunk1])
        P.op("act", f_act(gC[:, 2:3], gC[:, 0:1], AF.Ln, bias=cst[:, 2:3], scale=1.0 / 256), reads=[BgC, Bcst], writes=[BgC])
        P.op("act", f_act(gC[:, 3:4], gC[:, 1:2], AF.Ln, bias=cst[:, 2:3], scale=1.0 / 128), reads=[BgC, Bcst], writes=[BgC])
        P.op("act", f_act(gC[:, 2:4], gC[:, 2:4], AF.Exp, scale=-0.5), reads=[BgC], writes=[BgC])
        yield
        P.op("dve", f_ts(cqn[:, 0:256], cq32[:, 0:256], gC[:, 2:3], None, ALU.mult), reads=[Bcq, BgC], writes=[Bcqn])
        P.op("dve", f_ts(cqn[:, 256:384], cq32[:, 256:384], gC[:, 3:4], None, ALU.mult), reads=[Bcq, BgC], writes=[Bcqn])
        k = nb1()
        for c in range(3):
            P.op("pe", f_tr(tpv(k)[:, c * 128:(c + 1) * 128], cqn[:, c * 128:(c + 1) * 128], C.ident[:]),
                 reads=[Bcqn, C.Bident], writes=[Bps[k]], signal=(c == 2))
        P.op("dve", f_copy(cqT[:].rearrange("p c t -> p (c t)"), tpv(k)[:, 0:384]), reads=[Bps[k]], writes=[BcqT])
        yield
        q3 = q2[(i // 2) % 3][:, :, (i % 2) * 128:(i % 2 + 1) * 128]
        Bq3 = Bq2[(i // 2) % 3][i % 2]
        ka = nb1()
        for m in range(4):
            for c in range(2):
                P.op("pe", f_mm(ps[ka][:, m * 128:(m + 1) * 128], wuq[:, c, m * 128:(m + 1) * 128], cqT[:, c, :], c == 0, c == 1),
                     reads=[Bw, BcqT], writes=[Bps[ka]], signal=(c == 1 and m == 3))
        P.op("act", f_act(q3[:, 0:4, :], ps[ka][:, :].rearrange("p (h t) -> p h t", h=4), AF.Copy), reads=[Bps[ka]], writes=[Bq3])
        yield
        kb_ = nb1()
        for m in range(4, 8):
            for c in range(2):
                P.op("pe", f_mm(ps[kb_][:, (m - 4) * 128:(m - 3) * 128], wuq[:, c, m * 128:(m + 1) * 128], cqT[:, c, :], c == 0, c == 1),
                     reads=[Bw, BcqT], writes=[Bps[kb_]], signal=(c == 1 and m == 7))
        for blk in range(2):
            P.op("dve", f_tt(rt1[:], ps[kb_][:, blk * 128:(blk + 1) * 128], cs[b][:, 0, :], ALU.mult), reads=[Bps[kb_], Bcs[b]], writes=[Brt])
            P.op("dve", f_tt(rt2[:], ps[kb_][:, (2 + blk) * 128:(3 + blk) * 128], cs[b][:, 1, :], ALU.mult), reads=[Bps[kb_], Bcs[b]], writes=[Brt])
            P.op("dve", f_tt(q3[:, 4 + blk, :], rt1[:], rt2[:], ALU.add), reads=[Brt], writes=[Bq3])
        yield
        k = nb1()
        for h in range(4):
            P.op("pe", f_mm(ps[k][:, h * 128:(h + 1) * 128], wukv[:, h * 128:(h + 1) * 128], cqT[:, 2, :], True, True),
                 reads=[Bw, BcqT], writes=[Bps[k]], signal=(h == 3))
        P.op("act", f_act(knT[:, :, r0:r0 + 128], ps[k][:, :].rearrange("p (h t) -> p h t", h=4), AF.Copy), reads=[Bps[k]], writes=[Bkn[i]])
        yield
        k = nb1()
        P.op("pe", f_mm(ps[k][:, :], cqT[:, 2, :], wukv[:, 512:1024], True, True), reads=[Bw, BcqT], writes=[Bps[k]])
        P.op("dve", f_copy(vaug[:, i, :, 0:128], ps[k][:, :].rearrange("p (h d) -> p h d", h=4)), reads=[Bps[k]], writes=[Bva[i]])
        yield

    def M2(i):
        b = i % 2
        g, Bg = gA[b], BgA[b]
        eb, Beb = ebt[i % 3], Bebt[i % 3]
        ebp, Bebp = ebt[(i - 1) % 3], Bebt[(i - 1) % 3]
        hm3 = Xs[:, :, 0:128]
        k = nb2()
        for h in range(4):
            P.op("pe", f_mm(ps[k][:, h * 128:(h + 1) * 128], kmT[b][:, h, :], qmT[b][:, h, :], True, True),
                 reads=[BkmT[b], BqmT[b]], writes=[Bps[k]], signal=(h == 3))
        P.op("dve", f_tt(APT[:], ps[k][:, :], mask4[:], ALU.mult), reads=[Bps[k], Bcst], writes=[BAPT])
        yield
        for h in range(4):
            P.op("act", f_act(vpa[:, h, 0:128], vs[b][:, h * 128:(h + 1) * 128], AF.Copy, scale=g[:, 24 + h:25 + h]),
                 reads=[Bvs[b], Bg], writes=[Bvpa])
        P.op("pool", f_copy(vpa[:, :, 128:129], g[:, 24:28].rearrange("p (h o) -> p h o", o=1)), reads=[Bg], writes=[Bvpa])
        yield
        kx = [nb2(), nb2()]
        for h in range(4):
            o_ = ps[kx[h // 2]][:, (h % 2) * 129:(h % 2) * 129 + 129]
            P.op("pe", f_mm(o_, APT[:, h * 128:(h + 1) * 128], vpa[:, h, 0:129], True, False), reads=[BAPT, Bvpa], writes=[Bps[kx[h // 2]]], signal=False)
            P.op("pe", f_mm(o_, qmT[b][:, h, :], Cbf[:, h, 0:129], False, True), reads=[BqmT[b], BC], writes=[Bps[kx[h // 2]]], signal=(h % 2 == 1))
            if h % 2 == 1:
                j = h // 2
                P.op("act", f_act(Xs[:, 2 * j:2 * j + 2, :].rearrange("p h d -> p (h d)"), ps[kx[j]][:, 0:258], AF.Copy), reads=[Bps[kx[j]]], writes=[BXs])
                yield
        kd = [nb2(), nb2()]
        for h in range(4):
            o_ = ps[kd[h // 2]][:, (h % 2) * 129:(h % 2) * 129 + 129]
            P.op("pe", f_mm(o_, ktok[b][:, h * 128:(h + 1) * 128], vpa[:, h, 0:129], True, True), reads=[Bktok[b], Bvpa], writes=[Bps[kd[h // 2]]], signal=(h % 2 == 1))
        yield
        for h in range(4):
            o_ = ps[kd[h // 2]][:, (h % 2) * 129:(h % 2) * 129 + 129]
            P.op("dve", f_stt(U32[:, h, :], U32[:, h, :], ebp[:, h:h + 1], o_, ALU.mult, ALU.add),
                 reads=[Bebp, Bps[kd[h // 2]]], writes=[BU])
        yield
        for h in range(4):
            P.op("act", f_act(Cbf[:, h, 0:129], U32[:, h, :], AF.Copy, scale=eb[:, h:h + 1]), reads=[BU, Beb], writes=[BC])
        yield
        P.op("dve", f_tt(gB[:, 0:4], Xs[:, :, 128:129].rearrange("p h o -> p (h o)"), g[:, 28:32], ALU.mult), reads=[BXs, Bg], writes=[BgB])
        P.op("act", f_act(gB[:, 0:4], gB[:, 0:4], AF.Abs), reads=[BgB], writes=[BgB])
        P.op("dve", f_ts(gB[:, 0:4], gB[:, 0:4], 1.0, None, ALU.max), reads=[BgB], writes=[BgB])
        P.op("dve", lambda e: e.reciprocal(gB[:, 4:8], gB[:, 0:4]), reads=[BgB], writes=[BgB])
        P.op("dve", f_tt(gB[:, 4:8], gB[:, 4:8], g[:, 28:32], ALU.mult), reads=[BgB, Bg], writes=[BgB])
        yield
        P.op("pool", f_memset(gB[:, 8:12], 0.0), reads=[BgB], writes=[BgB])
        for h in range(4):
            P.op("act", f_act(Xs[:, h, 0:128], Xs[:, h, 0:128], AF.Copy, scale=gB[:, 4 + h:5 + h]), reads=[BgB], writes=[BXs])
        yield
        for h in range(4):
            P.op("act", f_act(junk2[:, 0:128], Xs[:, h, 0:128], AF.Square, accum_out=gB[:, 8 + h:9 + h]), reads=[BXs], writes=[BgB, Bjunk2])
        P.op("act", f_act(gB[:, 12:16], gB[:, 8:12], AF.Ln, bias=cst[:, 2:3], scale=1.0 / 128), reads=[BgB, Bcst], writes=[BgB])
        P.op("act", f_act(gB[:, 12:16], gB[:, 12:16], AF.Exp, scale=-0.5), reads=[BgB], writes=[BgB])
        yield
        P.op("dve", f_tt(hm3, hm3, gmn[:, :].rearrange("p (h d) -> p h d", h=4), ALU.mult), reads=[Bw], writes=[BXs])
        P.op("dve", f_tt(hm3, hm3, og[b][:, :].rearrange("p (h d) -> p h d", h=4), ALU.mult), reads=[Bog[b]], writes=[BXs])
        yield
        ymi, Bymi = ym[i % 4], Bym[i % 4]
        for h in range(4):
            P.op("act", f_act(ymi[:, h * 128:(h + 1) * 128], Xs[:, h, 0:128], AF.Copy, scale=gB[:, 12 + h:13 + h]), reads=[BXs, BgB], writes=[Bymi])
        m2_done[i] = True
        yield

    def M3(p):
        i0, i1 = 2 * p, 2 * p + 1
        q2p = q2[p % 3]
        Bq = Bq2[p % 3]
        G = [(h, g) for h in range(4) for g in range(p + 1)]

        def scores(n):
            h, g = G[n]
            k = SCB[n % len(SCB)]
            pr = (h % 2) * 64
            for j, kb in enumerate((2 * g, 2 * g + 1)):
                o_ = ps[k][:, j * 256:(j + 1) * 256]
                P.op("pe", f_mm(o_, knT[:, h, kb * 128:(kb + 1) * 128], q2p[:, h, :], True, False), reads=[Bkn[kb], Bq[0], Bq[1]], writes=[Bps[k]], signal=False)
                P.op("pe", f_mm(o_, krT[pr:pr + 64, kb * 128:(kb + 1) * 128], q2p[pr:pr + 64, 4 + h // 2, :], False, True),
                     reads=[Bkr[kb], Bq[0], Bq[1]], writes=[Bps[k]], signal=(j == 1))

        def expmask(n):
            h, g = G[n]
            k = SCB[n % len(SCB)]
            pb = n % (LA + 1)
            P.op("act", f_act(PT[pb][:, :], ps[k][:, :], AF.Exp, scale=ATT_SCALE), reads=[Bps[k]], writes=[BPT[pb]])
            if g == p:
                for off in (0, 256 + 128):
                    dsl = PT[pb][:, off:off + 128]
                    P.op("pool", (lambda d: (lambda e: e.affine_select(out=d, in_=d, compare_op=ALU.is_ge, fill=0.0, base=0,
                                                                      pattern=[[1, 128]], channel_multiplier=-1)))(dsl),
                         reads=[BPT[pb]], writes=[BPT[pb]])

        def pv(n):
            h, g = G[n]
            pb = n % (LA + 1)
            for j, kb in enumerate((2 * g, 2 * g + 1)):
                for t in range(2):
                    if kb > 2 * p + t:
                        continue
                    acc = ps[t][:, (h % 2) * 129:(h % 2) * 129 + 129]
                    P.op("pe", f_mm(acc, PT[pb][:, j * 256 + t * 128:j * 256 + (t + 1) * 128], vaug[:, kb, h, 0:129], kb == 0, kb == 2 * p + t),
                         reads=[BPT[pb], Bva[kb]], writes=[Bps[t]], signal=(j == 1 and t == 1))

        if not DBG.get("skip_att"):
            for n0 in range(min(LA, len(G))):
                scores(n0)
            for n in range(len(G)):
                if n + LA < len(G):
                    scores(n + LA)
                expmask(n)
                pv(n)
                h, g = G[n]
                if g == p:
                    for t in range(2):
                        a2 = ps[t][:, (h % 2) * 129:(h % 2) * 129 + 129]
                        P.op("dve", lambda e, a2=a2, c_=2 * (h % 2) + t: e.reciprocal(rc[:, c_:c_ + 1], a2[:, 128:129]), reads=[Bps[t]], writes=[Brc])
                        P.op("dve", f_ts(ya[t][:, h * 128:(h + 1) * 128], a2[:, 0:128], rc[:, 2 * (h % 2) + t:2 * (h % 2) + t + 1], None, ALU.mult),
                             reads=[Bps[t], Brc], writes=[Bya[t]])
                if n % DBG.get("m3g", 1) == 0:
                    yield
        for t, i in enumerate((i0, i1)):
            r0 = i * 128
            while not m2_done.get(i):
                yield
            P.dma("sp", xr[:], x_in[r0:r0 + 128, :], reads=[Bxin], writes=[Bxr])
            k = 2
            for c in range(8):
                src = ym[i % 4][:, c * 128:(c + 1) * 128] if c < 4 else ya[t][:, (c - 4) * 128:(c - 3) * 128]
                P.op("pe", f_tr(tpv(k)[:, c * 128:(c + 1) * 128], src, C.ident[:]),
                     reads=[Bym[i % 4], Bya[t], C.Bident], writes=[Bps[k]], signal=(c == 7))
            P.op("dve", f_copy(yT[:].rearrange("p c t -> p (c t)"), tpv(k)), reads=[Bps[k]], writes=[ByT])
            yield
            ko = [3, 2]
            for n in range(2):
                for c in range(8):
                    P.op("pe", f_mm(ps[ko[n]][:, :], yT[:, c, :], wout[:, c, n * 512:(n + 1) * 512], c == 0, c == 7),
                         reads=[ByT, Bwout], writes=[Bps[ko[n]]], signal=(c == 7))
                yield
            for n in range(2):
                P.op("dve", f_stt(xr[:, n * 512:(n + 1) * 512], xr[:, n * 512:(n + 1) * 512], ALPHA, ps[ko[n]][:, :], ALU.mult, ALU.add),
                     reads=[Bps[ko[n]]], writes=[Bxr])
            yield
            ln_stats(P, xr[:], Bxr, st, Bst, 0)
            ln_rstd(P, st, Bst, 1)
            yield
            ln_apply(P, xr[:], Bxr, st, Bst, 0, gb, Bw, x_out[r0:r0 + 128, :], Bxout)
            yield

    def interleave(gens):
        gens = [g_ for g_ in gens if g_ is not None]
        while gens:
            for g_ in list(gens):
                try:
                    next(g_)
                except StopIteration:
                    gens.remove(g_)

    def step(gen):
        try:
            next(gen)
            return True
        except StopIteration:
            return False

    load_x(0)
    for c0 in range(0, 8, 4):
        P.dma("pool", wout[:, c0:c0 + 4, :], w_out_v[:, c0:c0 + 4, :], writes=[Bwout])
    m3 = None
    m2_done = {}
    for t in range(NT + 3):
        gens = []
        if 0 <= t - 1 < NT:
            gens.append(M2(t - 1))
        if t < NT:
            gens.append(M1(t))
        if t >= 2 and (t - 2) % 2 == 0 and (t - 2) // 2 < NT // 2:
            while m3 is not None and step(m3):
                pass
            m3 = M3((t - 2) // 2)
        while gens:
            if not DBG.get("m3last"):
                if m3 is not None and not step(m3):
                    m3 = None
            for g_ in (list(gens) if not DBG.get("rev12") else list(gens)[::-1]):
                if g_ in gens and not step(g_):
                    gens.remove(g_)
            if DBG.get("m3last"):
                if m3 is not None and not step(m3):
                    m3 = None
    while m3 is not None and step(m3):
        pass
    P.barrier()
    A.off = save


def pool_phase(P, C, x_in, Bxin, x_out, Bxout, prm, after_setup=None):
    nc, A = C.nc, C.A
    save = A.off
    ps, Bps = C.ps, C.Bps
    NT = S // 128
    WIN = (2, 4, 8, 16)
    Wc = A.alloc("Wc", [128, 4, 128], BF16)
    Wc0 = A.alloc("Wc0", [128, 4, 128], BF16)
    Wp = A.alloc("Wp", [128, 4, 128], BF16)
    pw = A.alloc("pw", [128, 8, 256], BF16)
    gb = A.alloc("gb", [128, 2, D], F32)
    lsb = A.alloc("lsb", [128, D], F32)
    stg = A.alloc("stg", [128, 8, 256], F32)
    idf = A.alloc("idf", [128, 128], F32)
    tmp = A.alloc("tmp", [128, 128], F32)
    rcn = A.alloc("rcn", [128, 128], F32)
    xs = [A.alloc("xs", [128, D], F32) for _ in range(4)]
    xb = [A.alloc("xb", [128, D], BF16) for _ in range(2)]
    pT = [A.alloc("pT", [128, 8, 128], BF16) for _ in range(2)]
    st = [A.alloc("st", [128, 1, 16], F32) for _ in range(2)]
    Bw, Bt = Buf("w"), Buf("t")
    Bxs = [Buf("xs0"), Buf("xs1"), Buf("xs2"), Buf("xs3")]
    Bxb = [Buf("xb0"), Buf("xb1")]
    BpT, Bst = [Buf("pT0"), Buf("pT1")], [Buf("st0"), Buf("st1")]

    def asel(t, pattern, cm, base, op):
        return lambda e: e.affine_select(out=t, in_=t, compare_op=op, fill=0.0, base=base, pattern=pattern, channel_multiplier=cm)

    Bs1, Bs2 = Buf("stg"), Buf("lsb")
    P.dma("sp", stg[:], prm["pool_w"].rearrange("g (cc p) d -> p (g cc) d", p=128), writes=[Bs1])
    P.dma("sp", lsb[:], prm["lscale"].partition_broadcast(128), writes=[Bs2])
    Bgb = Buf("gb")
    P.dma("sp", gb[:, 0, :], prm["ln_g"].partition_broadcast(128), writes=[Bgb])
    P.dma("sp", gb[:, 1, :], prm["ln_b"].partition_broadcast(128), writes=[Bgb])
    for j in range(8):
        g = j // 2
        P.op("dve", f_tt(pw[:, j, :], stg[:, j, :], lsb[:, g * 256:(g + 1) * 256], ALU.mult), reads=[Bs1, Bs2], writes=[Bw])
    P.op("pool", f_memset(idf[:], 1.0), writes=[Bt])
    P.op("pool", asel(idf[:], [[-1, 128]], 1, 0, ALU.is_equal), reads=[Bt], writes=[Bt])
    for g, w in enumerate(WIN):
        P.op("pool", f_memset(tmp[:], 1.0 / w), reads=[Bt], writes=[Bt])
        P.op("pool", asel(tmp[:], [[1, 128]], -1, 0, ALU.is_ge), reads=[Bt], writes=[Bt])
        P.op("pool", asel(tmp[:], [[-1, 128]], 1, w - 1, ALU.is_ge), reads=[Bt], writes=[Bt])
        P.op("pool", f_tt(Wc[:, g, :], tmp[:], idf[:], ALU.subtract), reads=[Bt], writes=[Bw])
        P.op("pool", f_memset(tmp[:], 1.0 / w), reads=[Bt, Bw], writes=[Bt])
        P.op("pool", asel(tmp[:], [[-1, 128]], 1, w - 1 - 128, ALU.is_ge), reads=[Bt], writes=[Bt])
        P.op("pool", f_copy(Wp[:, g, :], tmp[:]), reads=[Bt], writes=[Bw])
        P.op("pool", f_memset(rcn[:], 1.0 / w), reads=[Bt, Bw], writes=[Bt])
        for kk in range(w - 1, 0, -1):
            P.op("pool", (lambda kk=kk: (lambda e: e.affine_select(out=rcn[:], in_=rcn[:], compare_op=ALU.is_ge, fill=1.0 / kk, base=-kk,
                                                                pattern=[[1, 128]], channel_multiplier=0)))(),
                 reads=[Bt], writes=[Bt])
        P.op("pool", asel(rcn[:], [[1, 128]], -1, 0, ALU.is_ge), reads=[Bt], writes=[Bt])
        P.op("pool", asel(rcn[:], [[-1, 128]], 1, w - 1, ALU.is_ge), reads=[Bt], writes=[Bt])
        P.op("pool", f_tt(Wc0[:, g, :], rcn[:], idf[:], ALU.subtract), reads=[Bt], writes=[Bw])

    def load_x(i):
        P.dma("sp", xs[i % 4][:], x_in[i * 128:(i + 1) * 128, :], reads=[Bxin], writes=[Bxs[i % 4]])

    if after_setup is not None:
        after_setup()

    def TA(i):
        b = i % 2
        x4 = i % 4
        if i + 1 < NT:
            load_x(i + 1)
        P.op("act", f_act(xb[b][:], xs[x4][:], AF.Copy), reads=[Bxs[x4]], writes=[Bxb[b]])
        yield
        kp = [2 + (i % 2) * 2, 3 + (i % 2) * 2]
        for m in range(8):
            g = m // 2
            o_ = ps[kp[m // 4]][:, (m % 4) * 128:(m % 4 + 1) * 128]
            Wcur = Wc0 if i == 0 else Wc
            P.op("pe", f_mm(o_, xb[b][:, m * 128:(m + 1) * 128], Wcur[:, g, :], True, i == 0), reads=[Bxb[b], Bw], writes=[Bps[kp[m // 4]]],
                 signal=(i == 0 and m % 4 == 3))
            if i > 0:
                P.op("pe", f_mm(o_, xb[1 - b][:, m * 128:(m + 1) * 128], Wp[:, g, :], False, True), reads=[Bxb[1 - b], Bw], writes=[Bps[kp[m // 4]]],
                     signal=(m % 4 == 3))
            if m == 3:
                yield
        yield
        for j in range(2):
            P.op("act" if j == 0 else "dve",
                 (f_act(pT[b][:, 4 * j:4 * j + 4, :].rearrange("p c t -> p (c t)"), ps[kp[j]][:, :], AF.Copy) if j == 0 else
                  f_copy(pT[b][:, 4 * j:4 * j + 4, :].rearrange("p c t -> p (c t)"), ps[kp[j]][:, :])),
                 reads=[Bps[kp[j]]], writes=[BpT[b]])
        yield

    def TB(i):
        b = i % 2
        x4 = i % 4
        ko = [(i % 2), 6 + (i % 2)]
        for g in range(4):
            o_ = ps[ko[g // 2]][:, (g % 2) * 256:(g % 2 + 1) * 256]
            for cc in range(2):
                P.op("pe", f_mm(o_, pT[b][:, 2 * g + cc, :], pw[:, 2 * g + cc, :], cc == 0, cc == 1), reads=[BpT[b], Bw], writes=[Bps[ko[g // 2]]],
                     signal=(cc == 1 and g % 2 == 1))
            if g == 1:
                yield
        yield
        for n in range(2):
            P.op("dve", f_stt(xs[x4][:, n * 512:(n + 1) * 512], xs[x4][:, n * 512:(n + 1) * 512], ALPHA, ps[ko[n]][:, :], ALU.mult, ALU.add),
                 reads=[Bps[ko[n]]], writes=[Bxs[x4]])
            yield

    def TC(i):
        x4 = i % 4
        r0 = i * 128
        ln_stats(P, xs[x4][:], Bxs[x4], st[i % 2], Bst[i % 2], 0)
        yield
        ln_rstd(P, st[i % 2], Bst[i % 2], 1)
        yield
        ln_apply(P, xs[x4][:], Bxs[x4], st[i % 2], Bst[i % 2], 0, gb, Bgb, x_out[r0:r0 + 128, :], Bxout)
        yield

    def interleave(gens):
        gens = [g_ for g_ in gens if g_ is not None]
        while gens:
            for g_ in list(gens):
                try:
                    next(g_)
                except StopIteration:
                    gens.remove(g_)

    load_x(0)
    for t in range(NT + 2):
        interleave([TC(t - 2) if 0 <= t - 2 < NT else None, TB(t - 1) if 0 <= t - 1 < NT else None, TA(t) if t < NT else None])
    P.barrier()
    A.off = save


def build_full(phases=("A", "B", "C", "D"), debug=False):
    nc, C = new_ctx()
    ein = lambda n, s, d=F32: nc.dram_tensor(n, list(s), d, kind="ExternalInput").ap()
    x = ein("x", [S, D])
    prmA = dict(pos=ein("pos", [S], I32), w_in=ein("w_in", [D, 2504]), bgate=ein("bgate", [8]), mnorm=ein("mnorm", [512]),
                qnorm=ein("qnorm", [128, 2]), kvnorm=ein("kvnorm", [128, 1]), w_uq=ein("w_uq", [256, 768]),
                w_ukv=ein("w_ukv", [128, 1024]), w_out=ein("w_out", [D, D]), rope_freq=ein("rope_freq", [128, 1]),
                ln_g=ein("ln_mix_g0", [D]), ln_b=ein("ln_mix_b0", [D]))
    ffn = [dict(w_up=ein("w_up%d" % l, [D, 2 * FF]), cwb=ein("cwb%d" % l, [128, 44, 4]), w_dn=ein("w_dn%d" % l, [FF, D]),
                g=ein("ln_ffn_g%d" % l, [D]), b=ein("ln_ffn_b%d" % l, [D])) for l in range(2)]
    prmC = dict(pool_w=ein("pool_w", [4, 256, 256]), lscale=ein("lscale", [D]), ln_g=ein("ln_mix_g1", [D]), ln_b=ein("ln_mix_b1", [D]))
    out = nc.dram_tensor("out", [S, D], F32, kind="ExternalOutput").ap()
    kind = "ExternalOutput" if debug else "Internal"
    x1 = nc.dram_tensor("x1", [S, D], F32, kind=kind).ap()
    x2 = nc.dram_tensor("x2", [S, D], F32, kind=kind).ap()
    x3 = nc.dram_tensor("x3", [S, D], F32, kind=kind).ap()
    tab = nc.dram_tensor("ropetab", [2, 128, S], F32, kind="Internal").ap()
    P = Prog(nc)
    make_consts(P, C)
    Bx, B1, B2, B3, Bo, Btab = Buf("x"), Buf("x1"), Buf("x2"), Buf("x3"), Buf("out"), Buf("tab")
    precast = ("A" in phases) and ("B" in phases)
    if "A" in phases:
        bg = []
        if precast:
            wup_bf = nc.dram_tensor("wup0_bf", [D, 2 * FF], BF16, kind="Internal").ap()
            wdn_bf = nc.dram_tensor("wdn0_bf", [FF, D], BF16, kind="Internal").ap()
            for c in range(8):
                bg.append((lambda c=c: P.dma("pool", wup_bf[c * 128:(c + 1) * 128, :], ffn[0]["w_up"][c * 128:(c + 1) * 128, :], writes=[Buf()])))
            for c in range(4):
                bg.append((lambda c=c: P.dma("pool", wdn_bf[c * 704:(c + 1) * 704, :], ffn[0]["w_dn"][c * 704:(c + 1) * 704, :], writes=[Buf()])))
        mixer0_phase(P, C, x, Bx, x1, B1, prmA, tab, Btab, bgq=bg)
        assert not bg
    if "B" in phases:
        f = ffn[0]
        if precast:
            ffn_phase(P, C, x1, B1, x2, B2, wup_bf, f["cwb"], wdn_bf, f["g"], f["b"], wqueue="sp")
        else:
            ffn_phase(P, C, x1, B1, x2, B2, f["w_up"], f["cwb"], f["w_dn"], f["g"], f["b"])
    W1 = None
    if "C" in phases:
        if "D" in phases:
            W1 = ffn_alloc_weights(C)
            W1.end_off = C.A.off
        pool_phase(P, C, x2, B2, x3, B3, prmC,
                   after_setup=(lambda: ffn_load_weights(P, W1, ffn[1]["w_up"], ffn[1]["w_dn"])) if W1 is not None else None)
    if "D" in phases:
        f = ffn[1]
        ffn_phase(P, C, x3, B3, out, Bo, f["w_up"], f["cwb"], f["w_dn"], f["g"], f["b"], W=W1)
    P.finish()
    P.emit()
    return nc, P


def host_inputs(inp, bi):
    f32 = np.float32
    m = {}
    m["x"] = np.ascontiguousarray(inp["x"][bi])
    m["pos"] = np.ascontiguousarray(inp["positions"][bi]).astype(np.int32)
    m["w_in"] = np.ascontiguousarray(inp["even_w_in"][0])
    m["bgate"] = np.concatenate([inp["even_b_igate"][0], inp["even_b_fgate"][0]]).astype(f32)
    m["mnorm"] = np.ascontiguousarray(inp["even_mlstm_norm"][0])
    m["qnorm"] = np.ascontiguousarray(inp["even_q_norm"][0].reshape(2, 128).T)
    m["kvnorm"] = np.ascontiguousarray(inp["even_kv_norm"][0].reshape(128, 1))
    m["w_uq"] = np.ascontiguousarray(inp["even_w_uq"][0])
    m["w_ukv"] = np.ascontiguousarray(inp["even_w_ukv"][0])
    m["w_out"] = np.ascontiguousarray(inp["even_w_out"][0])
    m["rope_freq"] = (f32(10000.0) ** (-(np.arange(128) % 32).astype(f32) * f32(2) / f32(64))).astype(f32).reshape(128, 1)
    m["ln_mix_g0"] = np.ascontiguousarray(inp["ln_mix_g"][0]); m["ln_mix_b0"] = np.ascontiguousarray(inp["ln_mix_b"][0])
    m["ln_mix_g1"] = np.ascontiguousarray(inp["ln_mix_g"][1]); m["ln_mix_b1"] = np.ascontiguousarray(inp["ln_mix_b"][1])
    for l in range(2):
        m["w_up%d" % l] = np.ascontiguousarray(inp["ffn_w_up"][l])
        m["cwb%d" % l] = host_cwb(inp["ffn_conv_w"][l], inp["ffn_conv_b"][l])
        m["w_dn%d" % l] = np.ascontiguousarray(inp["ffn_w_down"][l])
        m["ln_ffn_g%d" % l] = np.ascontiguousarray(inp["ln_ffn_g"][l]); m["ln_ffn_b%d" % l] = np.ascontiguousarray(inp["ln_ffn_b"][l])
    m["pool_w"] = np.ascontiguousarray(inp["odd_pool_w"][0])
    m["lscale"] = np.ascontiguousarray(inp["odd_layer_scale"][0])
    return m


_CACHE = {}


def kernel(**inputs):
    inp = {k: np.asarray(v) for k, v in inputs.items()}
    if "nc" not in _CACHE:
        _CACHE["nc"] = build_full()[0]
    nc = _CACHE["nc"]
    in_maps = [host_inputs(inp, bi) for bi in range(8)]
    res = run_bass_kernel_spmd(nc, in_maps, core_ids=list(range(8)))
    return np.stack([np.asarray(r["out"]) for r in res.results], axis=0).astype(np.float32)
```

```python
import numpy as np
from contextlib import ExitStack
import concourse.bass as bass
import concourse.mybir as mybir
from concourse.bass_utils import run_bass_kernel_spmd

F32 = mybir.dt.float32
BF16 = mybir.dt.bfloat16
I32 = mybir.dt.int32
ALU = mybir.AluOpType
AF = mybir.ActivationFunctionType
AX = mybir.AxisListType

D = 1024
S = 4096
FF = 2816
ALPHA = float((2 * 2) ** 0.25)
LN_EPS = 1e-5
RMS_EPS = 1e-6
ARENA_LO, ARENA_HI = 16512, 229344


class Buf:
    __slots__ = ("name", "w", "r", "excl")

    def __init__(self, name="", excl=False):
        self.name = name
        self.w = None
        self.r = {}
        self.excl = excl


class Prog:
    ENG = ("pe", "act", "dve", "pool", "sp")

    def __init__(self, nc, n_dma_sems=24):
        self.nc = nc
        self.q = {e: [] for e in self.ENG}
        self.cnt = {e: 0 for e in self.ENG}
        self.waited = {e: {} for e in self.ENG}
        self.n_dma_sems = n_dma_sems
        self.dma_val = [0] * n_dma_sems
        self.dma_next = 0
        self.dma_next_sw = 0
        self.sems = {}
        self.ninstr = 0

    def _wait(self, eng, ev):
        if ev is None:
            return
        key, val = ev
        if val <= 0:
            return
        if key == eng and eng == "pe":
            return
        if self.waited[eng].get(key, 0) >= val:
            return
        self.waited[eng][key] = val
        self.q[eng].append(("w", key, val))

    def _deps(self, eng, reads, writes):
        for b in reads:
            self._wait(eng, b.w)
            if b.excl:
                for k, v in b.r.items():
                    if k != eng:
                        self._wait(eng, (k, v))
        for b in writes:
            self._wait(eng, b.w)
            for k, v in b.r.items():
                self._wait(eng, (k, v))

    def _record(self, ev, reads, writes):
        k, v = ev
        for b in reads:
            if b.r.get(k, 0) < v:
                b.r[k] = v
        for b in writes:
            b.w = ev
            b.r = {}

    def op(self, eng, fn, reads=(), writes=(), signal=True):
        self._deps(eng, reads, writes)
        self.ninstr += 1
        if signal:
            self.cnt[eng] += 1
            ev = (eng, self.cnt[eng])
            self.q[eng].append(("i", fn, eng))
        else:
            ev = (eng, self.cnt[eng] + 1)
            self.q[eng].append(("n", fn, None))
        self._record(ev, reads, writes)
        return ev

    def dma(self, queue, out_ap, in_ap, reads=(), writes=(), **kw):
        nsw = self.n_dma_sems // 3
        if queue == "pool":
            k = self.dma_next_sw
            self.dma_next_sw = (k + 1) % nsw
        else:
            k = nsw + self.dma_next
            self.dma_next = (self.dma_next + 1) % (self.n_dma_sems - nsw)
        key = ("dma", k)
        self._wait(queue, (key, self.dma_val[k]))
        self._deps(queue, reads, writes)
        self.dma_val[k] += 16
        ev = (key, self.dma_val[k])
        self.ninstr += 1
        self.q[queue].append(("d", out_ap, in_ap, key, kw))
        self._record(ev, reads, writes)
        return ev

    def barrier(self):
        for eng in self.ENG:
            for k in range(self.n_dma_sems):
                self._wait(eng, (("dma", k), self.dma_val[k]))
            for e in ("pe", "act", "dve", "pool"):
                if e != eng:
                    self._wait(eng, (e, self.cnt[e]))

    def finish(self, eng="sp"):
        for k in range(self.n_dma_sems):
            self._wait(eng, (("dma", k), self.dma_val[k]))
        for e in ("pe", "act", "dve", "pool"):
            self._wait(eng, (e, self.cnt[e]))

    def emit(self):
        nc = self.nc
        with ExitStack() as st:
            for e in ("pe", "act", "dve", "pool"):
                self.sems[e] = st.enter_context(nc.semaphore("c_" + e))
            for k in range(self.n_dma_sems):
                self.sems[("dma", k)] = st.enter_context(nc.semaphore("d_%d" % k))
            block = st.enter_context(nc.Block())
            sems = self.sems

            def run(eng_obj, items):
                for it in items:
                    t = it[0]
                    if t == "w":
                        eng_obj.wait_ge(sems[it[1]], it[2])
                    elif t == "i":
                        it[1](eng_obj).then_inc(sems[it[2]], 1)
                    elif t == "n":
                        it[1](eng_obj)
                    else:
                        eng_obj.dma_start(out=it[1], in_=it[2], **it[4]).then_inc(sems[it[3]], 16)

            q = self.q

            @block.tensor
            def _(e):
                run(e, q["pe"])

            @block.scalar
            def _(e):
                run(e, q["act"])

            @block.vector
            def _(e):
                run(e, q["dve"])

            @block.gpsimd
            def _(e):
                run(e, q["pool"])

            @block.sync
            def _(e):
                run(e, q["sp"])


def _dtsize(dt):
    return {F32: 4, BF16: 2, I32: 4}[dt]


class Arena:
    def __init__(self, nc):
        self.nc = nc
        self.off = ARENA_LO
        self.n = 0

    def alloc(self, name, shape, dtype):
        nb = _dtsize(dtype)
        for s in shape[1:]:
            nb *= s
        off = (self.off + 31) // 32 * 32
        assert off + nb <= ARENA_HI, ("SBUF overflow", name, off, nb)
        self.off = off + nb
        self.n += 1
        return self.nc.alloc_sbuf_tensor_at("%s_%d" % (name, self.n), list(shape), dtype, offset=off)


def f_mm(out, lhsT, rhs, start, stop):
    return lambda e: e.matmul(out, lhsT, rhs, start=start, stop=stop)


def f_tr(out, in_, ident):
    return lambda e: e.transpose(out, in_, ident)


def f_act(out, in_, func, bias=None, scale=None, accum_out=None):
    kw = {}
    if bias is not None:
        kw["bias"] = bias
    if scale is not None:
        kw["scale"] = scale
    if accum_out is not None:
        kw["accum_out"] = accum_out
    return lambda e: e.activation(out, in_, func, **kw)


def f_copy(out, in_):
    return lambda e: e.tensor_copy(out, in_)


def f_ts(out, in0, s1, s2, op0, op1=None):
    if op1 is None:
        return lambda e: e.tensor_scalar(out, in0, s1, None, op0=op0)
    return lambda e: e.tensor_scalar(out, in0, s1, s2, op0=op0, op1=op1)


def f_tt(out, in0, in1, op):
    return lambda e: e.tensor_tensor(out, in0, in1, op=op)


def f_stt(out, in0, scalar, in1, op0, op1):
    return lambda e: e.scalar_tensor_tensor(out, in0, scalar, in1, op0=op0, op1=op1)


def f_memset(ap, v):
    return lambda e: e.memset(ap, v)


class Ctx:
    pass


def make_consts(P, C):
    A = C.A
    C.ident = A.alloc("ident", [128, 128], BF16)
    C.Bident = Buf("ident")
    C.eps = A.alloc("eps", [128, 2], F32)
    C_EPS[0] = C.eps
    P.op("pool", f_memset(C.eps[:, 0:1], LN_EPS), writes=[C.Bident])
    P.op("pool", f_memset(C.eps[:, 1:2], RMS_EPS), writes=[C.Bident])
    P.op("pool", f_memset(C.ident[:], 1.0), writes=[C.Bident])
    P.op("pool", lambda e: e.affine_select(out=C.ident[:], in_=C.ident[:], compare_op=ALU.is_equal, fill=0.0,
                                           base=0, pattern=[[-1, 128]], channel_multiplier=1),
         reads=[C.Bident], writes=[C.Bident])


def ln_stats(P, z, Bz, st, Bst, col):
    P.op("dve", lambda e: e.bn_stats(st[:, col, 0:6], z[:, 0:512]), reads=[Bz], writes=[Bst])
    P.op("dve", lambda e: e.bn_stats(st[:, col, 6:12], z[:, 512:1024]), reads=[Bz], writes=[Bst])
    P.op("dve", lambda e: e.bn_aggr(st[:, col, 12:14], st[:, col, 0:12]), reads=[Bst], writes=[Bst])


def ln_rstd(P, st, Bst, n):
    P.op("act", f_act(st[:, 0:n, 15:16], st[:, 0:n, 13:14], AF.Ln, bias=C_EPS[0][:, 0:1]), reads=[Bst], writes=[Bst])
    P.op("act", f_act(st[:, 0:n, 14:15], st[:, 0:n, 15:16], AF.Exp, scale=-0.5), reads=[Bst], writes=[Bst])


def ln_apply(P, z, Bz, st, Bst, col, gb, Bgb, out_dram_rows, Bout):
    P.op("dve", f_ts(z, z, st[:, col, 12:13], st[:, col, 14:15], ALU.subtract, ALU.mult), reads=[Bz, Bst], writes=[Bz])
    P.op("dve", f_tt(z, z, gb[:, 0, :], ALU.mult), reads=[Bz, Bgb], writes=[Bz])
    P.op("dve", f_tt(z, z, gb[:, 1, :], ALU.add), reads=[Bz, Bgb], writes=[Bz])
    P.dma("sp", out_dram_rows, z, reads=[Bz], writes=[Bout])


C_EPS = [None]


def ffn_alloc_weights(C):
    A = C.A
    W = Ctx()
    W.wup = A.alloc("wup", [128, 8, 2 * FF], BF16)
    W.wdn = A.alloc("wdn", [128, 22, D], BF16)
    W.Bwup = [Buf("wup%d" % j) for j in range(22)]
    W.Bwdn = Buf("wdn")
    return W


def ffn_load_weights(P, W, w_up, w_down, queue="pool"):
    w_up_v = w_up.rearrange("(c p) (h j n) -> p c h j n", p=128, h=2, j=22)
    wv = W.wup[:, :, :].rearrange("p c (h j n) -> p c h j n", h=2, j=22)
    for j in range(22):
        for h in range(2):
            P.dma(queue, wv[:, :, h, j, :], w_up_v[:, :, h, j, :], writes=[W.Bwup[j]])
        if j == 10 or j == 21:
            j0 = 0 if j == 10 else 11
            w_dn_v = w_down.rearrange("(j p) n -> p j n", p=128)
            P.dma(queue, W.wdn[:, j0:j0 + 11, :], w_dn_v[:, j0:j0 + 11, :], writes=[W.Bwdn])


def ffn_phase(P, C, x_in, Bxin, x_out, Bxout, w_up, cwb, w_down, ln_g, ln_b, W=None, wqueue="pool"):
    nc, A = C.nc, C.A
    save = A.off
    TT, NT = 256, S // 256
    NSET = DBG.get("nset", 5)
    preloaded = W is not None
    if W is None:
        W = ffn_alloc_weights(C)
    else:
        A.off = W.end_off
    wup, wdn, Bwup, Bwdn = W.wup, W.wdn, W.Bwup, W.Bwdn
    cw = A.alloc("cw", [128, 44, 4], F32)
    gb = A.alloc("gb", [128, 2, D], F32)
    hT = [A.alloc("hT", [128, 22, TT], BF16) for _ in range(2)]
    xT = [A.alloc("xT", [128, 8, TT], BF16) for _ in range(2)]
    xr = [A.alloc("xr", [128, D], F32) for _ in range(2)]
    xb = [A.alloc("xb", [128, 2, D], BF16) for _ in range(2)]
    u = [A.alloc("u", [128, 2, TT], F32) for _ in range(NSET)]
    upx = [A.alloc("upx", [128, 2, TT + 2], F32) for _ in range(NSET)]
    halo = [A.alloc("halo", [128, 22, 2, 2], F32)] * 2
    st = A.alloc("st", [128, 2, 16], F32)
    Bcw, Bgb = Buf("cw"), Buf("gb")
    BhT = [Buf("hT0"), Buf("hT1")]
    BxT = [Buf("xT0"), Buf("xT1")]
    Bxr = [Buf("xr0"), Buf("xr1")]
    Bxb = [[Buf("xb"), Buf("xb")] for _ in range(2)]
    Bu = [[Buf("u"), Buf("u")] for _ in range(NSET)]
    Bupx = [Buf("upx") for _ in range(NSET)]
    Bhalo = [Buf("halo0")] * 2
    Bst = Buf("st")
    ps = C.ps
    Bps = C.Bps
    up_banks = [0, 1, 2]
    dn_banks = [3, 4, 5, 6]
    tp_bank = 7
    tp = ps[tp_bank][:, :].bitcast(BF16)

    P.dma("sp", cw[:], cwb, writes=[Bcw])
    P.dma("sp", gb[:, 0, :], ln_g.partition_broadcast(128), writes=[Bgb])
    P.dma("sp", gb[:, 1, :], ln_b.partition_broadcast(128), writes=[Bgb])
    P.op("pool", f_memset(halo[0][:], 0.0), writes=[Bhalo[0]])

    def load_xb(tt):
        b = tt % 2
        for s in range(2):
            r0 = tt * TT + s * 128
            P.dma("pool", xb[b][:, s, :], x_in[r0:r0 + 128, :], reads=[Bxin], writes=[Bxb[b][s]])

    def prep_xT(tt):
        b = tt % 2
        for s in range(2):
            for c in range(8):
                P.op("pe", f_tr(tp[:, c * 128:(c + 1) * 128], xb[b][:, s, c * 128:(c + 1) * 128], C.ident[:]),
                     reads=[Bxb[b][s], C.Bident], writes=[Bps[tp_bank]], signal=(c == 7))
            P.op("dve", f_copy(xT[b][:, :, s * 128:(s + 1) * 128], tp.rearrange("p (c t) -> p c t", c=8)),
                 reads=[Bps[tp_bank]], writes=[BxT[b]])

    def pair_ctx(n):
        tt, j = divmod(n, 22)
        k = up_banks[n % 3]
        ub = n % NSET
        pk = ps[k][:, :].rearrange("p (h t) -> p h t", h=2)
        return tt, j, tt % 2, k, ub, pk

    def S0(n):
        tt, j, b, k, ub, pk = pair_ctx(n)
        for half, jj in ((0, j), (1, j + 22)):
            for c in range(8):
                P.op("pe", f_mm(pk[:, half, :], wup[:, c, jj * 128:(jj + 1) * 128], xT[b][:, c, :], c == 0, c == 7),
                     reads=[Bwup[j], BxT[b]], writes=[Bps[k]], signal=(c == 7 and half == 1))

    def S1(n):
        tt, j, b, k, ub, pk = pair_ctx(n)
        ho, hn = halo[tt % 2], halo[(tt + 1) % 2]
        Bho, Bhn = Bhalo[tt % 2], Bhalo[(tt + 1) % 2]
        ux = upx[ub]
        P.op("act", f_act(ux[:, :, 2:TT + 2], pk, AF.Copy), reads=[Bps[k]], writes=[Bupx[ub]])
        P.op("act", f_act(ux[:, :, 0:2], ho[:, j, :, :], AF.Copy), reads=[Bho], writes=[Bupx[ub]])
        P.op("act", f_act(hn[:, j, :, :], ux[:, :, TT:TT + 2], AF.Copy), reads=[Bupx[ub]], writes=[Bhn])

    def S2(n):
        tt, j, b, k, ub, pk = pair_ctx(n)
        uu, ux = u[ub], upx[ub]
        for half, jj in ((0, j), (1, j + 22)):
            P.op("act", f_act(uu[:, half, :], ux[:, half, 2:TT + 2], AF.Identity, bias=cw[:, jj, 3:4], scale=cw[:, jj, 2:3]),
                 reads=[Bupx[ub], Bcw], writes=[Bu[ub][half]])

    def S3(n):
        tt, j, b, k, ub, pk = pair_ctx(n)
        uu, ux = u[ub], upx[ub]
        for half, jj in ((0, j), (1, j + 22)):
            P.op("dve", f_stt(uu[:, half, :], ux[:, half, 1:TT + 1], cw[:, jj, 1:2], uu[:, half, :], ALU.mult, ALU.add),
                 reads=[Bupx[ub], Bcw, Bu[ub][half]], writes=[Bu[ub][half]])
        for half, jj in ((0, j), (1, j + 22)):
            P.op("dve", f_stt(uu[:, half, :], ux[:, half, 0:TT], cw[:, jj, 0:1], uu[:, half, :], ALU.mult, ALU.add),
                 reads=[Bupx[ub], Bcw, Bu[ub][half]], writes=[Bu[ub][half]])

    def S4(n):
        tt, j, b, k, ub, pk = pair_ctx(n)
        uu = u[ub]
        P.op("act", f_act(uu[:, 0, :], uu[:, 0, :], AF.Silu), reads=[Bu[ub][0]], writes=[Bu[ub][0]])

    def S5(n):
        tt, j, b, k, ub, pk = pair_ctx(n)
        uu = u[ub]
        P.op("dve", f_tt(hT[b][:, j, :], uu[:, 0, :], uu[:, 1, :], ALU.mult),
             reads=[Bu[ub][0], Bu[ub][1]], writes=[BhT[b]])

    def down_pieces(tt):
        b = tt % 2
        pieces = []

        def p_load():
            for s in range(2):
                r0 = tt * TT + s * 128
                P.dma("sp", xr[s][:], x_in[r0:r0 + 128, :], reads=[Bxin], writes=[Bxr[s]])
        pieces.append(p_load)

        def mk_mm(s, n):
            def f():
                k = dn_banks[s * 2 + n]
                for j in range(22):
                    P.op("pe", f_mm(ps[k][:, :], hT[b][:, j, s * 128:(s + 1) * 128], wdn[:, j, n * 512:(n + 1) * 512], j == 0, j == 21),
                         reads=[BhT[b], Bwdn], writes=[Bps[k]], signal=(j == 21))
            return f

        def mk_res(s, n):
            def f():
                k = dn_banks[s * 2 + n]
                zz = xr[s][:]
                P.op("dve", f_stt(zz[:, n * 512:(n + 1) * 512], zz[:, n * 512:(n + 1) * 512], ALPHA, ps[k][:, :], ALU.mult, ALU.add),
                     reads=[Bps[k]], writes=[Bxr[s]])
            return f

        for s in range(2):
            for n in range(2):
                pieces.append(mk_mm(s, n))
                if s * 2 + n >= 1:
                    s2, n2 = divmod(s * 2 + n - 1, 2)
                    pieces.append(mk_res(s2, n2))
        pieces.append(mk_res(1, 1))
        pieces.append(lambda: ln_stats(P, xr[0][:], Bxr[0], st, Bst, 0))
        pieces.append(lambda: ln_stats(P, xr[1][:], Bxr[1], st, Bst, 1))
        pieces.append(lambda: ln_rstd(P, st, Bst, 2))
        for s in range(2):
            r0 = tt * TT + s * 128
            pieces.append((lambda s=s: P.op("dve", f_ts(xr[s][:], xr[s][:], st[:, s, 12:13], st[:, s, 14:15], ALU.subtract, ALU.mult),
                                            reads=[Bxr[s], Bst], writes=[Bxr[s]])))
            pieces.append((lambda s=s: P.op("dve", f_tt(xr[s][:], xr[s][:], gb[:, 0, :], ALU.mult), reads=[Bxr[s], Bgb], writes=[Bxr[s]])))
            pieces.append((lambda s=s, r0=r0: (P.op("dve", f_tt(xr[s][:], xr[s][:], gb[:, 1, :], ALU.add), reads=[Bxr[s], Bgb], writes=[Bxr[s]]),
                                               P.dma("sp", x_out[r0:r0 + 128, :], xr[s][:], reads=[Bxr[s]], writes=[Bxout]))))
        return pieces

    load_xb(0)
    load_xb(1)
    if not preloaded:
        ffn_load_weights(P, W, w_up, w_down, queue=wqueue)
    prep_xT(0)
    NP = NT * 22
    SKEW = ((S0, 0), (S1, 1), (S2, 1), (S3, 2), (S4, 3), (S5, 3 if NSET == 4 else 4))
    pending = []
    for n in range(NP + 26):
        for fn, lag in SKEW:
            m = n - lag
            if 0 <= m < NP:
                fn(m)
        tt, j = divmod(n, 22)
        if j == 4 and 1 <= tt <= NT:
            pending = down_pieces(tt - 1)
        if pending:
            pending.pop(0)()
        if n < NP:
            if j == 12 and tt + 1 < NT:
                prep_xT(tt + 1)
            if j == 16 and tt + 2 < NT:
                load_xb(tt + 2)
    assert not pending
    P.barrier()
    A.off = save


def new_ctx():
    nc = bass.Bass("TRN2", target_bir_lowering=False)
    C = Ctx()
    C.nc = nc
    C.A = Arena(nc)
    C.ps = [nc.alloc_psum_tensor("psb%d" % i, [128, 512], F32) for i in range(8)]
    C.Bps = [Buf("ps%d" % i, excl=True) for i in range(8)]
    return nc, C


def host_cwb(conv_w, conv_b):
    a = np.concatenate([conv_w, conv_b[None, :]], axis=0)
    return np.ascontiguousarray(a.reshape(4, 44, 128).transpose(2, 1, 0))


def build_ffn_only():
    nc, C = new_ctx()
    x_in = nc.dram_tensor("x_in", [S, D], F32, kind="ExternalInput").ap()
    w_up = nc.dram_tensor("w_up", [D, 2 * FF], F32, kind="ExternalInput").ap()
    cwb = nc.dram_tensor("cwb", [128, 44, 4], F32, kind="ExternalInput").ap()
    w_dn = nc.dram_tensor("w_dn", [FF, D], F32, kind="ExternalInput").ap()
    g = nc.dram_tensor("ln_g", [D], F32, kind="ExternalInput").ap()
    b = nc.dram_tensor("ln_b", [D], F32, kind="ExternalInput").ap()
    x_out = nc.dram_tensor("x_out", [S, D], F32, kind="ExternalOutput").ap()
    P = Prog(nc)
    make_consts(P, C)
    ffn_phase(P, C, x_in, Buf("xin"), x_out, Buf("xout"), w_up, cwb, w_dn, g, b)
    P.finish()
    P.emit()
    return nc, P


DBG = {}
NWIN = 2696


def mixer0_phase(P, C, x_in, Bxin, x_out, Bxout, prm, tab, Btab, bgq=None):
    nc, A = C.nc, C.A
    save = A.off
    ps, Bps = C.ps, C.Bps
    NT = S // 128
    ATT_SCALE = float(192 ** -0.5)
    LNK = float(np.log(128 ** -0.5))
    PI = float(np.pi)
    win = A.alloc("win", [128, 8, NWIN], BF16)
    wuq = A.alloc("wuq", [128, 2, 1024], BF16)
    wukv = A.alloc("wukv", [128, 1024], BF16)
    wout = A.alloc("wout", [128, 8, D], BF16)
    off_kn = A.off
    knT = A.alloc("knT", [128, 4, S], BF16)
    krT = A.alloc("krT", [128, S], BF16)
    vaug = A.alloc("vaug", [128, NT, 4, 130], BF16)
    gb = A.alloc("gb", [128, 2, D], F32)
    gmn = A.alloc("gmn", [128, 512], F32)
    bg = A.alloc("bg", [128, 8], F32)
    frq = A.alloc("frq", [128, 1], F32)
    cst = A.alloc("cst", [128, 4], F32)
    U32 = A.alloc("U32", [128, 4, 129], F32)
    Cbf = A.alloc("Cbf", [128, 4, 130], BF16)
    mask4 = A.alloc("mask4", [128, 512], BF16)
    tri = A.alloc("tri", [128, 128], F32)
    ones = A.alloc("ones", [128, 128], F32)
    Bw = Buf("w")
    Bcst, BU, BC = Buf("cst"), Buf("U32"), Buf("Cbf")
    xb = [A.alloc("xb", [128, D], BF16) for _ in range(2)]; Bxb = [Buf("xb0"), Buf("xb1")]
    xT = A.alloc("xT", [128, 8, 128], BF16); BxT = Buf("xT")
    qmT = [A.alloc("qmT", [128, 4, 128], BF16) for _ in range(2)]; BqmT = [Buf("qmT0"), Buf("qmT1")]
    kmT = [A.alloc("kmT", [128, 4, 128], BF16) for _ in range(2)]; BkmT = [Buf("kmT0"), Buf("kmT1")]
    ktok = [A.alloc("ktok", [128, 512], BF16) for _ in range(2)]; Bktok = [Buf("ktok0"), Buf("ktok1")]
    vs = [A.alloc("vs", [128, 512], F32) for _ in range(2)]; Bvs = [Buf("vs0"), Buf("vs1")]
    og = [A.alloc("og", [128, 512], F32) for _ in range(2)]; Bog = [Buf("og0"), Buf("og1")]
    gA = [A.alloc("gA", [128, 32], F32) for _ in range(2)]; BgA = [Buf("gA0"), Buf("gA1")]
    ebt = [A.alloc("ebt", [128, 4], F32) for _ in range(3)]; Bebt = [Buf("ebt0"), Buf("ebt1"), Buf("ebt2")]
    gC = A.alloc("gC", [128, 8], F32); BgC = Buf("gC")
    gB = A.alloc("gB", [128, 16], F32); BgB = Buf("gB")
    vpa = A.alloc("vpa", [128, 4, 130], BF16); Bvpa = Buf("vpa")
    APT = A.alloc("APT", [128, 512], BF16); BAPT = Buf("APT")
    Xs = A.alloc("Xs", [128, 4, 129], F32); BXs = Buf("Xs")
    junk1 = A.alloc("junk1", [128, 256], F32); Bjunk1 = Buf("junk1")
    junk2, Bjunk2 = junk1, Bjunk1
    cq32 = A.alloc("cq32", [128, 384], F32); Bcq = Buf("cq32")
    cqn = A.alloc("cqn", [128, 384], BF16); Bcqn = Buf("cqn")
    cqT = A.alloc("cqT", [128, 3, 128], BF16); BcqT = Buf("cqT")
    q2 = [A.alloc("q2", [128, 6, 256], BF16) for _ in range(3)]; Bq2 = [[Buf("q2"), Buf("q2")] for _ in range(3)]
    cs = [A.alloc("cs", [128, 2, 128], F32) for _ in range(2)]; Bcs = [Buf("cs0"), Buf("cs1")]
    rt1 = A.alloc("rt1", [128, 128], F32); rt2 = A.alloc("rt2", [128, 128], F32); Brt = Buf("rt")
    LA = DBG.get("la", 1)
    PT = [A.alloc("PT", [128, 512], BF16) for _ in range(LA + 1)]; BPT = [Buf("PT%d" % i) for i in range(LA + 1)]
    SCB = [2, 3] if LA == 1 else [2, 3, 4]
    ym = [A.alloc("ym", [128, 512], BF16) for _ in range(4)]; Bym = [Buf("ym%d" % i) for i in range(4)]
    ya = [A.alloc("ya", [128, 512], BF16) for _ in range(2)]; Bya = [Buf("ya0"), Buf("ya1")]
    yT = A.alloc("yT", [128, 8, 128], BF16); ByT = Buf("yT")
    st = A.alloc("st", [128, 1, 16], F32); Bst = Buf("st")
    rc = A.alloc("rc", [128, 4], F32); Brc = Buf("rc")
    xr = A.alloc("xr", [128, D], F32); Bxr = Buf("xr")
    work_end = A.off
    Bkn = [Buf("kn%d" % i) for i in range(NT)]
    Bkr = [Buf("kr%d" % i) for i in range(NT)]
    Bva = [Buf("va%d" % i) for i in range(NT)]

    bank_ctr = [0]

    def nb():
        k = 2 + bank_ctr[0] % 6
        bank_ctr[0] += 1
        return k

    P.op("pool", f_memset(cst[:, 0:1], 1.0), writes=[Bcst])
    P.op("pool", f_memset(cst[:, 1:2], LNK), writes=[Bcst])
    P.op("pool", f_memset(cst[:, 2:3], RMS_EPS), writes=[Bcst])
    P.op("pool", f_memset(ones[:], 1.0), writes=[Bcst])
    P.op("pool", f_memset(tri[:], 1.0), writes=[Bcst])
    P.op("pool", lambda e: e.affine_select(out=tri[:], in_=tri[:], compare_op=ALU.is_ge, fill=0.0, base=0,
                                           pattern=[[1, 128]], channel_multiplier=-1), reads=[Bcst], writes=[Bcst])
    for h in range(4):
        P.op("pool", f_copy(mask4[:, h * 128:(h + 1) * 128], tri[:]), reads=[Bcst], writes=[Bcst])
    P.op("pool", f_memset(U32[:], 0.0), writes=[BU])
    P.op("pool", f_memset(Cbf[:], 0.0), writes=[BC])
    P.op("pool", f_memset(vaug[:, :, :, 128:129], 1.0), writes=Bva)
    P.op("pool", f_memset(ebt[2][:], 1.0), writes=[Bebt[2]])
    Bfrq = Buf("frq")
    w_in_v = prm["w_in"].rearrange("(c p) n -> p c n", p=128)
    for c in range(8):
        P.dma("pool", win[:, c, 0:2440], w_in_v[:, c, 0:2440], writes=[Buf()])
    w_out_v = prm["w_out"].rearrange("(c p) n -> p c n", p=128)
    Bwout = Buf("wout")
    P.dma("sp", gb[:, 0, :], prm["ln_g"].partition_broadcast(128), writes=[Buf()])
    P.dma("sp", gb[:, 1, :], prm["ln_b"].partition_broadcast(128), writes=[Buf()])
    P.dma("sp", gmn[:], prm["mnorm"].partition_broadcast(128), writes=[Buf()])
    P.dma("sp", bg[:], prm["bgate"].partition_broadcast(128), writes=[Buf()])
    P.dma("sp", frq[:], prm["rope_freq"], writes=[Bfrq])
    A.off = off_kn
    krs = A.alloc("krs", [128, 8, 64], F32)
    wqs = A.alloc("wqs", [128, 2, 768], F32)
    wks = A.alloc("wks", [128, 1024], F32)
    gq = A.alloc("gq", [128, 2], F32)
    gkv = A.alloc("gkv", [128, 1], F32)
    Bstg = Buf("stg")
    Bw2 = Buf("w2")
    BstgL = [Buf("stg%d" % i) for i in range(5)]
    P.dma("sp", krs[:], w_in_v[:, :, 2440:2504], writes=[BstgL[0]])
    P.dma("sp", wqs[:], prm["w_uq"].rearrange("(c p) n -> p c n", p=128), writes=[BstgL[1]])
    P.dma("sp", wks[:], prm["w_ukv"], writes=[BstgL[2]])
    P.dma("sp", gq[:], prm["qnorm"], writes=[BstgL[3]])
    P.dma("sp", gkv[:], prm["kvnorm"], writes=[BstgL[4]])
    def staging_ops():
        for o in (2440, 2504):
            P.op("dve", f_copy(win[:, :, o:o + 64], krs[:]), reads=BstgL, writes=[Bw2])
        for o in (2568, 2632):
            P.op("dve", f_ts(win[:, :, o:o + 32], krs[:, :, 32:64], -1.0, None, ALU.mult), reads=BstgL, writes=[Bw2])
            P.op("dve", f_copy(win[:, :, o + 32:o + 64], krs[:, :, 0:32]), reads=BstgL, writes=[Bw2])
        for c in range(2):
            src = wqs[:, c, :].rearrange("p (h d) -> p h d", d=192)
            g1 = gq[:, c:c + 1]
            P.op("dve", f_ts(wuq[:, c, 0:512].rearrange("p (h d) -> p h d", d=128), src[:, :, 0:128], g1, None, ALU.mult),
                 reads=BstgL, writes=[Bw2])
            P.op("dve", f_ts(wuq[:, c, 512:768].rearrange("p (h d) -> p h d", d=64), src[:, :, 128:192], g1, None, ALU.mult),
                 reads=BstgL, writes=[Bw2])
            rot = wuq[:, c, 768:1024].rearrange("p (h d) -> p h d", d=64)
            P.op("dve", f_ts(rot[:, :, 0:32], src[:, :, 160:192], g1, -1.0, ALU.mult, ALU.mult), reads=BstgL, writes=[Bw2])
            P.op("dve", f_ts(rot[:, :, 32:64], src[:, :, 128:160], g1, None, ALU.mult), reads=BstgL, writes=[Bw2])
        wk4 = wks[:, :].rearrange("p (h d) -> p h d", d=256)
        P.op("dve", f_ts(wukv[:, 0:512].rearrange("p (h d) -> p h d", d=128), wk4[:, :, 0:128], gkv[:, 0:1], None, ALU.mult),
             reads=BstgL, writes=[Bw2])
        P.op("dve", f_ts(wukv[:, 512:1024].rearrange("p (h d) -> p h d", d=128), wk4[:, :, 128:256], gkv[:, 0:1], None, ALU.mult),
             reads=BstgL, writes=[Bw2])

    CH = 1024
    posi = A.alloc("posi", [128, CH], I32)
    ang = A.alloc("ang", [128, CH], F32)
    ki = A.alloc("ki", [128, CH], I32)
    kf = A.alloc("kf", [128, CH], F32)
    r2 = A.alloc("r2", [128, CH], F32)
    sn = A.alloc("sn", [128, CH], F32)
    Bt = Buf("ropetmp")
    C1 = 6.28125
    C2 = float(2 * np.pi - 6.28125)

    def wrap(r):
        P.op("dve", f_ts(kf[:], r[:], PI, None, ALU.is_gt), reads=[Bt], writes=[Bt])
        P.op("dve", f_stt(r[:], kf[:], -2 * PI, r[:], ALU.mult, ALU.add), reads=[Bt], writes=[Bt])
        P.op("dve", f_ts(kf[:], r[:], -PI, None, ALU.is_lt), reads=[Bt], writes=[Bt])
        P.op("dve", f_stt(r[:], kf[:], 2 * PI, r[:], ALU.mult, ALU.add), reads=[Bt], writes=[Bt])

    for ch in range(S // CH):
        P.dma("sp", posi[:], prm["pos"][ch * CH:(ch + 1) * CH].partition_broadcast(128), writes=[Bt])
        P.op("dve", f_copy(ang[:], posi[:]), reads=[Bt], writes=[Bt])
        P.op("dve", f_ts(ang[:], ang[:], frq[:, 0:1], None, ALU.mult), reads=[Bt, Bfrq], writes=[Bt])
        P.op("dve", f_ts(ki[:], ang[:], float(1 / (2 * np.pi)), None, ALU.mult), reads=[Bt], writes=[Bt])
        P.op("dve", f_copy(kf[:], ki[:]), reads=[Bt], writes=[Bt])
        P.op("dve", f_stt(ang[:], kf[:], -C1, ang[:], ALU.mult, ALU.add), reads=[Bt], writes=[Bt])
        P.op("dve", f_stt(ang[:], kf[:], -C2, ang[:], ALU.mult, ALU.add), reads=[Bt], writes=[Bt])
        wrap(ang)
        P.op("dve", f_ts(r2[:], ang[:], PI / 2, None, ALU.add), reads=[Bt], writes=[Bt])
        wrap(r2)
        P.op("act", f_act(sn[:], r2[:], AF.Sin), reads=[Bt], writes=[Bt])
        P.dma("sp", tab[0, :, ch * CH:(ch + 1) * CH], sn[:], reads=[Bt], writes=[Btab])
        P.op("act", f_act(sn[:], ang[:], AF.Sin), reads=[Bt], writes=[Bt])
        P.dma("sp", tab[1, :, ch * CH:(ch + 1) * CH], sn[:], reads=[Bt], writes=[Btab])
        if ch == 0:
            staging_ops()
    P.barrier()
    A.off = work_end

    def load_x(i):
        P.dma("pool", xb[i % 2][:], x_in[i * 128:(i + 1) * 128, :], reads=[Bxin], writes=[Bxb[i % 2]])
        P.dma("sp", cs[i % 2][:], tab[:, :, i * 128:(i + 1) * 128].rearrange("a p t -> p a t"), reads=[Btab], writes=[Bcs[i % 2]])

    def tpv(k):
        return ps[k][:, :].bitcast(BF16)

    bctr = {"m1": 0, "m2": 0}

    def nb1():
        bctr["m1"] += 1
        return 6 + bctr["m1"] % 2

    def nb2():
        bctr["m2"] += 1
        return 4 + bctr["m2"] % 2

    def M1(i):
        b = i % 2
        r0 = i * 128
        if i + 1 < NT:
            load_x(i + 1)
        if bgq and i >= 1:
            bgq.pop(0)()
        k = nb1()
        for c in range(8):
            P.op("pe", f_tr(tpv(k)[:, c * 128:(c + 1) * 128], xb[b][:, c * 128:(c + 1) * 128], C.ident[:]),
                 reads=[Bxb[b], C.Bident], writes=[Bps[k]], signal=(c == 7))
        P.op("dve", f_copy(xT[:], tpv(k).rearrange("p (c t) -> p c t", c=8)), reads=[Bps[k]], writes=[BxT])
        yield
        for (dst, Bdst, c0, eng) in ((qmT[b], BqmT[b], 0, "act"), (kmT[b], BkmT[b], 512, "dve")):
            k = nb1()
            for h in range(4):
                for c in range(8):
                    P.op("pe", f_mm(ps[k][:, h * 128:(h + 1) * 128], win[:, c, c0 + h * 128:c0 + (h + 1) * 128], xT[:, c, :], c == 0, c == 7),
                         reads=[Bw, BxT], writes=[Bps[k]], signal=(c == 7 and h == 3))
                if DBG.get("coarse", 1) < 1:
                    yield
            if eng == "act":
                P.op("act", f_act(dst[:].rearrange("p h t -> p (h t)"), ps[k][:, :], AF.Copy), reads=[Bps[k]], writes=[Bdst])
            else:
                P.op("dve", f_copy(dst[:].rearrange("p h t -> p (h t)"), ps[k][:, :]), reads=[Bps[k]], writes=[Bdst])
            yield
        k = nb1()
        for m in range(2):
            for c in range(8):
                P.op("pe", f_mm(ps[k][:, m * 128:(m + 1) * 128], win[:, c, 2440 + m * 128:2440 + (m + 1) * 128], xT[:, c, :], c == 0, c == 7),
                     reads=[Bw, BxT], writes=[Bps[k]], signal=(c == 7 and m == 1))
        P.op("dve", f_tt(rt1[:], ps[k][:, 0:128], cs[b][:, 0, :], ALU.mult), reads=[Bps[k], Bcs[b]], writes=[Brt])
        P.op("dve", f_tt(rt2[:], ps[k][:, 128:256], cs[b][:, 1, :], ALU.mult), reads=[Bps[k], Bcs[b]], writes=[Brt])
        P.op("dve", f_tt(krT[:, r0:r0 + 128], rt1[:], rt2[:], ALU.add), reads=[Brt], writes=[Bkr[i]])
        yield
        g = gA[b]
        Bg = BgA[b]
        for gi, (c0, c1) in enumerate(((512, 1024), (1024, 1536), (1536, 2048), (2048, 2440))):
            k = nb1()
            for c in range(8):
                P.op("pe", f_mm(ps[k][:, 0:c1 - c0], xT[:, c, :], win[:, c, c0:c1], c == 0, c == 7),
                     reads=[Bw, BxT], writes=[Bps[k]], signal=(c == 7))
            if DBG.get("coarse", 1) < 2:
                yield
            if gi == 0:
                P.op("act", f_act(ktok[b][:], ps[k][:, :], AF.Copy), reads=[Bps[k]], writes=[Bktok[b]])
            elif gi == 1:
                P.op("dve", f_copy(vs[b][:], ps[k][:, :]), reads=[Bps[k]], writes=[Bvs[b]])
            elif gi == 2:
                P.op("act", f_act(og[b][:], ps[k][:, :], AF.Exp, scale=-1.0), reads=[Bps[k]], writes=[Bog[b]])
            else:
                P.op("dve", f_tt(g[:, 0:8], ps[k][:, 0:8], bg[:], ALU.add), reads=[Bps[k], Bw], writes=[Bg])
                P.op("act", f_act(cq32[:], ps[k][:, 8:392], AF.Copy), reads=[Bps[k]], writes=[Bcq])
            yield
        P.op("dve", f_ts(og[b][:], og[b][:], 1.0, None, ALU.add), reads=[Bog[b]], writes=[Bog[b]])
        P.op("dve", lambda e: e.reciprocal(og[b][:], og[b][:]), reads=[Bog[b]], writes=[Bog[b]])
        yield
        P.op("act", f_act(g[:, 8:12], g[:, 4:8], AF.Exp, scale=-1.0), reads=[Bg], writes=[Bg])
        P.op("act", f_act(g[:, 8:12], g[:, 8:12], AF.Ln, bias=cst[:, 0:1]), reads=[Bg, Bcst], writes=[Bg])
        yield
        k = nb1()
        P.op("pe", f_mm(ps[k][:, 0:4], tri[:], g[:, 8:12], True, True), reads=[Bcst, Bg], writes=[Bps[k]], signal=False)
        P.op("pe", f_mm(ps[k][:, 4:8], ones[:], g[:, 8:12], True, True), reads=[Bcst, Bg], writes=[Bps[k]])
        P.op("dve", f_copy(g[:, 12:20], ps[k][:, 0:8]), reads=[Bps[k]], writes=[Bg])
        P.op("dve", f_tt(g[:, 20:24], g[:, 0:4], g[:, 12:16], ALU.add), reads=[Bg], writes=[Bg])
        yield
        P.op("act", f_act(g[:, 24:28], g[:, 20:24], AF.Exp, bias=cst[:, 1:2]), reads=[Bg, Bcst], writes=[Bg])
        P.op("act", f_act(g[:, 28:32], g[:, 12:16], AF.Exp, scale=-1.0), reads=[Bg], writes=[Bg])
        P.op("act", f_act(ebt[i % 3][:], g[:, 16:20], AF.Exp, scale=-1.0), reads=[Bg], writes=[Bebt[i % 3]])
        yield
        P.op("pool", f_memset(gC[:, 0:2], 0.0), reads=[BgC], writes=[BgC])
        P.op("act", f_act(junk1[:, 0:256], cq32[:, 0:256], AF.Square, accum_out=gC[:, 0:1]), reads=[Bcq], writes=[BgC, Bjunk1])
        P.op("act", f_act(junk1[:, 0:128], cq32[:, 256:384], AF.Square, accum_out=gC[:, 1:2]), reads=[Bcq], writes=[BgC, Bjunk1])
        P.op("act", f_act(gC[:, 2:3], gC[:, 0:1], AF.Ln, bias=cst[:, 2:3], scale=1.0 / 256), reads=[BgC, Bcst], writes=[BgC])
        P.op("act", f_act(gC[:, 3:4], gC[:, 1:2], AF.Ln, bias=cst[:, 2:3], scale=1.0 / 128), reads=[BgC, Bcst], writes=[BgC])
        P.op("act", f_act(gC[:, 2:4], gC[:, 2:4], AF.Exp, scale=-0.5), reads=[BgC], writes=[BgC])
        yield
        P.op("dve", f_ts(cqn[:, 0:256], cq32[:, 0:256], gC[:, 2:3], None, ALU.mult), reads=[Bcq, BgC], writes=[Bcqn])
        P.op("dve", f_ts(cqn[:, 256:384], cq32[:, 256:384], gC[:, 3:4], None, ALU.mult), reads=[Bcq, BgC], writes=[Bcqn])
        k = nb1()
        for c in range(3):
            P.op("pe", f_tr(tpv(k)[:, c * 128:(c + 1) * 128], cqn[:, c * 128:(c + 1) * 128], C.ident[:]),
                 reads=[Bcqn, C.Bident], writes=[Bps[k]], signal=(c == 2))
        P.op("dve", f_copy(cqT[:].rearrange("p c t -> p (c t)"), tpv(k)[:, 0:384]), reads=[Bps[k]], writes=[BcqT])
        yield
        q3 = q2[(i // 2) % 3][:, :, (i % 2) * 128:(i % 2 + 1) * 128]
        Bq3 = Bq2[(i // 2) % 3][i % 2]
        ka = nb1()
        for m in range(4):
            for c in range(2):
                P.op("pe", f_mm(ps[ka][:, m * 128:(m + 1) * 128], wuq[:, c, m * 128:(m + 1) * 128], cqT[:, c, :], c == 0, c == 1),
                     reads=[Bw, BcqT], writes=[Bps[ka]], signal=(c == 1 and m == 3))
        P.op("act", f_act(q3[:, 0:4, :], ps[ka][:, :].rearrange("p (h t) -> p h t", h=4), AF.Copy), reads=[Bps[ka]], writes=[Bq3])
        yield
        kb_ = nb1()
        for m in range(4, 8):
            for c in range(2):
                P.op("pe", f_mm(ps[kb_][:, (m - 4) * 128:(m - 3) * 128], wuq[:, c, m * 128:(m + 1) * 128], cqT[:, c, :], c == 0, c == 1),
                     reads=[Bw, BcqT], writes=[Bps[kb_]], signal=(c == 1 and m == 7))
        for blk in range(2):
            P.op("dve", f_tt(rt1[:], ps[kb_][:, blk * 128:(blk + 1) * 128], cs[b][:, 0, :], ALU.mult), reads=[Bps[kb_], Bcs[b]], writes=[Brt])
            P.op("dve", f_tt(rt2[:], ps[kb_][:, (2 + blk) * 128:(3 + blk) * 128], cs[b][:, 1, :], ALU.mult), reads=[Bps[kb_], Bcs[b]], writes=[Brt])
            P.op("dve", f_tt(q3[:, 4 + blk, :], rt1[:], rt2[:], ALU.add), reads=[Brt], writes=[Bq3])
        yield
        k = nb1()
        for h in range(4):
            P.op("pe", f_mm(ps[k][:, h * 128:(h + 1) * 128], wukv[:, h * 128:(h + 1) * 128], cqT[:, 2, :], True, True),
                 reads=[Bw, BcqT], writes=[Bps[k]], signal=(h == 3))
        P.op("act", f_act(knT[:, :, r0:r0 + 128], ps[k][:, :].rearrange("p (h t) -> p h t", h=4), AF.Copy), reads=[Bps[k]], writes=[Bkn[i]])
        yield
        k = nb1()
        P.op("pe", f_mm(ps[k][:, :], cqT[:, 2, :], wukv[:, 512:1024], True, True), reads=[Bw, BcqT], writes=[Bps[k]])
        P.op("dve", f_copy(vaug[:, i, :, 0:128], ps[k][:, :].rearrange("p (h d) -> p h d", h=4)), reads=[Bps[k]], writes=[Bva[i]])
        yield

    def M2(i):
        b = i % 2
        g, Bg = gA[b], BgA[b]
        eb, Beb = ebt[i % 3], Bebt[i % 3]
        ebp, Bebp = ebt[(i - 1) % 3], Bebt[(i - 1) % 3]
        hm3 = Xs[:, :, 0:128]
        k = nb2()
        for h in range(4):
            P.op("pe", f_mm(ps[k][:, h * 128:(h + 1) * 128], kmT[b][:, h, :], qmT[b][:, h, :], True, True),
                 reads=[BkmT[b], BqmT[b]], writes=[Bps[k]], signal=(h == 3))
        P.op("dve", f_tt(APT[:], ps[k][:, :], mask4[:], ALU.mult), reads=[Bps[k], Bcst], writes=[BAPT])
        yield
        for h in range(4):
            P.op("act", f_act(vpa[:, h, 0:128], vs[b][:, h * 128:(h + 1) * 128], AF.Copy, scale=g[:, 24 + h:25 + h]),
                 reads=[Bvs[b], Bg], writes=[Bvpa])
        P.op("pool", f_copy(vpa[:, :, 128:129], g[:, 24:28].rearrange("p (h o) -> p h o", o=1)), reads=[Bg], writes=[Bvpa])
        yield
        kx = [nb2(), nb2()]
        for h in range(4):
            o_ = ps[kx[h // 2]][:, (h % 2) * 129:(h % 2) * 129 + 129]
            P.op("pe", f_mm(o_, APT[:, h * 128:(h + 1) * 128], vpa[:, h, 0:129], True, False), reads=[BAPT, Bvpa], writes=[Bps[kx[h // 2]]], signal=False)
            P.op("pe", f_mm(o_, qmT[b][:, h, :], Cbf[:, h, 0:129], False, True), reads=[BqmT[b], BC], writes=[Bps[kx[h // 2]]], signal=(h % 2 == 1))
            if h % 2 == 1:
                j = h // 2
                P.op("act", f_act(Xs[:, 2 * j:2 * j + 2, :].rearrange("p h d -> p (h d)"), ps[kx[j]][:, 0:258], AF.Copy), reads=[Bps[kx[j]]], writes=[BXs])
                yield
        kd = [nb2(), nb2()]
        for h in range(4):
            o_ = ps[kd[h // 2]][:, (h % 2) * 129:(h % 2) * 129 + 129]
            P.op("pe", f_mm(o_, ktok[b][:, h * 128:(h + 1) * 128], vpa[:, h, 0:129], True, True), reads=[Bktok[b], Bvpa], writes=[Bps[kd[h // 2]]], signal=(h % 2 == 1))
        yield
        for h in range(4):
            o_ = ps[kd[h // 2]][:, (h % 2) * 129:(h % 2) * 129 + 129]
            P.op("dve", f_stt(U32[:, h, :], U32[:, h, :], ebp[:, h:h + 1], o_, ALU.mult, ALU.add),
                 reads=[Bebp, Bps[kd[h // 2]]], writes=[BU])
        yield
        for h in range(4):
            P.op("act", f_act(Cbf[:, h, 0:129], U32[:, h, :], AF.Copy, scale=eb[:, h:h + 1]), reads=[BU, Beb], writes=[BC])
        yield
        P.op("dve", f_tt(gB[:, 0:4], Xs[:, :, 128:129].rearrange("p h o -> p (h o)"), g[:, 28:32], ALU.mult), reads=[BXs, Bg], writes=[BgB])
        P.op("act", f_act(gB[:, 0:4], gB[:, 0:4], AF.Abs), reads=[BgB], writes=[BgB])
        P.op("dve", f_ts(gB[:, 0:4], gB[:, 0:4], 1.0, None, ALU.max), reads=[BgB], writes=[BgB])
        P.op("dve", lambda e: e.reciprocal(gB[:, 4:8], gB[:, 0:4]), reads=[BgB], writes=[BgB])
        P.op("dve", f_tt(gB[:, 4:8], gB[:, 4:8], g[:, 28:32], ALU.mult), reads=[BgB, Bg], writes=[BgB])
        yield
        P.op("pool", f_memset(gB[:, 8:12], 0.0), reads=[BgB], writes=[BgB])
        for h in range(4):
            P.op("act", f_act(Xs[:, h, 0:128], Xs[:, h, 0:128], AF.Copy, scale=gB[:, 4 + h:5 + h]), reads=[BgB], writes=[BXs])
        yield
        for h in range(4):
            P.op("act", f_act(junk2[:, 0:128], Xs[:, h, 0:128], AF.Square, accum_out=gB[:, 8 + h:9 + h]), reads=[BXs], writes=[BgB, Bjunk2])
        P.op("act", f_act(gB[:, 12:16], gB[:, 8:12], AF.Ln, bias=cst[:, 2:3], scale=1.0 / 128), reads=[BgB, Bcst], writes=[BgB])
        P.op("act", f_act(gB[:, 12:16], gB[:, 12:16], AF.Exp, scale=-0.5), reads=[BgB], writes=[BgB])
        yield
        P.op("dve", f_tt(hm3, hm3, gmn[:, :].rearrange("p (h d) -> p h d", h=4), ALU.mult), reads=[Bw], writes=[BXs])
        P.op("dve", f_tt(hm3, hm3, og[b][:, :].rearrange("p (h d) -> p h d", h=4), ALU.mult), reads=[Bog[b]], writes=[BXs])
        yield
        ymi, Bymi = ym[i % 4], Bym[i % 4]
        for h in range(4):
            P.op("act", f_act(ymi[:, h * 128:(h + 1) * 128], Xs[:, h, 0:128], AF.Copy, scale=gB[:, 12 + h:13 + h]), reads=[BXs, BgB], writes=[Bymi])
        m2_done[i] = True
        yield

    def M3(p):
        i0, i1 = 2 * p, 2 * p + 1
        q2p = q2[p % 3]
        Bq = Bq2[p % 3]
        def att(heads, SCb, accb, PTl, BPTl, rct, Brct, rco):
            G = [(h, g) for h in heads for g in range(p + 1)]

            def scores(n):
                h, g = G[n]
                k = SCb[n % len(SCb)]
                pr = (h % 2) * 64
                for j, kb in enumerate((2 * g, 2 * g + 1)):
                    o_ = ps[k][:, j * 256:(j + 1) * 256]
                    P.op("pe", f_mm(o_, knT[:, h, kb * 128:(kb + 1) * 128], q2p[:, h, :], True, False), reads=[Bkn[kb], Bq[0], Bq[1]], writes=[Bps[k]], signal=False)
                    P.op("pe", f_mm(o_, krT[pr:pr + 64, kb * 128:(kb + 1) * 128], q2p[pr:pr + 64, 4 + h // 2, :], False, True),
                         reads=[Bkr[kb], Bq[0], Bq[1]], writes=[Bps[k]], signal=(j == 1))

            def expmask(n):
                h, g = G[n]
                k = SCb[n % len(SCb)]
                pb = n % len(PTl)
                P.op("act", f_act(PTl[pb][:, 0:512], ps[k][:, :], AF.Exp, scale=ATT_SCALE), reads=[Bps[k]], writes=[BPTl[pb]])
                if g == p:
                    for off in (0, 256 + 128):
                        dsl = PTl[pb][:, off:off + 128]
                        P.op("pool", (lambda d: (lambda e: e.affine_select(out=d, in_=d, compare_op=ALU.is_ge, fill=0.0, base=0,
                                                                          pattern=[[1, 128]], channel_multiplier=-1)))(dsl),
                             reads=[BPTl[pb]], writes=[BPTl[pb]])

            def pv(n):
                h, g = G[n]
                pb = n % len(PTl)
                for j, kb in enumerate((2 * g, 2 * g + 1)):
                    for t in range(2):
                        if kb > 2 * p + t:
                            continue
                        acc = ps[accb[t]][:, (h % 2) * 129:(h % 2) * 129 + 129]
                        P.op("pe", f_mm(acc, PTl[pb][:, j * 256 + t * 128:j * 256 + (t + 1) * 128], vaug[:, kb, h, 0:129], kb == 0, kb == 2 * p + t),
                             reads=[BPTl[pb], Bva[kb]], writes=[Bps[accb[t]]], signal=(j == 1 and t == 1))

            la = len(PTl) - 1
            for n0 in range(min(la, len(G))):
                scores(n0)
            for n in range(len(G)):
                if n + la < len(G):
                    scores(n + la)
                expmask(n)
                pv(n)
                h, g = G[n]
                if g == p:
                    for t in range(2):
                        a2 = ps[accb[t]][:, (h % 2) * 129:(h % 2) * 129 + 129]
                        c_ = rco + 2 * (h % 2) + t
                        P.op("dve", lambda e, a2=a2, c_=c_: e.reciprocal(rct[:, c_:c_ + 1], a2[:, 128:129]), reads=[Bps[accb[t]]], writes=[Brct])
                        P.op("dve", f_ts(ya[t][:, h * 128:(h + 1) * 128], a2[:, 0:128], rct[:, c_:c_ + 1], None, ALU.mult),
                             reads=[Bps[accb[t]], Brct], writes=[Bya[t]])
                yield

        if not DBG.get("skip_att"):
            if p == NT // 2 - 1 and not DBG.get("nosplit"):
                while not m2_done.get(NT - 1):
                    yield
                act_ = [att([0, 1], SCB, [0, 1], PT, BPT, rc, Brc, 0),
                        att([2, 3], [6, 7], [4, 5], xb, Bxb, gC, BgC, 4)]
                while act_:
                    for g_ in list(act_):
                        try:
                            next(g_)
                        except StopIteration:
                            act_.remove(g_)
                    yield
            else:
                yield from att([0, 1, 2, 3], SCB, [0, 1], PT, BPT, rc, Brc, 0)
        for t, i in enumerate((i0, i1)):
            r0 = i * 128
            while not m2_done.get(i):
                yield
            P.dma("sp", xr[:], x_in[r0:r0 + 128, :], reads=[Bxin], writes=[Bxr])
            k = 2
            for c in range(8):
                src = ym[i % 4][:, c * 128:(c + 1) * 128] if c < 4 else ya[t][:, (c - 4) * 128:(c - 3) * 128]
                P.op("pe", f_tr(tpv(k)[:, c * 128:(c + 1) * 128], src, C.ident[:]),
                     reads=[Bym[i % 4], Bya[t], C.Bident], writes=[Bps[k]], signal=(c == 7))
            P.op("dve", f_copy(yT[:].rearrange("p c t -> p (c t)"), tpv(k)), reads=[Bps[k]], writes=[ByT])
            yield
            ko = [3, 2]
            for n in range(2):
                for c in range(8):
                    P.op("pe", f_mm(ps[ko[n]][:, :], yT[:, c, :], wout[:, c, n * 512:(n + 1) * 512], c == 0, c == 7),
                         reads=[ByT, Bwout], writes=[Bps[ko[n]]], signal=(c == 7))
                yield
            for n in range(2):
                P.op("dve", f_stt(xr[:, n * 512:(n + 1) * 512], xr[:, n * 512:(n + 1) * 512], ALPHA, ps[ko[n]][:, :], ALU.mult, ALU.add),
                     reads=[Bps[ko[n]]], writes=[Bxr])
            yield
            ln_stats(P, xr[:], Bxr, st, Bst, 0)
            ln_rstd(P, st, Bst, 1)
            yield
            ln_apply(P, xr[:], Bxr, st, Bst, 0, gb, Bw, x_out[r0:r0 + 128, :], Bxout)
            yield

    def interleave(gens):
        gens = [g_ for g_ in gens if g_ is not None]
        while gens:
            for g_ in list(gens):
                try:
                    next(g_)
                except StopIteration:
                    gens.remove(g_)

    def step(gen):
        try:
            next(gen)
            return True
        except StopIteration:
            return False

    load_x(0)
    for c0 in range(0, 8, 4):
        P.dma("pool", wout[:, c0:c0 + 4, :], w_out_v[:, c0:c0 + 4, :], writes=[Bwout])
    m3 = None
    m2_done = {}
    for t in range(NT + 3):
        gens = []
        if 0 <= t - 1 < NT:
            gens.append(M2(t - 1))
        if t < NT:
            gens.append(M1(t))
        if t >= 2 and (t - 2) % 2 == 0 and (t - 2) // 2 < NT // 2:
            while m3 is not None and step(m3):
                pass
            m3 = M3((t - 2) // 2)
        while gens:
            if not DBG.get("m3last"):
                if m3 is not None and not step(m3):
                    m3 = None
            for g_ in (list(gens) if not DBG.get("rev12") else list(gens)[::-1]):
                if g_ in gens and not step(g_):
                    gens.remove(g_)
            if DBG.get("m3last"):
                if m3 is not None and not step(m3):
                    m3 = None
    while m3 is not None and step(m3):
        pass
    P.barrier()
    A.off = save


def pool_phase(P, C, x_in, Bxin, x_out, Bxout, prm, after_setup=None):
    nc, A = C.nc, C.A
    save = A.off
    ps, Bps = C.ps, C.Bps
    NT = S // 128
    WIN = (2, 4, 8, 16)
    Wc = A.alloc("Wc", [128, 4, 128], BF16)
    Wc0 = A.alloc("Wc0", [128, 4, 128], BF16)
    Wp = A.alloc("Wp", [128, 4, 128], BF16)
    pw = A.alloc("pw", [128, 8, 256], BF16)
    gb = A.alloc("gb", [128, 2, D], F32)
    lsb = A.alloc("lsb", [128, D], F32)
    stg = A.alloc("stg", [128, 8, 256], F32)
    idf = A.alloc("idf", [128, 128], F32)
    tmp = A.alloc("tmp", [128, 128], F32)
    rcn = A.alloc("rcn", [128, 128], F32)
    xs = [A.alloc("xs", [128, D], F32) for _ in range(4)]
    xb = [A.alloc("xb", [128, D], BF16) for _ in range(2)]
    pT = [A.alloc("pT", [128, 8, 128], BF16) for _ in range(2)]
    st = [A.alloc("st", [128, 1, 16], F32) for _ in range(2)]
    Bw, Bt = Buf("w"), Buf("t")
    Bxs = [Buf("xs0"), Buf("xs1"), Buf("xs2"), Buf("xs3")]
    Bxb = [Buf("xb0"), Buf("xb1")]
    BpT, Bst = [Buf("pT0"), Buf("pT1")], [Buf("st0"), Buf("st1")]

    def asel(t, pattern, cm, base, op):
        return lambda e: e.affine_select(out=t, in_=t, compare_op=op, fill=0.0, base=base, pattern=pattern, channel_multiplier=cm)

    Bs1, Bs2 = Buf("stg"), Buf("lsb")
    P.dma("sp", stg[:], prm["pool_w"].rearrange("g (cc p) d -> p (g cc) d", p=128), writes=[Bs1])
    P.dma("sp", lsb[:], prm["lscale"].partition_broadcast(128), writes=[Bs2])
    Bgb = Buf("gb")
    P.dma("sp", gb[:, 0, :], prm["ln_g"].partition_broadcast(128), writes=[Bgb])
    P.dma("sp", gb[:, 1, :], prm["ln_b"].partition_broadcast(128), writes=[Bgb])
    for j in range(8):
        g = j // 2
        P.op("dve", f_tt(pw[:, j, :], stg[:, j, :], lsb[:, g * 256:(g + 1) * 256], ALU.mult), reads=[Bs1, Bs2], writes=[Bw])
    P.op("pool", f_memset(idf[:], 1.0), writes=[Bt])
    P.op("pool", asel(idf[:], [[-1, 128]], 1, 0, ALU.is_equal), reads=[Bt], writes=[Bt])
    for g, w in enumerate(WIN):
        P.op("pool", f_memset(tmp[:], 1.0 / w), reads=[Bt], writes=[Bt])
        P.op("pool", asel(tmp[:], [[1, 128]], -1, 0, ALU.is_ge), reads=[Bt], writes=[Bt])
        P.op("pool", asel(tmp[:], [[-1, 128]], 1, w - 1, ALU.is_ge), reads=[Bt], writes=[Bt])
        P.op("pool", f_tt(Wc[:, g, :], tmp[:], idf[:], ALU.subtract), reads=[Bt], writes=[Bw])
        P.op("pool", f_memset(tmp[:], 1.0 / w), reads=[Bt, Bw], writes=[Bt])
        P.op("pool", asel(tmp[:], [[-1, 128]], 1, w - 1 - 128, ALU.is_ge), reads=[Bt], writes=[Bt])
        P.op("pool", f_copy(Wp[:, g, :], tmp[:]), reads=[Bt], writes=[Bw])
        P.op("pool", f_memset(rcn[:], 1.0 / w), reads=[Bt, Bw], writes=[Bt])
        for kk in range(w - 1, 0, -1):
            P.op("pool", (lambda kk=kk: (lambda e: e.affine_select(out=rcn[:], in_=rcn[:], compare_op=ALU.is_ge, fill=1.0 / kk, base=-kk,
                                                                pattern=[[1, 128]], channel_multiplier=0)))(),
                 reads=[Bt], writes=[Bt])
        P.op("pool", asel(rcn[:], [[1, 128]], -1, 0, ALU.is_ge), reads=[Bt], writes=[Bt])
        P.op("pool", asel(rcn[:], [[-1, 128]], 1, w - 1, ALU.is_ge), reads=[Bt], writes=[Bt])
        P.op("pool", f_tt(Wc0[:, g, :], rcn[:], idf[:], ALU.subtract), reads=[Bt], writes=[Bw])

    def load_x(i):
        P.dma("sp", xs[i % 4][:], x_in[i * 128:(i + 1) * 128, :], reads=[Bxin], writes=[Bxs[i % 4]])

    if after_setup is not None:
        after_setup()

    def TA(i):
        b = i % 2
        x4 = i % 4
        if i + 1 < NT:
            load_x(i + 1)
        P.op("act", f_act(xb[b][:], xs[x4][:], AF.Copy), reads=[Bxs[x4]], writes=[Bxb[b]])
        yield
        kp = [2 + (i % 2) * 2, 3 + (i % 2) * 2]
        for m in range(8):
            g = m // 2
            o_ = ps[kp[m // 4]][:, (m % 4) * 128:(m % 4 + 1) * 128]
            Wcur = Wc0 if i == 0 else Wc
            P.op("pe", f_mm(o_, xb[b][:, m * 128:(m + 1) * 128], Wcur[:, g, :], True, i == 0), reads=[Bxb[b], Bw], writes=[Bps[kp[m // 4]]],
                 signal=(i == 0 and m % 4 == 3))
            if i > 0:
                P.op("pe", f_mm(o_, xb[1 - b][:, m * 128:(m + 1) * 128], Wp[:, g, :], False, True), reads=[Bxb[1 - b], Bw], writes=[Bps[kp[m // 4]]],
                     signal=(m % 4 == 3))
            if m == 3:
                yield
        yield
        for j in range(2):
            P.op("act" if j == 0 else "dve",
                 (f_act(pT[b][:, 4 * j:4 * j + 4, :].rearrange("p c t -> p (c t)"), ps[kp[j]][:, :], AF.Copy) if j == 0 else
                  f_copy(pT[b][:, 4 * j:4 * j + 4, :].rearrange("p c t -> p (c t)"), ps[kp[j]][:, :])),
                 reads=[Bps[kp[j]]], writes=[BpT[b]])
        yield

    def TB(i):
        b = i % 2
        x4 = i % 4
        ko = [(i % 2), 6 + (i % 2)]
        for g in range(4):
            o_ = ps[ko[g // 2]][:, (g % 2) * 256:(g % 2 + 1) * 256]
            for cc in range(2):
                P.op("pe", f_mm(o_, pT[b][:, 2 * g + cc, :], pw[:, 2 * g + cc, :], cc == 0, cc == 1), reads=[BpT[b], Bw], writes=[Bps[ko[g // 2]]],
                     signal=(cc == 1 and g % 2 == 1))
            if g == 1:
                yield
        yield
        for n in range(2):
            P.op("dve", f_stt(xs[x4][:, n * 512:(n + 1) * 512], xs[x4][:, n * 512:(n + 1) * 512], ALPHA, ps[ko[n]][:, :], ALU.mult, ALU.add),
                 reads=[Bps[ko[n]]], writes=[Bxs[x4]])
            yield

    def TC(i):
        x4 = i % 4
        r0 = i * 128
        ln_stats(P, xs[x4][:], Bxs[x4], st[i % 2], Bst[i % 2], 0)
        yield
        ln_rstd(P, st[i % 2], Bst[i % 2], 1)
        yield
        ln_apply(P, xs[x4][:], Bxs[x4], st[i % 2], Bst[i % 2], 0, gb, Bgb, x_out[r0:r0 + 128, :], Bxout)
        yield

    def interleave(gens):
        gens = [g_ for g_ in gens if g_ is not None]
        while gens:
            for g_ in list(gens):
                try:
                    next(g_)
                except StopIteration:
                    gens.remove(g_)

    load_x(0)
    for t in range(NT + 2):
        interleave([TC(t - 2) if 0 <= t - 2 < NT else None, TB(t - 1) if 0 <= t - 1 < NT else None, TA(t) if t < NT else None])
    P.barrier()
    A.off = save


def build_full(phases=("A", "B", "C", "D"), debug=False):
    nc, C = new_ctx()
    ein = lambda n, s, d=F32: nc.dram_tensor(n, list(s), d, kind="ExternalInput").ap()
    x = ein("x", [S, D])
    prmA = dict(pos=ein("pos", [S], I32), w_in=ein("w_in", [D, 2504]), bgate=ein("bgate", [8]), mnorm=ein("mnorm", [512]),
                qnorm=ein("qnorm", [128, 2]), kvnorm=ein("kvnorm", [128, 1]), w_uq=ein("w_uq", [256, 768]),
                w_ukv=ein("w_ukv", [128, 1024]), w_out=ein("w_out", [D, D]), rope_freq=ein("rope_freq", [128, 1]),
                ln_g=ein("ln_mix_g0", [D]), ln_b=ein("ln_mix_b0", [D]))
    ffn = [dict(w_up=ein("w_up%d" % l, [D, 2 * FF]), cwb=ein("cwb%d" % l, [128, 44, 4]), w_dn=ein("w_dn%d" % l, [FF, D]),
                g=ein("ln_ffn_g%d" % l, [D]), b=ein("ln_ffn_b%d" % l, [D])) for l in range(2)]
    prmC = dict(pool_w=ein("pool_w", [4, 256, 256]), lscale=ein("lscale", [D]), ln_g=ein("ln_mix_g1", [D]), ln_b=ein("ln_mix_b1", [D]))
    out = nc.dram_tensor("out", [S, D], F32, kind="ExternalOutput").ap()
    kind = "ExternalOutput" if debug else "Internal"
    x1 = nc.dram_tensor("x1", [S, D], F32, kind=kind).ap()
    x2 = nc.dram_tensor("x2", [S, D], F32, kind=kind).ap()
    x3 = nc.dram_tensor("x3", [S, D], F32, kind=kind).ap()
    tab = nc.dram_tensor("ropetab", [2, 128, S], F32, kind="Internal").ap()
    P = Prog(nc)
    make_consts(P, C)
    Bx, B1, B2, B3, Bo, Btab = Buf("x"), Buf("x1"), Buf("x2"), Buf("x3"), Buf("out"), Buf("tab")
    precast = ("A" in phases) and ("B" in phases)
    if "A" in phases:
        bg = []
        if precast:
            wup_bf = nc.dram_tensor("wup0_bf", [D, 2 * FF], BF16, kind="Internal").ap()
            wdn_bf = nc.dram_tensor("wdn0_bf", [FF, D], BF16, kind="Internal").ap()
            for c in range(8):
                bg.append((lambda c=c: P.dma("pool", wup_bf[c * 128:(c + 1) * 128, :], ffn[0]["w_up"][c * 128:(c + 1) * 128, :], writes=[Buf()])))
            for c in range(4):
                bg.append((lambda c=c: P.dma("pool", wdn_bf[c * 704:(c + 1) * 704, :], ffn[0]["w_dn"][c * 704:(c + 1) * 704, :], writes=[Buf()])))
        mixer0_phase(P, C, x, Bx, x1, B1, prmA, tab, Btab, bgq=bg)
        assert not bg
    if "B" in phases:
        f = ffn[0]
        if precast:
            ffn_phase(P, C, x1, B1, x2, B2, wup_bf, f["cwb"], wdn_bf, f["g"], f["b"], wqueue="sp")
        else:
            ffn_phase(P, C, x1, B1, x2, B2, f["w_up"], f["cwb"], f["w_dn"], f["g"], f["b"])
    W1 = None
    if "C" in phases:
        if "D" in phases:
            W1 = ffn_alloc_weights(C)
            W1.end_off = C.A.off
        pool_phase(P, C, x2, B2, x3, B3, prmC,
                   after_setup=(lambda: ffn_load_weights(P, W1, ffn[1]["w_up"], ffn[1]["w_dn"])) if W1 is not None else None)
    if "D" in phases:
        f = ffn[1]
        ffn_phase(P, C, x3, B3, out, Bo, f["w_up"], f["cwb"], f["w_dn"], f["g"], f["b"], W=W1)
    P.finish()
    P.emit()
    return nc, P


def host_inputs(inp, bi):
    f32 = np.float32
    m = {}
    m["x"] = np.ascontiguousarray(inp["x"][bi])
    m["pos"] = np.ascontiguousarray(inp["positions"][bi]).astype(np.int32)
    m["w_in"] = np.ascontiguousarray(inp["even_w_in"][0])
    m["bgate"] = np.concatenate([inp["even_b_igate"][0], inp["even_b_fgate"][0]]).astype(f32)
    m["mnorm"] = np.ascontiguousarray(inp["even_mlstm_norm"][0])
    m["qnorm"] = np.ascontiguousarray(inp["even_q_norm"][0].reshape(2, 128).T)
    m["kvnorm"] = np.ascontiguousarray(inp["even_kv_norm"][0].reshape(128, 1))
    m["w_uq"] = np.ascontiguousarray(inp["even_w_uq"][0])
    m["w_ukv"] = np.ascontiguousarray(inp["even_w_ukv"][0])
    m["w_out"] = np.ascontiguousarray(inp["even_w_out"][0])
    m["rope_freq"] = (f32(10000.0) ** (-(np.arange(128) % 32).astype(f32) * f32(2) / f32(64))).astype(f32).reshape(128, 1)
    m["ln_mix_g0"] = np.ascontiguousarray(inp["ln_mix_g"][0]); m["ln_mix_b0"] = np.ascontiguousarray(inp["ln_mix_b"][0])
    m["ln_mix_g1"] = np.ascontiguousarray(inp["ln_mix_g"][1]); m["ln_mix_b1"] = np.ascontiguousarray(inp["ln_mix_b"][1])
    for l in range(2):
        m["w_up%d" % l] = np.ascontiguousarray(inp["ffn_w_up"][l])
        m["cwb%d" % l] = host_cwb(inp["ffn_conv_w"][l], inp["ffn_conv_b"][l])
        m["w_dn%d" % l] = np.ascontiguousarray(inp["ffn_w_down"][l])
        m["ln_ffn_g%d" % l] = np.ascontiguousarray(inp["ln_ffn_g"][l]); m["ln_ffn_b%d" % l] = np.ascontiguousarray(inp["ln_ffn_b"][l])
    m["pool_w"] = np.ascontiguousarray(inp["odd_pool_w"][0])
    m["lscale"] = np.ascontiguousarray(inp["odd_layer_scale"][0])
    return m


_CACHE = {}


def kernel(**inputs):
    inp = {k: np.asarray(v) for k, v in inputs.items()}
    if "nc" not in _CACHE:
        _CACHE["nc"] = build_full()[0]
    nc = _CACHE["nc"]
    in_maps = [host_inputs(inp, bi) for bi in range(8)]
    res = run_bass_kernel_spmd(nc, in_maps, core_ids=list(range(8)))
    return np.stack([np.asarray(r["out"]) for r in res.results], axis=0).astype(np.float32)
```
